# Optimizing a Trainium2 kernel written in Bass

```python
import math
import jax
import jax.numpy as jnp
from jax import lax
import numpy as np

D_MODEL = 1024
BATCH = 16
SEQ = 2048
DEPTH = 1
DEC_BATCH = 8
DEC_SEQ = 16
PAST_LEN = 2048

CHUNK = 64
LEFT_CHUNKS = 8
ATTN_REACH = LEFT_CHUNKS * CHUNK
D_SSM = D_MODEL // 2
D_ATTN = D_MODEL // 2
HEAD_DIM = 64
N_HEADS = D_ATTN // HEAD_DIM
MAX_REL = 128
N_REL = 2 * MAX_REL + 1
SSM_GROUP = 16
N_GROUPS = D_SSM // SSM_GROUP
SSM_STATE = 64
DT_MIN = 0.001
DT_MAX = 0.1
D_FF = ((8 * D_MODEL // 3 + 255) // 256) * 256
CONV_W = 3
D_IN = D_SSM + 3 * D_ATTN + 2 * D_MODEL
RMS_EPS = 1e-6
MASK_VALUE = -1e30

kernel_name = 'hybrid_s5_chunk_attn_convglu_stream_step'


def rms_norm(x, g):
    xf = x.astype(jnp.float32)
    y = xf * lax.rsqrt(jnp.mean(xf * xf, axis=-1, keepdims=True) + RMS_EPS)
    return (y * g.astype(jnp.float32)).astype(x.dtype)


def _linear_combine(left, right):
    a1, b1 = left
    a2, b2 = right
    return a1 * a2, a2 * b1 + b2


def s5_mixer(u, s0_re, s0_im, lam_re, lam_im, log_dt, b_re, b_im, c_re, c_im, d_skip):
    n, t, _ = u.shape
    f32 = jnp.float32
    uf = u.astype(f32).reshape(n, t, N_GROUPS, SSM_GROUP)
    lam = lax.complex(lam_re.astype(f32), lam_im.astype(f32))
    dt = jnp.exp(log_dt.astype(f32))[:, None]
    lam_bar = jnp.exp(lam * dt)
    b_mat = lax.complex(b_re.astype(f32), b_im.astype(f32))
    b_bar = ((lam_bar - 1.0) / lam)[..., None] * b_mat
    bu = jnp.einsum('ntgh,gph->ntgp', uf, b_bar)
    s0 = lax.complex(s0_re.astype(f32), s0_im.astype(f32))
    bu = bu.at[:, 0].add(lam_bar * s0)
    a = jnp.broadcast_to(lam_bar, (1, t) + lam_bar.shape)
    _, s = lax.associative_scan(_linear_combine, (a, bu), axis=1)
    c_mat = lax.complex(c_re.astype(f32), c_im.astype(f32))
    y = jnp.einsum('ntgp,ghp->ntgh', s, c_mat).real + d_skip.astype(f32) * uf
    s_last = s[:, -1]
    return y.reshape(n, t, D_SSM).astype(u.dtype), jnp.real(s_last), jnp.imag(s_last)


def rel_bias(table, q_off, k_off):
    rel = jnp.clip(q_off[:, None] - k_off[None, :], -MAX_REL, MAX_REL) + MAX_REL
    return table[:, rel].astype(jnp.float32)


def chunk_band_attention_prompt(q, k, v, table):
    n, s, h, dh = q.shape
    nc = s // CHUNK
    pad = LEFT_CHUNKS * CHUNK
    band = pad + CHUNK
    kp = jnp.pad(k, ((0, 0), (pad, 0), (0, 0), (0, 0)))
    vp = jnp.pad(v, ((0, 0), (pad, 0), (0, 0), (0, 0)))
    w = jnp.arange(band)
    bias = rel_bias(table, jnp.arange(CHUNK), w - pad)
    scale = HEAD_DIM ** -0.5

    def one_chunk(c):
        start = c * CHUNK
        qc = lax.dynamic_slice_in_dim(q, start, CHUNK, axis=1)
        kb = lax.dynamic_slice_in_dim(kp, start, band, axis=1)
        vb = lax.dynamic_slice_in_dim(vp, start, band, axis=1)
        sc = jnp.einsum('nihd,nkhd->nhik', qc, kb).astype(jnp.float32) * scale + bias
        valid = (start - pad + w) >= 0
        sc = jnp.where(valid, sc, MASK_VALUE)
        p = jax.nn.softmax(sc, axis=-1).astype(v.dtype)
        return jnp.einsum('nhik,nkhd->nihd', p, vb)

    out = lax.map(one_chunk, jnp.arange(nc))
    return out.transpose(1, 0, 2, 3, 4).reshape(n, s, h * dh)


def chunk_attention_sample(q, k, v, cache_k, cache_v, table):
    n, t, h, dh = q.shape
    w = cache_k.shape[1]
    kk = jnp.concatenate([cache_k.astype(k.dtype), k], axis=1)
    vv = jnp.concatenate([cache_v.astype(v.dtype), v], axis=1)
    k_off = jnp.concatenate([jnp.arange(w) - w, jnp.arange(t)])
    bias = rel_bias(table, jnp.arange(t), k_off)
    sc = jnp.einsum('nihd,nkhd->nhik', q, kk).astype(jnp.float32) * (HEAD_DIM ** -0.5) + bias
    p = jax.nn.softmax(sc, axis=-1).astype(v.dtype)
    return jnp.einsum('nhik,nkhd->nihd', p, vv).reshape(n, t, h * dh)


def causal_depthwise_conv(a, hist, taps, bias):
    t = a.shape[1]
    hp = jnp.concatenate([hist.astype(a.dtype), a], axis=1)
    y = bias
    for j in range(CONV_W):
        y = y + taps[j] * hp[:, j:j + t]
    return y, hp[:, hp.shape[1] - (CONV_W - 1):]


def hybrid_layer(x, ssm_re0, ssm_im0, past_k, past_v, conv_hist, p):
    n, t, _ = x.shape
    xn = rms_norm(x, p['g_mix'])
    z = xn @ p['w_in']
    cuts = [D_SSM, D_SSM + D_ATTN, D_SSM + 2 * D_ATTN, D_SSM + 3 * D_ATTN,
            D_SSM + 3 * D_ATTN + D_MODEL]
    u, q, k, v, g_ssm, g_att = jnp.split(z, cuts, axis=-1)
    y_ssm, s_re, s_im = s5_mixer(u, ssm_re0, ssm_im0, p['lam_re'], p['lam_im'], p['log_dt'],
                                 p['b_re'], p['b_im'], p['c_re'], p['c_im'], p['d'])
    gl = jax.nn.gelu(y_ssm) @ p['w_ssm_glu']
    br_ssm = gl[..., :D_MODEL] * jax.nn.sigmoid(gl[..., D_MODEL:])
    q = q.reshape(n, t, N_HEADS, HEAD_DIM)
    k = k.reshape(n, t, N_HEADS, HEAD_DIM)
    v = v.reshape(n, t, N_HEADS, HEAD_DIM)
    if past_k is None:
        att = chunk_band_attention_prompt(q, k, v, p['rel_bias'])
        keep = min(ATTN_REACH, t)
        new_k, new_v = k[:, t - keep:], v[:, t - keep:]
    else:
        att = chunk_attention_sample(q, k, v, past_k, past_v, p['rel_bias'])
        new_k, new_v = k, v
    br_att = att @ p['w_attn_up']
    mix = jax.nn.sigmoid(g_ssm) * br_ssm + jax.nn.sigmoid(g_att) * br_att
    h = x + mix @ p['w_o']
    hn = rms_norm(h, p['g_ffn'])
    up = hn @ p['w_up']
    a, b = jnp.split(up, [D_FF], axis=-1)
    c, new_conv = causal_depthwise_conv(a, conv_hist, p['conv_w'], p['conv_b'])
    out = h + (jax.nn.gelu(c) * b) @ p['w_down']
    return out, s_re, s_im, new_k, new_v, new_conv


def setup_inputs(seed: int = 0) -> dict:
    key = jax.random.key(seed)
    ks = jax.random.split(key, 32)
    f32 = jnp.float32

    def nrm(k, shape, s):
        return s * jax.random.normal(k, shape, f32)

    w_att = min(ATTN_REACH, PAST_LEN)
    n_idx = jnp.arange(SSM_STATE, dtype=f32)
    gp = (DEPTH, N_GROUPS, SSM_STATE)
    return {
        'x_prompt': nrm(ks[0], (BATCH, SEQ, D_MODEL), 1.0),
        'x_sample': nrm(ks[1], (DEC_BATCH, DEC_SEQ, D_MODEL), 1.0),
        'state_ssm_re': nrm(ks[2], (DEPTH, DEC_BATCH, N_GROUPS, SSM_STATE), 0.1),
        'state_ssm_im': nrm(ks[3], (DEPTH, DEC_BATCH, N_GROUPS, SSM_STATE), 0.1),
        'cache_attn_k': nrm(ks[4], (DEPTH, DEC_BATCH, w_att, N_HEADS, HEAD_DIM), 1.0),
        'cache_attn_v': nrm(ks[5], (DEPTH, DEC_BATCH, w_att, N_HEADS, HEAD_DIM), 1.0),
        'cache_conv': nrm(ks[6], (DEPTH, DEC_BATCH, CONV_W - 1, D_FF), 1.0),
        'g_mix': 1.0 + nrm(ks[7], (DEPTH, D_MODEL), 0.02),
        'w_in': nrm(ks[8], (DEPTH, D_MODEL, D_IN), D_MODEL ** -0.5),
        'ssm_lambda_re': -0.5 + nrm(ks[9], gp, 0.01),
        'ssm_lambda_im': math.pi * n_idx + nrm(ks[10], gp, 0.01),
        'ssm_log_dt': jax.random.uniform(ks[11], (DEPTH, N_GROUPS), dtype=f32,
                                         minval=math.log(DT_MIN), maxval=math.log(DT_MAX)),
        'ssm_b_re': nrm(ks[12], (DEPTH, N_GROUPS, SSM_STATE, SSM_GROUP), (2 * SSM_GROUP) ** -0.5),
        'ssm_b_im': nrm(ks[13], (DEPTH, N_GROUPS, SSM_STATE, SSM_GROUP), (2 * SSM_GROUP) ** -0.5),
        'ssm_c_re': nrm(ks[14], (DEPTH, N_GROUPS, SSM_GROUP, SSM_STATE), SSM_STATE ** -0.5),
        'ssm_c_im': nrm(ks[15], (DEPTH, N_GROUPS, SSM_GROUP, SSM_STATE), SSM_STATE ** -0.5),
        'ssm_d': nrm(ks[16], (DEPTH, N_GROUPS, SSM_GROUP), 1.0),
        'w_ssm_glu': nrm(ks[17], (DEPTH, D_SSM, 2 * D_MODEL), D_SSM ** -0.5),
        'attn_rel_bias': nrm(ks[18], (DEPTH, N_HEADS, N_REL), 0.1),
        'w_attn_up': nrm(ks[19], (DEPTH, D_ATTN, D_MODEL), D_ATTN ** -0.5),
        'w_o': nrm(ks[20], (DEPTH, D_MODEL, D_MODEL), D_MODEL ** -0.5),
        'g_ffn': 1.0 + nrm(ks[21], (DEPTH, D_MODEL), 0.02),
        'w_up': nrm(ks[22], (DEPTH, D_MODEL, 2 * D_FF), D_MODEL ** -0.5),
        'conv_w': nrm(ks[23], (DEPTH, CONV_W, D_FF), CONV_W ** -0.5),
        'conv_b': nrm(ks[24], (DEPTH, D_FF), 0.02),
        'w_down': nrm(ks[25], (DEPTH, D_FF, D_MODEL), D_FF ** -0.5),
        'g_final': 1.0 + nrm(ks[26], (D_MODEL,), 0.02),
    }


def reference(x_prompt, x_sample, state_ssm_re, state_ssm_im, cache_attn_k, cache_attn_v, cache_conv,
              g_mix, w_in, ssm_lambda_re, ssm_lambda_im, ssm_log_dt, ssm_b_re, ssm_b_im,
              ssm_c_re, ssm_c_im, ssm_d, w_ssm_glu, attn_rel_bias, w_attn_up, w_o,
              g_ffn, w_up, conv_w, conv_b, w_down, g_final):
    hp, hs = x_prompt, x_sample
    nb = x_prompt.shape[0]
    p_re, p_im, p_k, p_v, p_c = [], [], [], [], []
    s_re, s_im, s_k, s_v, s_c = [], [], [], [], []
    for l in range(DEPTH):
        prm = {
            'g_mix': g_mix[l], 'w_in': w_in[l],
            'lam_re': ssm_lambda_re[l], 'lam_im': ssm_lambda_im[l], 'log_dt': ssm_log_dt[l],
            'b_re': ssm_b_re[l], 'b_im': ssm_b_im[l], 'c_re': ssm_c_re[l], 'c_im': ssm_c_im[l],
            'd': ssm_d[l], 'w_ssm_glu': w_ssm_glu[l], 'rel_bias': attn_rel_bias[l],
            'w_attn_up': w_attn_up[l], 'w_o': w_o[l], 'g_ffn': g_ffn[l], 'w_up': w_up[l],
            'conv_w': conv_w[l], 'conv_b': conv_b[l], 'w_down': w_down[l],
        }
        zero_state = jnp.zeros((nb, N_GROUPS, SSM_STATE), jnp.float32)
        zero_conv = jnp.zeros((nb, CONV_W - 1, D_FF), x_prompt.dtype)
        hp, a_re, a_im, a_k, a_v, a_c = hybrid_layer(hp, zero_state, zero_state, None, None, zero_conv, prm)
        hs, b_re, b_im, b_k, b_v, b_c = hybrid_layer(hs, state_ssm_re[l], state_ssm_im[l],
                                                     cache_attn_k[l], cache_attn_v[l], cache_conv[l], prm)
        p_re.append(a_re); p_im.append(a_im); p_k.append(a_k); p_v.append(a_v); p_c.append(a_c)
        s_re.append(b_re); s_im.append(b_im); s_k.append(b_k); s_v.append(b_v); s_c.append(b_c)
    y_prompt = rms_norm(hp, g_final)
    y_sample = rms_norm(hs, g_final)
    new_ssm_re_prompt = jnp.stack(p_re)
    new_ssm_im_prompt = jnp.stack(p_im)
    new_k_prompt = jnp.stack(p_k)
    new_v_prompt = jnp.stack(p_v)
    new_conv_prompt = jnp.stack(p_c)
    new_ssm_re_sample = jnp.stack(s_re)
    new_ssm_im_sample = jnp.stack(s_im)
    new_k_sample = jnp.stack(s_k)
    new_v_sample = jnp.stack(s_v)
    new_conv_sample = jnp.stack(s_c)
    return (y_prompt, y_sample, new_ssm_re_prompt, new_ssm_im_prompt, new_k_prompt, new_v_prompt,
            new_conv_prompt, new_ssm_re_sample, new_ssm_im_sample, new_k_sample, new_v_sample,
            new_conv_sample)
```

```python
import math
import numpy as np
from contextlib import ExitStack
import concourse.bass as bass
import concourse.mybir as mybir
from concourse.bass_utils import run_bass_kernel_spmd

F32 = mybir.dt.float32
BF16 = mybir.dt.bfloat16
I32 = mybir.dt.int32
AF = mybir.ActivationFunctionType
ALU = mybir.AluOpType

NCORES = 8
D = 1024
DFF = 2816
NF = 22
TB = 512
EPS = 1e-6
NSLOT = 4
DEBUG_TAPS = False
LIMIT = None


class StopBuild(Exception):
    pass


CURSEQ = [0]


def chk(tag):
    if LIMIT is not None and (tag == LIMIT or ("%d:%s" % (CURSEQ[0], tag)) == LIMIT):
        raise StopBuild(tag)


class Res:
    __slots__ = ("name", "w", "r")

    def __init__(self, name):
        self.name = name
        self.w = None
        self.r = []


class Sched:
    ENG = ("pe", "act", "dve", "pool", "sp")

    def __init__(self, nc, es):
        self.nc = nc
        self.es = es
        self.q = {e: [] for e in self.ENG}
        self.cnt = {e: 0 for e in self.ENG}
        self.sem = {e: es.enter_context(nc.semaphore("sem_" + e)) for e in ("pe", "act", "dve", "pool")}
        self.waited = {e: {} for e in self.ENG}
        self.dsem = {}

    def _tokens(self, reads, writes):
        toks = []
        for r in reads:
            if r.w is not None:
                toks.append(r.w)
        for r in writes:
            if r.w is not None:
                toks.append(r.w)
            toks.extend(r.r)
        return toks

    def _waits(self, eng, toks):
        need = {}
        for (key, sem, val, teng) in toks:
            if teng == eng and eng == "pe":
                continue
            if self.waited[eng].get(key, -1) >= val:
                continue
            if need.get(key, (None, -1))[1] < val:
                need[key] = (sem, val)
        for key, (sem, val) in need.items():
            self.waited[eng][key] = val
        return list(need.values())

    def _post(self, tok, reads, writes):
        ws = set(id(r) for r in writes)
        for r in writes:
            r.w = tok
            r.r = []
        for r in reads:
            if id(r) not in ws:
                r.r.append(tok)
                if len(r.r) > 24:
                    r.r = r.r[-24:] if False else r.r
        return tok

    def op(self, eng, fn, reads=(), writes=()):
        waits = self._waits(eng, self._tokens(reads, writes))
        self.cnt[eng] += 1
        tok = (eng, self.sem[eng], self.cnt[eng], eng)
        self.q[eng].append((waits, fn, self.sem[eng], 1))
        return self._post(tok, reads, writes)

    def dma(self, eng, dkey, fn, reads=(), writes=()):
        if dkey not in self.dsem:
            self.dsem[dkey] = [self.es.enter_context(self.nc.semaphore("d_" + dkey)), 0]
        ent = self.dsem[dkey]
        toks = self._tokens(reads, writes)
        if ent[1] > 0:
            toks.append(("d_" + dkey, ent[0], ent[1], "dma"))
        waits = self._waits(eng, toks)
        ent[1] += 16
        tok = ("d_" + dkey, ent[0], ent[1], "dma")
        self.q[eng].append((waits, fn, ent[0], 16))
        return self._post(tok, reads, writes)

    def emit(self, final_res=()):
        toks = []
        for r in final_res:
            if r.w is not None:
                toks.append(r.w)
            toks.extend(r.r)
        self.q["sp"].append((self._waits("sp", toks), None, None, 0))
        hmap = {"pe": "tensor", "act": "scalar", "dve": "vector", "pool": "gpsimd", "sp": "sync"}
        with self.nc.Block() as block:
            for e in self.ENG:
                items = self.q[e]

                def body(h, items=items):
                    for (waits, fn, sem, inc) in items:
                        for (ws, wv) in waits:
                            h.wait_ge(ws, wv)
                        if fn is not None:
                            fn(h).then_inc(sem, inc)
                getattr(block, hmap[e])(body)


def rev2(ap):
    (ps, pn), (st, n) = ap.ap[0], ap.ap[1]
    return bass.AP(tensor=ap.tensor, offset=ap.offset + (n - 1) * st, ap=[[ps, pn], [-st, n]])


def bcast(ap, shape):
    return ap.broadcast_to(list(shape))


class Builder:
    def __init__(self):
        self.nc = nc = bass.Bass("TRN2", target_bir_lowering=False)
        self.es = es = ExitStack()
        self.S = Sched(nc, es)
        self.out_res = []
        self._decl_io()
        self._alloc()

    def din(self, name, shape):
        return self.nc.dram_tensor(name, list(shape), F32, kind="ExternalInput").ap()

    def dout(self, name, shape):
        return self.nc.dram_tensor(name, list(shape), F32, kind="ExternalOutput").ap()

    def _decl_io(self):
        d = self.din
        self.xp = d("xp", [2, 2048, D])
        self.xs = d("xs", [16, D])
        self.s0re = d("s0re", [32, 64])
        self.s0im = d("s0im", [32, 64])
        self.ck = d("ck", [512, 512])
        self.cv = d("cv", [512, 512])
        self.cc = d("cc", [2, DFF])
        self.g_mix = d("g_mix", [D])
        self.w_in = d("w_in", [D, 4096])
        self.lam_re = d("lam_re", [32, 64])
        self.lam_im = d("lam_im", [32, 64])
        self.log_dt = d("log_dt", [32])
        self.b_re = d("b_re", [32, 64, 16])
        self.b_im = d("b_im", [32, 64, 16])
        self.c_re = d("c_re", [512, 64])
        self.c_im = d("c_im", [512, 64])
        self.ssm_d = d("ssm_d", [512])
        self.w_glu = d("w_glu", [512, 2048])
        self.relb = d("relb", [8, 257])
        self.w_au = d("w_au", [512, D])
        self.w_o = d("w_o", [D, D])
        self.g_ffn = d("g_ffn", [D])
        self.w_up = d("w_up", [D, 2 * DFF])
        self.conv_w = d("conv_w", [3, DFF])
        self.conv_b = d("conv_b", [DFF])
        self.w_down = d("w_down", [DFF, D])
        self.g_fin = d("g_fin", [D])
        o = self.dout
        self.o_yp = o("o_yp", [2, 2048, D])
        self.o_ys = o("o_ys", [16, D])
        self.o_srep = o("o_srep", [2, 32, 64])
        self.o_simp = o("o_simp", [2, 32, 64])
        self.o_kp = o("o_kp", [2, 512, 512])
        self.o_vp = o("o_vp", [2, 512, 512])
        self.o_cp = o("o_cp", [2, 2, DFF])
        self.o_sres = o("o_sres", [32, 64])
        self.o_sims = o("o_sims", [32, 64])
        self.o_ks = o("o_ks", [16, 512])
        self.o_vs = o("o_vs", [16, 512])
        self.o_cs = o("o_cs", [2, DFF])
        self.extb = self.nc.dram_tensor("extb", [8, 768], F32, kind="Internal").ap()
        self.wsc = self.nc.dram_tensor("wsc", [32, 128, 4096], BF16, kind="Internal").ap()
        self.r_wsc = [Res("wsc%d" % i) for i in range(32)]
        if DEBUG_TAPS:
            mk = lambda name, shape, dt: self.nc.dram_tensor(name, list(shape), dt, kind="ExternalOutput").ap()
            self.dbg_y = mk("dbg_y", [128, 2048], BF16)
            self.dbg_a = mk("dbg_a", [128, 2048], BF16)
            self.dbg_u = mk("dbg_u", [128, 2048], BF16)
            self.dbg_m = mk("dbg_m", [128, 4096], BF16)
            self.dbg_h = mk("dbg_h", [128, 4096], F32)

    def sb(self, name, shape, dt):
        return self.es.enter_context(self.nc.sbuf_tensor(name, list(shape), dt))

    def _alloc(self):
        sb = self.sb
        R = Res
        self.bank = [self.es.enter_context(self.nc.psum_tensor("bk%d" % i, [128, 512], F32)) for i in range(8)]
        self.rbank = [R("bk%d" % i) for i in range(8)]
        self.pool_idx = {"mm": 0, "acc": 0}
        self.pools = {"mm": [0, 1, 2, 3], "acc": [4, 5, 6, 7]}
        self.ring = [sb("ring%d" % i, [128, 4096], BF16) for i in range(NSLOT)]
        self.rring = [R("ring%d" % i) for i in range(NSLOT)]
        self.xhs = [sb("xh_%d" % j, [128, 4, D], F32) for j in range(2)]
        self.r_xhs = [[R("xh%d_%d" % (j, i)) for i in range(4)] for j in range(2)]
        self.xnb = sb("xnb", [128, D], BF16)
        self.r_xnb = R("xnb")
        self.junk = self.xnb
        self.r_junk = self.r_xnb
        self.ss = sb("ss", [128, 8], F32)
        self.r_ss = R("ss")
        self.xnT = sb("xnT", [128, 8, TB], BF16)
        self.r_xnT = [R("xnT%d" % i) for i in range(4)]
        self.kT = sb("kTr", [128, 4, 2, TB], BF16)
        self.r_kT = [R("kT0"), R("kT1")]
        self.vr = sb("vr", [128, 2, 4, 512], BF16)
        self.r_vr = [[R("vr%d_%d" % (s, t)) for t in range(4)] for s in range(2)]
        self.ones = sb("ones", [128, 128], BF16)
        self.zerob = sb("zerob", [128, 128], BF16)
        self.kmask = sb("kmask", [128, 1], F32)
        self.r_const = R("const")
        self.EM = sb("EM", [128, 8, 640], BF16)
        self.r_EM = R("EM")
        self.identf = sb("identf", [128, 128], F32)
        self.identb = sb("identb", [128, 128], BF16)
        self.gmix_fm = sb("gmix_fm", [128, 8], F32)
        self.gffn_fm = sb("gffn_fm", [128, 8], F32)
        self.gfin_row = sb("gfin_row", [128, D], F32)
        self.taps = sb("taps", [128, 3, NF], F32)
        self.cbias = sb("cbias", [128, NF], F32)
        self.hist = sb("hist", [128, NF, 2], F32)
        self.r_hist = R("hist")
        self.W1 = sb("W1", [128, 4, 8, 2, 128], BF16)
        self.TK = sb("TK", [128, 4, 8, 128], BF16)
        self.W3 = sb("W3", [128, 16, 9, 2, 32], BF16)
        self.r_s5w = R("s5w")
        self.Ere = sb("Ere", [128, 16, 64], F32)
        self.Eim = sb("Eim", [128, 16, 64], F32)
        self.rho8 = sb("rho8", [128, 16], F32)
        self.car = sb("car", [128, 2, 16], F32)
        self.r_car = R("car")
        self.ml = sb("ml", [128, 40, 16], F32)
        self.r_ml = R("ml")
        self.mli = sb("mli", [128, 16], I32)
        self.Dch = sb("Dch", [128, 4], F32)
        self.RAK = 46
        self.RA = sb("RA", [128, self.RAK * 256], F32)
        self.r_RA = [R("RA%d" % i) for i in range(self.RAK)]

    def av(self, off_kb, nbytes, dt):
        ob = int(round(off_kb * 1024))
        e0 = ob // 4
        e1 = (ob + nbytes + 3) // 4
        ap = self.RA[:, e0:e1]
        if dt == BF16:
            ap = ap.bitcast(BF16)
        res = self.r_RA[ob // 1024:(ob + nbytes - 1) // 1024 + 1]
        return ap, res

    def getbank(self, pool):
        lst = self.pools[pool]
        i = lst[self.pool_idx[pool] % len(lst)]
        self.pool_idx[pool] += 1
        return self.bank[i], self.rbank[i]

    def build_panels(self, nblocks):
        w_in, w_glu, w_au, w_o, w_up, w_down = self.w_in, self.w_glu, self.w_au, self.w_o, self.w_up, self.w_down
        per = []
        for x in range(8):
            per.append([((8, 512, 0, 512), w_in[:, 512 * x:512 * x + 512].rearrange("(k p) n -> p k n", p=128))])
        per.append([((4, 1024, 0, 1024), w_glu[:, 1024:2048].rearrange("(k p) n -> p k n", p=128))])
        per.append([((4, 1024, 0, 1024), w_glu[:, 0:1024].rearrange("(k p) n -> p k n", p=128))])
        per.append([((4, 1024, 0, 1024), w_au.rearrange("(k p) n -> p k n", p=128))])
        for nh in range(2):
            per.append([((8, 512, 0, 512), w_o[:, 512 * nh:512 * nh + 512].rearrange("(k p) n -> p k n", p=128))])
        for pp in range(11):
            per.append([((8, 512, 0, 256), w_up[:, 256 * pp:256 * pp + 256].rearrange("(k p) n -> p k n", p=128)),
                        ((8, 512, 256, 512), w_up[:, DFF + 256 * pp:DFF + 256 * pp + 256].rearrange("(k p) n -> p k n", p=128))])
        for m in range(8):
            per.append([((22, 128, 0, 128), w_down[:, 128 * m:128 * m + 128].rearrange("(f p) n -> p f n", p=128))])
        assert len(per) == 32
        self.panels = per * nblocks
        self.issued = 0
        self.pcur = 0

    def _issue_panel(self, idx):
        s = idx % NSLOT
        p = idx % 32
        if idx >= 32:
            self.S.dma("sp", "ringH%d" % s, lambda h, s=s, p=p: h.dma_start(out=self.ring[s][:], in_=self.wsc[p]),
                       reads=[self.r_wsc[p]], writes=[self.rring[s]])
            return
        for (a, b, c0, c1), src in self.panels[idx]:
            dst = self.ring[s][:, 0:a * b].rearrange("p (a b) -> p a b", a=a)[:, :, c0:c1]
            self.S.dma("pool", "ring%d" % s, lambda h, dst=dst, src=src: h.dma_start(out=dst, in_=src),
                       writes=[self.rring[s]])
        self.S.dma("sp", "wst", lambda h, s=s, p=p: h.dma_start(out=self.wsc[p], in_=self.ring[s][:]),
                   reads=[self.rring[s]], writes=[self.r_wsc[p]])

    def panel(self, a, b):
        idx = self.pcur
        self.pcur += 1
        while self.issued < min(len(self.panels), idx + NSLOT - 1):
            self._issue_panel(self.issued)
            self.issued += 1
        s = idx % NSLOT
        return self.ring[s][:, 0:a * b].rearrange("p (a b) -> p a b", a=a), self.rring[s]

    def setup_consts(self):
        S = self.S
        rc = self.r_const
        iot = self.av(38, 512, F32)[0].bitcast(I32)
        S.op("pool", lambda h: h.iota(iot[:], pattern=[[1, 128]], base=0, channel_multiplier=-1), writes=[rc])
        S.op("dve", lambda h: h.tensor_scalar(out=self.identf[:], in0=iot[:], scalar1=0.0, scalar2=None, op0=ALU.is_equal), reads=[rc], writes=[rc])
        S.op("dve", lambda h: h.tensor_copy(out=self.identb[:], in_=self.identf[:]), reads=[rc], writes=[rc])
        S.op("pool", lambda h: h.memset(self.ones[:], 1.0), writes=[rc])
        S.op("pool", lambda h: h.memset(self.zerob[:], 0.0), writes=[rc])
        S.op("dve", lambda h: h.tensor_reduce(out=self.kmask[:], in_=self.identf[:, 0:16], axis=mybir.AxisListType.X, op=ALU.add), reads=[rc], writes=[rc])
        S.op("pool", lambda h: h.memset(self.hist[:], 0.0), writes=[self.r_hist])
        dm = lambda key, out, in_: S.dma("sp", key, lambda h: h.dma_start(out=out, in_=in_, allow_slow_non_contiguous=True), writes=[rc])
        dm("c0", self.gmix_fm[:], self.g_mix.rearrange("(k p) -> p k", p=128))
        dm("c0", self.gffn_fm[:], self.g_ffn.rearrange("(k p) -> p k", p=128))
        gf = self.g_fin
        dm("c0", self.gfin_row[:], bass.AP(tensor=gf.tensor, offset=gf.offset, ap=[[0, 128], [1, D]]))
        dm("c0", self.taps[:], self.conv_w.rearrange("j (f p) -> p j f", p=128))
        dm("c0", self.cbias[:], self.conv_b.rearrange("(f p) -> p f", p=128))
        dm("c0", self.Dch[:], self.ssm_d.rearrange("(o p) -> p o", p=128))
        ext = self.av(34, 3072, F32)[0][0:8, :]
        tbl = self.av(37, 1028, F32)[0][0:8, :]
        S.dma("sp", "c1", lambda h: h.dma_start(out=tbl[:], in_=self.relb), writes=[rc])
        S.op("dve", lambda h: h.tensor_copy(out=ext[:, 0:511], in_=bcast(tbl[:, 256:257], [8, 511])), reads=[rc], writes=[rc])
        S.op("dve", lambda h: h.tensor_copy(out=ext[:, 511:768], in_=rev2(tbl[:, 0:257])), reads=[rc], writes=[rc])
        r_ext = Res("extb")
        S.dma("sp", "c2", lambda h: h.dma_start(out=self.extb, in_=ext[:]), reads=[rc], writes=[r_ext])
        bmv, bmr = self.av(0, 640 * 4, F32)
        for hh in range(8):
            src = bass.AP(tensor=self.extb.tensor, offset=self.extb.offset + hh * 768, ap=[[1, 128], [1, 640]])
            S.dma("sp", "c3", lambda h, src=src: h.dma_start(out=bmv, in_=src), reads=[r_ext], writes=bmr)
            S.op("act", lambda h, hh=hh: h.activation(out=self.EM[:, hh, :], in_=rev2(bmv), func=AF.Exp), reads=bmr, writes=[self.r_EM])
        S.op("pool", lambda h: h.memset(self.EM[0:64, :, 576:640], 0.0), writes=[self.r_EM])
        S.op("pool", lambda h: h.memset(self.EM[64:128, :, 0:64], 0.0), writes=[self.r_EM])

    def s5_prep(self):
        S = self.S
        ml, rml = self.ml, self.r_ml
        R1 = [rml]

        def dve(fn, reads=R1, writes=R1):
            return S.op("dve", fn, reads, writes)

        def act(fn, reads=R1, writes=R1):
            return S.op("act", fn, reads, writes)

        def tt(out, a, b, op):
            dve(lambda h: h.tensor_tensor(out=out, in0=a, in1=b, op=op))

        def ts(out, a, s1, op0, s2=None, op1=None):
            if op1 is None:
                dve(lambda h: h.tensor_scalar(out=out, in0=a, scalar1=s1, scalar2=None, op0=op0))
            else:
                dve(lambda h: h.tensor_scalar(out=out, in0=a, scalar1=s1, scalar2=s2, op0=op0, op1=op1))

        V = lambda i: ml[:, i, :]
        LRE, LIM, DT, T0, MAG, X, KF, RR, M1, SIN, COS, LBR, LBI, NR, DEN, KRE, KIM, T1, T2, NPI = range(20)
        P0 = 20
        dm = lambda out, in_: S.dma("sp", "p0", lambda h: h.dma_start(out=out, in_=in_, allow_slow_non_contiguous=True), writes=R1)
        dm(V(LRE), self.lam_re.rearrange("(q g) p -> (g p) q", g=2))
        dm(V(LIM), self.lam_im.rearrange("(q g) p -> (g p) q", g=2))
        ld = self.log_dt
        for gl in range(2):
            dm(ml[64 * gl:64 * gl + 64, DT, :], bass.AP(tensor=ld.tensor, offset=ld.offset + gl, ap=[[0, 64], [2, 16]]))
        act(lambda h: h.activation(out=V(DT), in_=V(DT), func=AF.Exp))
        tt(V(T0), V(LRE), V(DT), ALU.mult)
        act(lambda h: h.activation(out=V(MAG), in_=V(T0), func=AF.Exp))
        act(lambda h: h.activation(out=self.rho8[:], in_=V(T0), func=AF.Exp, scale=8.0))
        act(lambda h: h.activation(out=V(38), in_=V(T0), func=AF.Exp, scale=-8.0))
        tt(V(X), V(LIM), V(DT), ALU.mult)
        ts(V(X), V(X), 1.0 / (2 * math.pi), ALU.mult)

        def sin_turns(dst, src_off):
            ts(V(RR), V(X), src_off, ALU.add)
            dve(lambda h: h.tensor_copy(out=self.mli[:], in_=V(RR)))
            dve(lambda h: h.tensor_copy(out=V(KF), in_=self.mli[:]))
            tt(V(RR), V(RR), V(KF), ALU.subtract)
            ts(V(M1), V(RR), 0.5, ALU.is_gt)
            tt(V(RR), V(RR), V(M1), ALU.subtract)
            ts(V(M1), V(RR), -0.5, ALU.is_lt)
            tt(V(RR), V(RR), V(M1), ALU.add)
            act(lambda h: h.activation(out=V(dst), in_=V(RR), func=AF.Sin, scale=2 * math.pi))

        sin_turns(SIN, 0.0)
        sin_turns(COS, 0.25)
        tt(V(LBR), V(MAG), V(COS), ALU.mult)
        tt(V(LBI), V(MAG), V(SIN), ALU.mult)
        ts(V(NR), V(LBR), -1.0, ALU.add)
        tt(V(T1), V(LRE), V(LRE), ALU.mult)
        tt(V(T2), V(LIM), V(LIM), ALU.mult)
        tt(V(DEN), V(T1), V(T2), ALU.add)
        dve(lambda h: h.reciprocal(out=V(DEN), in_=V(DEN)))
        tt(V(T1), V(NR), V(LRE), ALU.mult)
        tt(V(T2), V(LBI), V(LIM), ALU.mult)
        tt(V(T1), V(T1), V(T2), ALU.add)
        tt(V(KRE), V(T1), V(DEN), ALU.mult)
        tt(V(T1), V(LBI), V(LRE), ALU.mult)
        tt(V(T2), V(NR), V(LIM), ALU.mult)
        tt(V(T1), V(T1), V(T2), ALU.subtract)
        tt(V(KIM), V(T1), V(DEN), ALU.mult)
        PR = lambda n: ml[:, P0 + n, :]
        PI_ = lambda n: ml[:, P0 + 9 + n, :]
        dve(lambda h: h.memset(PR(0), 1.0))
        dve(lambda h: h.memset(PI_(0), 0.0))
        for n in range(1, 9):
            tt(V(T1), PR(n - 1), V(LBR), ALU.mult)
            tt(V(T2), PI_(n - 1), V(LBI), ALU.mult)
            tt(PR(n), V(T1), V(T2), ALU.subtract)
            tt(V(T1), PR(n - 1), V(LBI), ALU.mult)
            tt(V(T2), PI_(n - 1), V(LBR), ALU.mult)
            tt(PI_(n), V(T1), V(T2), ALU.add)
        Ere, Eim = self.Ere, self.Eim
        tt(Ere[:, :, 0], PR(8), V(38), ALU.mult)
        tt(Eim[:, :, 0], PI_(8), V(38), ALU.mult)
        tmpA, rA = self.av(40, 16 * 32 * 4, F32)
        tmpB, rB = self.av(42, 16 * 32 * 4, F32)
        RE = R1 + rA + rB
        for k in range(6):
            m = 1 << k
            last_re = bcast(Ere[:, :, m - 1:m], [128, 16, m])
            last_im = bcast(Eim[:, :, m - 1:m], [128, 16, m])
            a3 = tmpA.rearrange("p (q c) -> p q c", q=16)[:, :, 0:m]
            b3 = tmpB.rearrange("p (q c) -> p q c", q=16)[:, :, 0:m]
            src_re, src_im = Ere[:, :, 0:m], Eim[:, :, 0:m]
            dst_re, dst_im = Ere[:, :, m:2 * m], Eim[:, :, m:2 * m]
            S.op("dve", lambda h, a3=a3, s=src_re, l=last_re: h.tensor_tensor(out=a3, in0=s, in1=l, op=ALU.mult), RE, RE)
            S.op("dve", lambda h, b3=b3, s=src_im, l=last_im: h.tensor_tensor(out=b3, in0=s, in1=l, op=ALU.mult), RE, RE)
            S.op("dve", lambda h, a3=a3, b3=b3, d=dst_re: h.tensor_tensor(out=d, in0=a3, in1=b3, op=ALU.subtract), RE, RE)
            S.op("dve", lambda h, a3=a3, s=src_re, l=last_im: h.tensor_tensor(out=a3, in0=s, in1=l, op=ALU.mult), RE, RE)
            S.op("dve", lambda h, b3=b3, s=src_im, l=last_re: h.tensor_tensor(out=b3, in0=s, in1=l, op=ALU.mult), RE, RE)
            S.op("dve", lambda h, a3=a3, b3=b3, d=dst_im: h.tensor_tensor(out=d, in0=a3, in1=b3, op=ALU.add), RE, RE)
        ts(V(NPI), PI_(0), -1.0, ALU.mult)

        bre, r_bre = self.av(0, 1024, F32)
        bim, r_bim = self.av(1, 1024, F32)
        bbr, r_bbr = self.av(2, 1024, F32)
        bbi, r_bbi = self.av(3, 1024, F32)
        t1v, r_t1 = self.av(4, 1024, F32)
        t2v, r_t2 = self.av(5, 1024, F32)
        v3 = lambda ap: ap.rearrange("p (q h) -> p q h", q=16)
        for gl in range(2):
            for (dst, src, rr) in ((bre, self.b_re, r_bre), (bim, self.b_im, r_bim)):
                s_ap = bass.AP(tensor=src.tensor, offset=src.offset + gl * 1024, ap=[[16, 64], [2048, 16], [1, 16]])
                S.dma("sp", "p1", lambda h, d=v3(dst)[64 * gl:64 * gl + 64], s=s_ap: h.dma_start(out=d, in_=s), writes=rr)
        RB = R1 + r_bre + r_bim + r_bbr + r_bbi + r_t1 + r_t2
        kre_b = bcast(ml[:, KRE, :].unsqueeze(2), [128, 16, 16])
        kim_b = bcast(ml[:, KIM, :].unsqueeze(2), [128, 16, 16])
        db = lambda fn: S.op("dve", fn, RB, RB)
        db(lambda h: h.tensor_tensor(out=v3(t1v), in0=v3(bre), in1=kre_b, op=ALU.mult))
        db(lambda h: h.tensor_tensor(out=v3(t2v), in0=v3(bim), in1=kim_b, op=ALU.mult))
        db(lambda h: h.tensor_tensor(out=bbr, in0=t1v, in1=t2v, op=ALU.subtract))
        db(lambda h: h.tensor_tensor(out=v3(t1v), in0=v3(bim), in1=kre_b, op=ALU.mult))
        db(lambda h: h.tensor_tensor(out=v3(t2v), in0=v3(bre), in1=kim_b, op=ALU.mult))
        db(lambda h: h.tensor_tensor(out=bbi, in0=t1v, in1=t2v, op=ALU.add))
        lbr, r_lbr = self.av(6, 8192, F32)
        lbi, r_lbi = self.av(14, 8192, F32)
        u1, r_u1 = self.av(22, 8192, F32)
        u2, r_u2 = self.av(30, 8192, F32)
        v4 = lambda ap: ap.rearrange("p (j q h) -> p j q h", j=8, q=16)
        RL = RB + r_lbr + r_lbi + r_u1 + r_u2
        mlap = self.ml[:, P0 + 7, :]
        prr = bass.AP(tensor=self.ml, offset=mlap.offset, ap=[list(mlap.ap[0]), [-16, 8], [1, 16], [0, 16]])
        mlap2 = self.ml[:, P0 + 9 + 7, :]
        pir = bass.AP(tensor=self.ml, offset=mlap2.offset, ap=[list(mlap2.ap[0]), [-16, 8], [1, 16], [0, 16]])
        bbr4 = bcast(v3(bbr).unsqueeze(1), [128, 8, 16, 16])
        bbi4 = bcast(v3(bbi).unsqueeze(1), [128, 8, 16, 16])
        dl = lambda fn: S.op("dve", fn, RL, RL)
        dl(lambda h: h.tensor_tensor(out=v4(u1), in0=bbr4, in1=prr, op=ALU.mult))
        dl(lambda h: h.tensor_tensor(out=v4(u2), in0=bbi4, in1=pir, op=ALU.mult))
        dl(lambda h: h.tensor_tensor(out=lbr, in0=u1, in1=u2, op=ALU.subtract))
        dl(lambda h: h.tensor_tensor(out=v4(u1), in0=bbr4, in1=pir, op=ALU.mult))
        dl(lambda h: h.tensor_tensor(out=v4(u2), in0=bbi4, in1=prr, op=ALU.mult))
        dl(lambda h: h.tensor_tensor(out=lbi, in0=u1, in1=u2, op=ALU.add))
        tp = [self.av(22 + 2 * ri, 2048, BF16) for ri in range(2)]
        RT = RL + tp[0][1] + tp[1][1]
        lbs = (lbr, lbi)
        for r in range(4):
            for ri in range(2):
                S.op("pool", lambda h, t=tp[ri][0]: h.memset(t, 0.0), RT, RT)
            for o in range(4):
                q = 4 * o + r
                for ri in range(2):
                    t3 = tp[ri][0].rearrange("p (j c) -> p j c", j=8)
                    for gl in range(2):
                        src = v4(lbs[ri])[64 * gl:64 * gl + 64, :, q, :]
                        dst = t3[64 * gl:64 * gl + 64, :, 32 * r + 16 * gl:32 * r + 16 * gl + 16]
                        S.op("dve" if gl == 0 else "act",
                             (lambda h, d=dst, s=src: h.tensor_copy(out=d, in_=s)) if gl == 0 else
                             (lambda h, d=dst, s=src: h.activation(out=d, in_=s, func=AF.Copy)), RT, RT)
                    bk, rbk = self.getbank("mm")
                    bkb = bk[:].bitcast(BF16).rearrange("p (j c) -> p j c", j=8)
                    for j in range(8):
                        S.op("pe", lambda h, j=j, bkb=bkb, t3=t3: h.transpose(out=bkb[:, j, :], in_=t3[:, j, :], identity=self.identb[:]),
                             RT + [self.r_const], [rbk])
                    S.op("act" if ri == 0 else "dve",
                         (lambda h, o=o, r=r, ri=ri, bkb=bkb: h.activation(out=self.W1[32 * r:32 * r + 32, o, :, ri, :], in_=bkb[32 * r:32 * r + 32, :, :], func=AF.Copy)) if ri == 0 else
                         (lambda h, o=o, r=r, ri=ri, bkb=bkb: h.tensor_copy(out=self.W1[32 * r:32 * r + 32, o, :, ri, :], in_=bkb[32 * r:32 * r + 32, :, :])),
                         [rbk], [self.r_s5w])
        cml = [self.av(26 + ri, 1024, F32) for ri in range(2)]
        zst, r_z = self.av(28, 128 * 4, F32)
        RC = R1 + cml[0][1] + cml[1][1] + r_z
        for ri, csrc in enumerate((self.c_re, self.c_im)):
            for t in range(4):
                s_ap = bass.AP(tensor=csrc.tensor, offset=csrc.offset + t * 128 * 64, ap=[[64, 128], [0, 2], [1, 64]])
                S.dma("sp", "p2", lambda h, s=s_ap: h.dma_start(out=zst.rearrange("p (a b) -> p a b", a=2), in_=s), reads=r_z, writes=r_z)
                bk, rbk = self.getbank("mm")
                S.op("pe", lambda h, bk=bk: h.transpose(out=bk[:, 0:128], in_=zst, identity=self.identf[:]), r_z + [self.r_const], [rbk])
                c3 = cml[ri][0].rearrange("p (q h) -> p q h", q=16)
                bk3 = bk[:, 0:128].rearrange("p (q g h) -> p q g h", q=4, g=2)
                for gl in range(2):
                    S.op("dve", lambda h, gl=gl, t=t, c3=c3, bk3=bk3: h.tensor_copy(out=c3[64 * gl:64 * gl + 64, 4 * t:4 * t + 4, :], in_=bk3[64 * gl:64 * gl + 64, :, gl, :]),
                         [rbk], cml[ri][1])
        w1v, r_w1 = self.av(6, 9 * 1024, F32)
        w2v, r_w2 = self.av(15, 9 * 1024, F32)
        v9 = lambda ap: ap.rearrange("p (d q h) -> p d q h", d=9, q=16)
        RW = RC + r_w1 + r_w2 + [self.r_s5w]
        S.op("pool", lambda h: h.memset(self.W3[:], 0.0), [self.r_s5w], [self.r_s5w])
        cre4 = bcast(cml[0][0].rearrange("p (q h) -> p q h", q=16).unsqueeze(1), [128, 9, 16, 16])
        cim4 = bcast(cml[1][0].rearrange("p (q h) -> p q h", q=16).unsqueeze(1), [128, 9, 16, 16])
        pr9 = bcast(self.ml[:, P0:P0 + 9, :].unsqueeze(3), [128, 9, 16, 16])
        pi9 = bcast(self.ml[:, P0 + 9:P0 + 18, :].unsqueeze(3), [128, 9, 16, 16])
        dw = lambda fn: S.op("dve", fn, RW, RW)
        W3v = self.W3
        dw(lambda h: h.tensor_tensor(out=v9(w1v), in0=cre4, in1=pr9, op=ALU.mult))
        dw(lambda h: h.tensor_tensor(out=v9(w2v), in0=cim4, in1=pi9, op=ALU.mult))
        for gl in range(2):
            dw(lambda h, gl=gl: h.tensor_tensor(out=W3v[64 * gl:64 * gl + 64, :, :, 0, 16 * gl:16 * gl + 16].rearrange("p q d h -> p d q h"),
                                                in0=v9(w1v)[64 * gl:64 * gl + 64], in1=v9(w2v)[64 * gl:64 * gl + 64], op=ALU.subtract))
        dw(lambda h: h.tensor_tensor(out=v9(w1v), in0=cre4, in1=pi9, op=ALU.mult))
        dw(lambda h: h.tensor_tensor(out=v9(w2v), in0=cim4, in1=pr9, op=ALU.mult))
        dw(lambda h: h.tensor_tensor(out=w1v, in0=w1v, in1=w2v, op=ALU.add))
        for gl in range(2):
            dw(lambda h, gl=gl: h.tensor_scalar(out=W3v[64 * gl:64 * gl + 64, :, :, 1, 16 * gl:16 * gl + 16].rearrange("p q d h -> p d q h"),
                                                in0=v9(w1v)[64 * gl:64 * gl + 64], scalar1=-1.0, scalar2=None, op0=ALU.mult))
        S.op("pool", lambda h: h.memset(self.TK[:], 0.0), [self.r_s5w], [self.r_s5w])
        tb = [self.av(30 + ri, 256, BF16) for ri in range(2)]
        RTB = RB + tb[0][1] + tb[1][1]
        bbs = (bbr, bbi)
        for r in range(4):
            for ri in range(2):
                S.op("pool", lambda h, t=tb[ri][0]: h.memset(t, 0.0), RTB, RTB)
            for o in range(4):
                q = 4 * o + r
                for ri in range(2):
                    for gl in range(2):
                        src = v3(bbs[ri])[64 * gl:64 * gl + 64, q, :]
                        dst = tb[ri][0][64 * gl:64 * gl + 64, 32 * r + 16 * gl:32 * r + 16 * gl + 16]
                        S.op("dve", lambda h, d=dst, s=src: h.tensor_copy(out=d, in_=s), RTB, RTB)
                bk, rbk = self.getbank("mm")
                for d_ in range(8):
                    for ri in range(2):
                        S.op("pe", lambda h, ri=ri, d_=d_, q=q, bk=bk: h.matmul(bk[:, 32 * d_:32 * d_ + 32], lhsT=tb[ri][0], rhs=W3v[:, q, d_, ri, :],
                                                                               start=(ri == 0), stop=(ri == 1)), RTB + [self.r_s5w], [rbk])
                S.op("act", lambda h, o=o, r=r, bk=bk: h.activation(out=self.TK[32 * r:32 * r + 32, o, :, 32 * r:32 * r + 32],
                                                                    in_=bk[32 * r:32 * r + 32, 0:256].rearrange("p (d c) -> p d c", d=8), func=AF.Copy),
                     [rbk], [self.r_s5w])
        for o in range(4):
            S.op("dve", lambda h, o=o: h.scalar_tensor_tensor(out=self.TK[:, o, 0, :], in0=self.identb[:], scalar=self.Dch[:, o:o + 1], in1=self.TK[:, o, 0, :],
                                                              op0=ALU.mult, op1=ALU.add), [self.r_s5w, self.r_const], [self.r_s5w])

    def norm_T(self, n, gfm, outT, r_outT, xh, r_xh, tts=None):
        S = self.S
        ntt = (n + 127) // 128
        for tt in (range(ntt) if tts is None else tts):
            tn = min(128, n - tt * 128)
            xin = xh[0:tn, tt, :]
            ssv = self.ss[0:tn, tt:tt + 1]
            S.op("act", lambda h, xin=xin, ssv=ssv, tn=tn: h.activation(out=self.junk[0:tn, :], in_=xin, func=AF.Square, accum_out=ssv),
                 [r_xh[tt]], [self.r_junk, self.r_ss])
            S.op("act", lambda h, ssv=ssv: h.activation(out=ssv, in_=ssv, func=AF.Sqrt, scale=1.0 / D, bias=EPS), [self.r_ss], [self.r_ss])
            S.op("dve", lambda h, ssv=ssv: h.reciprocal(out=ssv, in_=ssv), [self.r_ss], [self.r_ss])
            S.op("act", lambda h, xin=xin, ssv=ssv, tn=tn: h.activation(out=self.xnb[0:tn, :], in_=xin, func=AF.Copy, scale=ssv),
                 [r_xh[tt], self.r_ss], [self.r_xnb])
            bk, rbk = self.getbank("mm")
            bkb = bk[:].bitcast(BF16).rearrange("p (k c) -> p k c", k=8)
            for kt in range(8):
                S.op("pe", lambda h, kt=kt, tn=tn, bkb=bkb: h.transpose(out=bkb[:, kt, 0:tn], in_=self.xnb[0:tn, 128 * kt:128 * kt + 128], identity=self.identb[0:tn, 0:tn]),
                     [self.r_xnb, self.r_const], [rbk])
            S.op("dve", lambda h, tt=tt, tn=tn, bkb=bkb: h.tensor_tensor(out=outT[:, :, 128 * tt:128 * tt + tn], in0=bkb[:, :, 0:tn],
                                                                      in1=bcast(gfm[:, :].unsqueeze(2), [128, 8, tn]), op=ALU.mult),
                 [rbk, self.r_const], [r_outT[tt]])

    def stage1(self, seq, b, gi, load_only=False):
        S = self.S
        n = seq["n"]
        nr = seq.get("nreal", n)
        ntt = (n + 127) // 128
        T0 = b * TB
        x_ap = seq["x"]
        xh, r_xh = self.xhs[gi % 2], self.r_xhs[gi % 2]
        for tt in range(ntt):
            tn = min(128, n - tt * 128)
            tnr = min(tn, nr - tt * 128)
            if tnr < tn:
                S.op("pool", lambda h, tt=tt, tn=tn: h.memset(xh[0:tn, tt, :], 0.0), [], [r_xh[tt]])
            S.dma("sp", "xh%d_%d" % (gi % 2, tt), lambda h, tt=tt, tnr=tnr: h.dma_start(out=xh[0:tnr, tt, :], in_=x_ap[T0 + 128 * tt:T0 + 128 * tt + tnr, :]), writes=[r_xh[tt]])
        if not load_only:
            self.norm_T(n, self.gmix_fm, self.xnT, self.r_xnT, xh, r_xh)

    def stage1_norm(self, seq, gi, tt):
        n = seq["n"]
        if tt < (n + 127) // 128:
            self.norm_T(n, self.gmix_fm, self.xnT, self.r_xnT, self.xhs[gi % 2], self.r_xhs[gi % 2], tts=[tt])

    def block(self, seq, b, gi, pre8=None):
        S = self.S
        n = seq["n"]
        nr = seq.get("nreal", n)
        ntt = (n + 127) // 128
        T0 = b * TB
        cur = seq["slot"](b)
        prev = 1 - cur
        has_prev = seq["has_prev"](b)
        is_last = seq["is_last"](b)
        rxnT = self.r_xnT[:ntt]
        xh, r_xh = self.xhs[gi % 2], self.r_xhs[gi % 2]
        chk("s1")
        uT, r_uT = self.av(0, 4096, BF16)
        qT, r_qT = self.av(4, 4096, BF16)
        yT, r_yT = self.av(8, 4096, BF16)
        aT, r_aT = self.av(12, 4096, BF16)
        uT3, qT3, yT3, aT3 = [v.rearrange("p (f t) -> p f t", f=4) for v in (uT, qT, yT, aT)]
        evac_i = [0]

        def evac_copy(out, in_, reads, writes, eng=None):
            evac_i[0] += 1
            if eng == "act" or (eng is None and evac_i[0] % 2):
                S.op("act", lambda h: h.activation(out=out, in_=in_, func=AF.Copy), reads, writes)
            else:
                S.op("dve", lambda h: h.tensor_copy(out=out, in_=in_), reads, writes)

        def proj_fm(pan, rpan, nk, col0, rhs3, r_rhs, width=128):
            bk, rbk = self.getbank("mm")
            for kt in range(nk):
                S.op("pe", lambda h, kt=kt, bk=bk: h.matmul(bk[0:width, 0:n], lhsT=pan[:, kt, col0:col0 + width], rhs=rhs3[:, kt, 0:n],
                                                            start=(kt == 0), stop=(kt == nk - 1)), [rpan] + list(r_rhs), [rbk])
            return bk, rbk

        def proj_tm(pan, rpan, nk, lhs3, r_lhs, tt, tn, c0=0, c1=512):
            bk, rbk = self.getbank("mm")
            for kt in range(nk):
                S.op("pe", lambda h, kt=kt, bk=bk: h.matmul(bk[0:tn, 0:c1 - c0], lhsT=lhs3[:, kt, 128 * tt:128 * tt + tn], rhs=pan[:, kt, c0:c1],
                                                            start=(kt == 0), stop=(kt == nk - 1)), [rpan] + list(r_lhs), [rbk])
            return bk, rbk

        stg, r_stg = self.av(44, 2048, F32)
        pan, rpan = self.panel(8, 512)
        for ft in range(4):
            bk, rbk = proj_fm(pan, rpan, 8, 128 * ft, self.xnT, rxnT)
            evac_copy(uT3[:, ft, 0:n], bk[:, 0:n], [rbk], [r_uT[ft]], eng="act")
        chk("s2u")
        pan, rpan = self.panel(8, 512)
        for ft in range(4):
            bk, rbk = proj_fm(pan, rpan, 8, 128 * ft, self.xnT, rxnT)
            evac_copy(qT3[:, ft, 0:n], bk[:, 0:n], [rbk], [r_qT[ft]], eng="act")
        pan, rpan = self.panel(8, 512)
        for ft in range(4):
            bk, rbk = proj_fm(pan, rpan, 8, 128 * ft, self.xnT, rxnT)
            evac_copy(self.kT[:, ft, cur, 0:n], bk[:, 0:n], [rbk], [self.r_kT[cur]], eng="act")
        chk("s2k")
        if is_last:
            for tt in range(ntt):
                tn = min(128, n - tt * 128)
                bk, rbk = proj_tm(pan, rpan, 8, self.xnT, rxnT, tt, tn)
                evac_copy(stg[0:tn, :], bk[0:tn, :], [rbk], r_stg)
                tnr = min(tn, nr - tt * 128)
                dst = seq["o_k"][128 * tt:128 * tt + tnr, :]
                self.out_res.append(Res("ok"))
                S.dma("sp", "stg", lambda h, dst=dst, tnr=tnr: h.dma_start(out=dst, in_=stg[0:tnr, :]), reads=r_stg, writes=[self.out_res[-1]])
        chk("s2kt")
        pan, rpan = self.panel(8, 512)
        for tt in range(ntt):
            tn = min(128, n - tt * 128)
            bk, rbk = proj_tm(pan, rpan, 8, self.xnT, rxnT, tt, tn)
            evac_copy(self.vr[0:tn, cur, tt, :], bk[0:tn, :], [rbk], [self.r_vr[cur][tt]], eng="act")
            if is_last:
                evac_copy(stg[0:tn, :], bk[0:tn, :], [rbk], r_stg)
                tnr = min(tn, nr - tt * 128)
                dst = seq["o_v"][128 * tt:128 * tt + tnr, :]
                self.out_res.append(Res("ov"))
                S.dma("sp", "stg", lambda h, dst=dst, tnr=tnr: h.dma_start(out=dst, in_=stg[0:tnr, :]), reads=r_stg, writes=[self.out_res[-1]])

        chk("s2")
        NCb = n // 8
        Lv, r_L = self.av(16, 8192, F32)
        Tv, r_T = self.av(24, 8192, F32)
        Spv, r_Sp = self.av(32, 4096, BF16)
        L4 = Lv.rearrange("p (r q c) -> p r q c", r=2, q=16)
        T4 = Tv.rearrange("p (r q c) -> p r q c", r=2, q=16)
        Sp4 = Spv.rearrange("p (r q c) -> p r q c", r=2, q=16)
        sl = [self.getbank("mm") for _ in range(4)]
        sl4 = [bk_[:].rearrange("p (a r c) -> p a r c", a=4, r=2) for (bk_, _) in sl]
        for o in range(4):
            for ri in range(2):
                for j in range(8):
                    for r in range(4):
                        S.op("pe", lambda h, o=o, r=r, ri=ri, j=j: h.matmul(
                            sl4[r][:, o, ri, 0:NCb], lhsT=self.W1[32 * r:32 * r + 32, o, j, ri, :], rhs=uT3[32 * r:32 * r + 32, o, j:n:8],
                            start=(j == 0), stop=(j == 7), tile_position=(32 * r, 0)), [self.r_s5w, r_uT[o]], [sl[r][1]])
        for r in range(4):
            evac_copy(L4[:, :, r:16:4, 0:NCb].rearrange("p r a c -> p a r c"), sl4[r][:, :, :, 0:NCb], [sl[r][1]], r_L, eng="act")
        chk("s3a")
        Er = self.Ere[:, :, 0:NCb]
        Ei = self.Eim[:, :, 0:NCb]
        RS = r_L + r_T + [self.r_s5w]
        Lr, Li = L4[:, 0, :, 0:NCb], L4[:, 1, :, 0:NCb]
        Tr, Ti = T4[:, 0, :, 0:NCb], T4[:, 1, :, 0:NCb]
        chain = []
        dv = lambda fn: chain.append((fn, RS, RS))
        dv(lambda h: h.tensor_tensor(out=Tr, in0=Er, in1=Lr, op=ALU.mult))
        dv(lambda h: h.tensor_tensor(out=Ti, in0=Ei, in1=Li, op=ALU.mult))
        dv(lambda h: h.tensor_tensor(out=Tr, in0=Tr, in1=Ti, op=ALU.add))
        dv(lambda h: h.tensor_tensor(out=Ti, in0=Er, in1=Li, op=ALU.mult))
        dv(lambda h: h.tensor_tensor(out=Lr, in0=Ei, in1=Lr, op=ALU.mult))
        dv(lambda h: h.tensor_tensor(out=Ti, in0=Ti, in1=Lr, op=ALU.subtract))
        for ri in range(2):
            for q in range(16):
                chain.append((lambda h, ri=ri, q=q: h.tensor_tensor_scan(out=L4[:, ri, q, 0:NCb], data0=bcast(self.rho8[:, q:q + 1], [128, NCb]),
                                                                       data1=T4[:, ri, q, 0:NCb], initial=self.car[:, ri, q:q + 1],
                                                                       op0=ALU.mult, op1=ALU.add), RS + [self.r_car], RS))
        chk("s3b")
        dv(lambda h: h.tensor_tensor(out=Tr, in0=Er, in1=Lr, op=ALU.mult))
        dv(lambda h: h.tensor_tensor(out=Ti, in0=Ei, in1=Li, op=ALU.mult))
        dv(lambda h: h.tensor_tensor(out=Tr, in0=Tr, in1=Ti, op=ALU.subtract))
        dv(lambda h: h.tensor_tensor(out=Ti, in0=Er, in1=Li, op=ALU.mult))
        dv(lambda h: h.tensor_tensor(out=Lr, in0=Ei, in1=Lr, op=ALU.mult))
        dv(lambda h: h.tensor_tensor(out=Ti, in0=Ti, in1=Lr, op=ALU.add))
        def s5_y():
            RSP = RS + r_Sp + [self.r_car]
            S.op("act", lambda h: h.activation(out=Sp4[:, :, :, 0:1], in_=self.car[:, :, :].unsqueeze(3), func=AF.Copy), RSP, RSP)
            if NCb > 1:
                S.op("act", lambda h: h.activation(out=Sp4[:, :, :, 1:NCb], in_=T4[:, :, :, 0:NCb - 1], func=AF.Copy), RSP, RSP)
            NCr = nr // 8
            S.op("dve", lambda h: h.tensor_copy(out=self.car[:, :, :].unsqueeze(3), in_=T4[:, :, :, NCr - 1:NCr]), RSP, RSP)
            chk("s3c")
            for o in range(4):
                bk, rbk = self.getbank("mm")
                Y3 = bk[:, 0:8 * NCb].rearrange("p (i c) -> p i c", i=8)
                uj = uT3[:, o, 0:n].rearrange("p (c j) -> p j c", j=8)

                def toep_lag(d, first, last, o=o, Y3=Y3, uj=uj):
                    S.op("pe", lambda h: h.matmul(Y3[:, d:8, 0:NCb], lhsT=self.TK[:, o, d, :], rhs=uj[:, 0:8 - d, :],
                                                  start=first, stop=last), [self.r_s5w, r_uT[o]], [rbk])
                toep_lag(0, True, False)
                for i in range(8):
                    for ri in range(2):
                        for r in range(4):
                            q = 4 * o + r
                            S.op("pe", lambda h, r=r, q=q, ri=ri, i=i, Y3=Y3: h.matmul(Y3[32 * r:32 * r + 32, i, 0:NCb], lhsT=self.W3[:, q, i + 1, ri, :],
                                                                                       rhs=Sp4[:, ri, q, 0:NCb], start=False, stop=False, tile_position=(0, 32 * r)),
                                 [self.r_s5w] + r_Sp, [rbk])
                for d in range(1, 8):
                    toep_lag(d, False, d == 7)
                S.op("act", lambda h, o=o, Y3=Y3: h.activation(out=yT3[:, o, 0:n].rearrange("p (c i) -> p i c", i=8), in_=Y3[:, :, 0:NCb], func=AF.Gelu_apprx_tanh),
                     [rbk], [r_yT[o]])
            if is_last:
                bk, rbk = self.getbank("mm")
                S.op("pe", lambda h, bk=bk: h.transpose(out=bk[0:32, 0:128], in_=self.car[:, :, :].rearrange("p r q -> p (r q)"), identity=self.identf[:]),
                     [self.r_car, self.r_const], [rbk])
                so, r_so = self.av(40, 512, F32)
                evac_copy(so[0:32, 0:128], bk[0:32, 0:128], [rbk], r_so)
                for ri, dst in enumerate((seq["o_sre"], seq["o_sim"])):
                    self.out_res.append(Res("os"))
                    S.dma("sp", "so", lambda h, ri=ri, dst=dst: h.dma_start(out=dst.rearrange("(q g) p -> q g p", g=2),
                                                                            in_=so[16 * ri:16 * ri + 16, 0:128].rearrange("q (g p) -> q g p", g=2)),
                          reads=r_so, writes=[self.out_res[-1]])

        chk("s3")
        tap = DEBUG_TAPS and seq.get("first") and b == 0
        if tap:
            self.out_res.append(Res("dbg"))
            S.dma("sp", "dbg", lambda h: h.dma_start(out=self.dbg_y, in_=yT), reads=r_yT, writes=[self.out_res[-1]])
            self.out_res.append(Res("dbg"))
            S.dma("sp", "dbg", lambda h: h.dma_start(out=self.dbg_u, in_=uT), reads=r_uT, writes=[self.out_res[-1]])
        exv = [self.av(o_, 1024, BF16) for o_ in (36, 37, 42, 43)]
        pmv = [self.av(o_, 1024, BF16) for o_ in (38, 39, 44, 45)]
        recv, r_rec = self.av(40, 2048, F32)
        tiles = [(4 + t, cur, t, min(128, n - 128 * t)) for t in range(ntt)]
        padded_keys = nr < n
        if has_prev:
            tiles += [(t, prev, t, 128) for t in range(4)]
        it = 0
        heads = {}

        def hctx(hh):
            if hh not in heads:
                ob, r_ob = self.getbank("acc")
                sbk, r_sb = self.getbank("acc")
                heads[hh] = (hh // 2, 64 * (hh % 2), ob, r_ob, sbk, r_sb)
            return heads[hh]

        def front(hh, ti):
            nonlocal it
            hp, po, ob, r_ob, sbk, r_sb = hctx(hh)
            (arel, slot, t, kc) = tiles[ti]
            cq_min = max(8, 2 * arel)
            cq_max = min(15, 2 * arel + 9)
            c0 = 64 * (cq_min - 8)
            c1 = min(n, 64 * (cq_max - 7))
            N = c1 - c0
            qq0 = 512 + c0 - 128 * arel
            stb, r_st = self.getbank("mm")
            S.op("pe", lambda h: h.matmul(stb[0:kc, 0:N], lhsT=self.kT[po:po + 64, hp, slot, 128 * t:128 * t + kc], rhs=qT3[po:po + 64, hp, c0:c1], start=True, stop=True),
                 [self.r_kT[slot], r_qT[hp]], [r_st])
            ex, r_ex = exv[it % 4]
            pm, r_pm = pmv[it % 4]
            it += 1
            S.op("act", lambda h: h.activation(out=ex[0:kc, 0:N], in_=stb[0:kc, 0:N], func=AF.Exp, scale=0.125), [r_st], r_ex)
            if padded_keys and slot == cur:
                S.op("dve", lambda h: h.scalar_tensor_tensor(out=pm[0:kc, 0:N], in0=ex[0:kc, 0:N], scalar=self.kmask[0:kc, 0:1], in1=self.EM[0:kc, hh, qq0:qq0 + N],
                                                             op0=ALU.mult, op1=ALU.mult), r_ex + [self.r_EM, self.r_const], r_pm)
            else:
                S.op("dve" if it % 4 == 0 else "pool", lambda h: h.tensor_tensor(out=pm[0:kc, 0:N], in0=ex[0:kc, 0:N], in1=self.EM[0:kc, hh, qq0:qq0 + N], op=ALU.mult), r_ex + [self.r_EM], r_pm)
            return (pm, r_pm, c0, c1, N, kc, slot, t)

        def back(hh, ti, st):
            hp, po, ob, r_ob, sbk, r_sb = hctx(hh)
            (pm, r_pm, c0, c1, N, kc, slot, t) = st
            first = (ti == 0)
            last = (ti == len(tiles) - 1)
            S.op("pe", lambda h: h.matmul(ob[:, c0:c1], lhsT=self.vr[0:kc, slot, t, 128 * hp:128 * hp + 128], rhs=pm[0:kc, 0:N], start=first, stop=last),
                 r_pm + [self.r_vr[slot][t]], [r_ob])
            S.op("pe", lambda h: h.matmul(sbk[:, c0:c1], lhsT=self.ones[0:kc, :], rhs=pm[0:kc, 0:N], start=first, stop=last),
                 r_pm + [self.r_const], [r_sb])

        def finish(hh):
            hp, po, ob, r_ob, sbk, r_sb = hctx(hh)
            S.op("act", lambda h: h.activation(out=recv[po:po + 64, 0:n], in_=sbk[po:po + 64, 0:n], func=AF.Ln), [r_sb], r_rec)
            S.op("act", lambda h: h.activation(out=recv[po:po + 64, 0:n], in_=recv[po:po + 64, 0:n], func=AF.Exp, scale=-1.0), r_rec, r_rec)
            S.op("dve", lambda h: h.tensor_tensor(out=aT3[po:po + 64, hp, 0:n], in0=ob[po:po + 64, 0:n], in1=recv[po:po + 64, 0:n], op=ALU.mult),
                 [r_ob] + r_rec, [r_aT[hp]])
            for _ in range((44 + 7) // 8):
                if chain:
                    fn_, rd_, wr_ = chain.pop(0)
                    S.op("dve", fn_, rd_, wr_)

        items = [(hh, ti) for hh in range(8) for ti in range(len(tiles))]
        SKEW = 2
        pend = [front(*items[k_]) for k_ in range(min(SKEW, len(items)))]
        for k_, (hh, ti) in enumerate(items):
            if k_ + SKEW < len(items):
                pend.append(front(*items[k_ + SKEW]))
            back(hh, ti, pend.pop(0))
            if ti == len(tiles) - 1:
                finish(hh)

        if tap:
            self.out_res.append(Res("dbg"))
            S.dma("sp", "dbg", lambda h: h.dma_start(out=self.dbg_a, in_=aT), reads=r_aT, writes=[self.out_res[-1]])
        while chain:
            fn_, rd_, wr_ = chain.pop(0)
            S.op("dve", fn_, rd_, wr_)
        s5_y()
        chk("s4")
        G1, r_G1 = self.av(16, 8192, BF16)
        G2, r_G2 = self.av(24, 8192, BF16)
        G13 = G1.rearrange("p (m t) -> p m t", m=8)
        G23 = G2.rearrange("p (m t) -> p m t", m=8)
        g3v = [self.av(32 + i, 1024, BF16) for i in range(2)]
        for (G3_, rG) in ((G13, r_G1), (G23, r_G2)):
            for half in range(2):
                pan, rpan = self.panel(8, 512)
                for c in range(4):
                    m = 4 * half + c
                    bk, rbk = proj_fm(pan, rpan, 8, 128 * c, self.xnT, rxnT)
                    S.op("act", lambda h, G3_=G3_, m=m, bk=bk: h.activation(out=G3_[:, m, 0:n], in_=bk[:, 0:n], func=AF.Sigmoid), [rbk], [rG[m]])
        panB, rpanB = self.panel(4, 1024)
        panA, rpanA = self.panel(4, 1024)
        for m in range(8):
            bkb_, rbkb = proj_fm(panB, rpanB, 4, 128 * m, yT3, r_yT)
            g3, r_g3 = g3v[m % 2]
            S.op("act", lambda h, g3=g3, bkb_=bkb_: h.activation(out=g3[:, 0:n], in_=bkb_[:, 0:n], func=AF.Sigmoid), [rbkb], r_g3)
            bka, rbka = proj_fm(panA, rpanA, 4, 128 * m, yT3, r_yT)
            S.op("dve", lambda h, g3=g3, bka=bka: h.tensor_tensor(out=g3[:, 0:n], in0=bka[:, 0:n], in1=g3[:, 0:n], op=ALU.mult), [rbka] + r_g3, r_g3)
            S.op("pool", lambda h, g3=g3, m=m: h.tensor_tensor(out=G13[:, m, 0:n], in0=g3[:, 0:n], in1=G13[:, m, 0:n], op=ALU.mult), r_g3 + [r_G1[m]], [r_G1[m]])
        pan, rpan = self.panel(4, 1024)
        for m in range(8):
            bk, rbk = proj_fm(pan, rpan, 4, 128 * m, aT3, r_aT)
            S.op("dve", lambda h, m=m, bk=bk: h.tensor_tensor(out=G23[:, m, 0:n], in0=bk[:, 0:n], in1=G23[:, m, 0:n], op=ALU.mult), [rbk, r_G2[m]], [r_G2[m]])
            S.op("pool", lambda h, m=m: h.tensor_tensor(out=G13[:, m, 0:n], in0=G13[:, m, 0:n], in1=G23[:, m, 0:n], op=ALU.add), [r_G1[m], r_G2[m]], [r_G1[m]])
        mix3, r_mix = G13, r_G1

        if tap:
            self.out_res.append(Res("dbg"))
            S.dma("sp", "dbg", lambda h: h.dma_start(out=self.dbg_m, in_=G1), reads=r_G1, writes=[self.out_res[-1]])
        chk("s5")
        pw0, rpw0 = self.panel(8, 512)
        pw1, rpw1 = self.panel(8, 512)
        for tt in range(ntt):
            tn = min(128, n - tt * 128)
            for nh, (pw, rpw) in enumerate(((pw0, rpw0), (pw1, rpw1))):
                bk, rbk = proj_tm(pw, rpw, 8, mix3, r_mix, tt, tn)
                S.op("dve", lambda h, tt=tt, tn=tn, nh=nh, bk=bk: h.tensor_tensor(out=xh[0:tn, tt, 512 * nh:512 * nh + 512], in0=bk[0:tn, :],
                                                                               in1=xh[0:tn, tt, 512 * nh:512 * nh + 512], op=ALU.add),
                     [rbk, r_xh[tt]], [r_xh[tt]])
            if tt >= 1:
                self.norm_T(n, self.gffn_fm, self.xnT, self.r_xnT, xh, r_xh, tts=[tt - 1])
        self.norm_T(n, self.gffn_fm, self.xnT, self.r_xnT, xh, r_xh, tts=[ntt - 1])
        hnT = self.xnT

        chk("s6")
        gT, r_gT = self.av(0, NF * 1024, BF16)
        gT3 = gT.rearrange("p (f t) -> p f t", f=NF)
        asb = [self.av(22 + 3 * i, (n + 2) * 4, F32) for i in range(2)]
        csb = [self.av(28 + 2 * i, 2048, F32) for i in range(2)]
        geb = [self.av(32 + 2 * i, 2048, F32) for i in range(2)]
        for pp in range(11):
            pan, rpan = self.panel(8, 512)
            for s_ in range(2):
                f = 2 * pp + s_
                bka, rbka = proj_fm(pan, rpan, 8, 128 * s_, hnT, rxnT)
                bkb_, rbkb = proj_fm(pan, rpan, 8, 256 + 128 * s_, hnT, rxnT)
                a_, r_a = asb[f % 2]
                c_, r_c = csb[f % 2]
                g_, r_g = geb[f % 2]
                S.op("pool", lambda h, a_=a_, f=f: h.tensor_copy(out=a_[:, 0:2], in_=self.hist[:, f, :]), [self.r_hist], r_a)
                S.op("act", lambda h, a_=a_, bka=bka: h.activation(out=a_[:, 2:2 + n], in_=bka[:, 0:n], func=AF.Copy), [rbka], r_a)
                S.op("pool", lambda h, a_=a_, f=f: h.tensor_copy(out=self.hist[:, f, :], in_=a_[:, nr:nr + 2]), r_a, [self.r_hist])
                S.op("act", lambda h, c_=c_, bka=bka, f=f: h.activation(out=c_[:, 0:n], in_=bka[:, 0:n], func=AF.Identity, scale=self.taps[:, 2, f:f + 1],
                                                                       bias=self.cbias[:, f:f + 1]), [rbka, self.r_const], r_c)
                S.op("dve", lambda h, c_=c_, a_=a_, f=f: h.scalar_tensor_tensor(out=c_[:, 0:n], in0=a_[:, 1:1 + n], scalar=self.taps[:, 1, f:f + 1], in1=c_[:, 0:n],
                                                                               op0=ALU.mult, op1=ALU.add), r_a + r_c + [self.r_const], r_c)
                S.op("dve", lambda h, c_=c_, a_=a_, f=f: h.scalar_tensor_tensor(out=c_[:, 0:n], in0=a_[:, 0:n], scalar=self.taps[:, 0, f:f + 1], in1=c_[:, 0:n],
                                                                               op0=ALU.mult, op1=ALU.add), r_a + r_c + [self.r_const], r_c)
                S.op("act", lambda h, c_=c_, g_=g_: h.activation(out=g_[:, 0:n], in_=c_[:, 0:n], func=AF.Gelu_apprx_tanh), r_c, r_g)
                S.op("dve", lambda h, g_=g_, bkb_=bkb_, f=f: h.tensor_tensor(out=gT3[:, f, 0:n], in0=bkb_[:, 0:n], in1=g_[:, 0:n], op=ALU.mult), [rbkb] + r_g, [r_gT[f]])
        if is_last:
            bk, rbk = self.getbank("mm")
            S.op("pe", lambda h, bk=bk: h.transpose(out=bk[0:44, 0:128], in_=self.hist[:, :, :].rearrange("p f j -> p (f j)"), identity=self.identf[:]),
                 [self.r_hist, self.r_const], [rbk])
            co, r_co = self.av(40, 512, F32)
            evac_copy(co[0:44, 0:128], bk[0:44, 0:128], [rbk], r_co)
            for j in range(2):
                self.out_res.append(Res("oc"))
                S.dma("sp", "so", lambda h, j=j: h.dma_start(out=seq["o_c"][j, :].rearrange("(f p) -> f p", p=128), in_=co[j:44:2, 0:128]),
                      reads=r_co, writes=[self.out_res[-1]])

        chk("s7")
        if pre8 is not None:
            pre8[0]()
        ffv = [self.av(36 + 2 * i, 2048, F32) for i in range(2)]
        def down_tail(m, ff, r_ff):
            bk2, rbk2 = self.getbank("acc")
            for tt in range(ntt):
                tn = min(128, n - tt * 128)
                S.op("pe", lambda h, tt=tt, tn=tn: h.transpose(out=bk2[0:tn, 128 * tt:128 * tt + 128], in_=ff[:, 128 * tt:128 * tt + tn], identity=self.identf[:]),
                     r_ff + [self.r_const], [rbk2])
            for tt in range(ntt):
                tn = min(128, n - tt * 128)
                S.op("dve", lambda h, tt=tt, tn=tn: h.tensor_tensor(out=xh[0:tn, tt, 128 * m:128 * m + 128], in0=bk2[0:tn, 128 * tt:128 * tt + 128],
                                                                  in1=xh[0:tn, tt, 128 * m:128 * m + 128], op=ALU.add),
                     [rbk2, r_xh[tt]], [r_xh[tt]])

        prev_m = None
        for m in range(8):
            pan, rpan = self.panel(NF, 128)
            bk, rbk = self.getbank("mm")
            for f in range(NF):
                S.op("pe", lambda h, f=f, bk=bk, pan=pan: h.matmul(bk[:, 0:n], lhsT=pan[:, f, :], rhs=gT3[:, f, 0:n], start=(f == 0), stop=(f == NF - 1)),
                     [rpan, r_gT[f]], [rbk])
            ff, r_ff = ffv[m % 2]
            evac_copy(ff[:, 0:n], bk[:, 0:n], [rbk], r_ff)
            if prev_m is not None:
                down_tail(*prev_m)
            prev_m = (m, ff, r_ff)
            if pre8 is not None and m % 2 == 1:
                pre8[1](m // 2)
        down_tail(*prev_m)
        ost, r_ost = self.av(40, 4096, F32)
        for tt in range(ntt):
            tn = min(128, n - tt * 128)
            xin = xh[0:tn, tt, :]
            ssv = self.ss[0:tn, 4 + tt:5 + tt]
            S.op("act", lambda h, xin=xin, ssv=ssv, tn=tn: h.activation(out=self.junk[0:tn, :], in_=xin, func=AF.Square, accum_out=ssv), [r_xh[tt]], [self.r_junk, self.r_ss])
            S.op("act", lambda h, ssv=ssv: h.activation(out=ssv, in_=ssv, func=AF.Sqrt, scale=1.0 / D, bias=EPS), [self.r_ss], [self.r_ss])
            S.op("dve", lambda h, ssv=ssv: h.reciprocal(out=ssv, in_=ssv), [self.r_ss], [self.r_ss])
            S.op("dve", lambda h, xin=xin, ssv=ssv, tn=tn: h.scalar_tensor_tensor(out=ost[0:tn, :], in0=xin, scalar=ssv, in1=self.gfin_row[0:tn, :], op0=ALU.mult, op1=ALU.mult),
                 [r_xh[tt], self.r_ss, self.r_const], r_ost)
            tnr = min(tn, nr - tt * 128)
            dst = seq["o_y"][T0 + 128 * tt:T0 + 128 * tt + tnr, :]
            self.out_res.append(Res("oy"))
            S.dma("sp", "ost", lambda h, dst=dst, tnr=tnr: h.dma_start(out=dst, in_=ost[0:tnr, :]), reads=r_ost, writes=[self.out_res[-1]])

    def seq_reset(self, seq):
        S = self.S
        if seq["kind"] == "prompt":
            S.op("pool", lambda h: h.memset(self.car[:], 0.0), [self.r_car], [self.r_car])
            S.op("pool", lambda h: h.memset(self.hist[:], 0.0), [self.r_hist], [self.r_hist])
        else:
            for ri, src in enumerate((self.s0re, self.s0im)):
                S.dma("sp", "sq", lambda h, ri=ri, src=src: h.dma_start(out=self.car[:, ri, :], in_=src.rearrange("(q g) p -> (g p) q", g=2), allow_slow_non_contiguous=True),
                      reads=[self.r_car], writes=[self.r_car])
            for j in range(2):
                S.dma("sp", "sq", lambda h, j=j: h.dma_start(out=self.hist[:, :, j], in_=self.cc[j, :].rearrange("(f p) -> p f", p=128), allow_slow_non_contiguous=True),
                      reads=[self.r_hist], writes=[self.r_hist])
            for t in range(4):
                S.dma("pool", "cv%d" % t, lambda h, t=t: h.dma_start(out=self.vr[:, 0, t, :], in_=self.cv[128 * t:128 * t + 128, :]), writes=[self.r_vr[0][t]])
            kst, r_kst = self.av(40, 1024, BF16)
            for t in range(4):
                S.dma("pool", "ckk", lambda h, t=t: h.dma_start(out=kst, in_=self.ck[128 * t:128 * t + 128, :]), reads=r_kst, writes=r_kst)
                bk, rbk = self.getbank("mm")
                bkb = bk[:].bitcast(BF16).rearrange("p (k c) -> p k c", k=8)
                for hp in range(4):
                    S.op("pe", lambda h, hp=hp, bkb=bkb: h.transpose(out=bkb[:, hp, :], in_=kst[:, 128 * hp:128 * hp + 128], identity=self.identb[:]), r_kst + [self.r_const], [rbk])
                S.op("dve", lambda h, t=t, bkb=bkb: h.tensor_copy(out=self.kT[:, :, 0, 128 * t:128 * t + 128], in_=bkb[:, 0:4, :]), [rbk], [self.r_kT[0]])

    def build(self):
        seqs = []
        for i in range(2):
            seqs.append(dict(kind="prompt", first=(i == 0), n=TB, nblocks=4, x=self.xp[i], o_y=self.o_yp[i], o_k=self.o_kp[i], o_v=self.o_vp[i],
                             o_sre=self.o_srep[i], o_sim=self.o_simp[i], o_c=self.o_cp[i],
                             slot=lambda b: b % 2, has_prev=lambda b: b > 0, is_last=lambda b: b == 3))
        seqs.append(dict(kind="sample", n=128, nreal=16, nblocks=1, x=self.xs, o_y=self.o_ys, o_k=self.o_ks, o_v=self.o_vs,
                         o_sre=self.o_sres, o_sim=self.o_sims, o_c=self.o_cs,
                         slot=lambda b: 1, has_prev=lambda b: True, is_last=lambda b: True))
        self.build_panels(sum(s["nblocks"] for s in seqs))
        try:
            self.setup_consts()
            chk("c")
            self.s5_prep()
            chk("p")
            allb = [(si, seq, b) for si, seq in enumerate(seqs) for b in range(seq["nblocks"])]
            self.stage1(allb[0][1], allb[0][2], 0)
            for gi, (si, seq, b) in enumerate(allb):
                CURSEQ[0] = si
                if b == 0:
                    self.seq_reset(seq)
                    chk("r")
                nxt = allb[gi + 1] if gi + 1 < len(allb) else None
                pre8 = ((lambda nxt=nxt, gi=gi: self.stage1(nxt[1], nxt[2], gi + 1, load_only=True)),
                        (lambda tt, nxt=nxt, gi=gi: self.stage1_norm(nxt[1], gi + 1, tt))) if nxt is not None else None
                self.block(seq, b, gi, pre8)
                chk("b%d_%d" % (si, b))
        except StopBuild:
            pass
        self.S.emit(final_res=self.out_res)
        return self.nc


_CACHE = {}


def kernel(**inp):
    f = lambda a: np.ascontiguousarray(np.asarray(a, dtype=np.float32))
    if "nc" not in _CACHE:
        _CACHE["nc"] = Builder().build()
    nc = _CACHE["nc"]
    shared = {
        "g_mix": f(inp["g_mix"][0]), "w_in": f(inp["w_in"][0]),
        "lam_re": f(inp["ssm_lambda_re"][0]), "lam_im": f(inp["ssm_lambda_im"][0]), "log_dt": f(inp["ssm_log_dt"][0]),
        "b_re": f(inp["ssm_b_re"][0]), "b_im": f(inp["ssm_b_im"][0]),
        "c_re": f(inp["ssm_c_re"][0]).reshape(512, 64), "c_im": f(inp["ssm_c_im"][0]).reshape(512, 64),
        "ssm_d": f(inp["ssm_d"][0]).reshape(512), "w_glu": f(inp["w_ssm_glu"][0]), "relb": f(inp["attn_rel_bias"][0]),
        "w_au": f(inp["w_attn_up"][0]), "w_o": f(inp["w_o"][0]), "g_ffn": f(inp["g_ffn"][0]), "w_up": f(inp["w_up"][0]),
        "conv_w": f(inp["conv_w"][0]), "conv_b": f(inp["conv_b"][0]), "w_down": f(inp["w_down"][0]), "g_fin": f(inp["g_final"]),
    }
    xp = f(inp["x_prompt"])
    xs = f(inp["x_sample"])
    in_maps = []
    for c in range(NCORES):
        m = dict(shared)
        m["xp"] = xp[2 * c:2 * c + 2]
        m["xs"] = xs[c]
        m["s0re"] = f(inp["state_ssm_re"][0, c])
        m["s0im"] = f(inp["state_ssm_im"][0, c])
        m["ck"] = f(inp["cache_attn_k"][0, c]).reshape(512, 512)
        m["cv"] = f(inp["cache_attn_v"][0, c]).reshape(512, 512)
        m["cc"] = f(inp["cache_conv"][0, c])
        in_maps.append(m)
    res = run_bass_kernel_spmd(nc, in_maps, core_ids=list(range(NCORES)))
    R = res.results
    cat = lambda k: np.concatenate([np.asarray(R[c][k]) for c in range(NCORES)], axis=0)
    stk = lambda k: np.stack([np.asarray(R[c][k]) for c in range(NCORES)], axis=0)
    y_prompt = cat("o_yp").astype(np.float32)
    y_sample = stk("o_ys").astype(np.float32)
    outs = (
        y_prompt, y_sample,
        cat("o_srep")[None].astype(np.float32), cat("o_simp")[None].astype(np.float32),
        cat("o_kp").reshape(1, 16, 512, 8, 64).astype(np.float32), cat("o_vp").reshape(1, 16, 512, 8, 64).astype(np.float32),
        cat("o_cp")[None].astype(np.float32),
        stk("o_sres")[None].astype(np.float32), stk("o_sims")[None].astype(np.float32),
        stk("o_ks").reshape(1, 8, 16, 8, 64).astype(np.float32), stk("o_vs").reshape(1, 8, 16, 8, 64).astype(np.float32),
        stk("o_cs")[None].astype(np.float32),
    )
    return outs
```

```python
import math
import numpy as np
from contextlib import ExitStack
import concourse.bass as bass
import concourse.mybir as mybir
from concourse.bass_utils import run_bass_kernel_spmd

F32 = mybir.dt.float32
BF16 = mybir.dt.bfloat16
I32 = mybir.dt.int32
AF = mybir.ActivationFunctionType
ALU = mybir.AluOpType

NCORES = 8
D = 1024
DFF = 2816
NF = 22
TB = 512
EPS = 1e-6
NSLOT = 4
DEBUG_TAPS = False
LIMIT = None


class StopBuild(Exception):
    pass


CURSEQ = [0]


def chk(tag):
    if LIMIT is not None and (tag == LIMIT or ("%d:%s" % (CURSEQ[0], tag)) == LIMIT):
        raise StopBuild(tag)


class Res:
    __slots__ = ("name", "w", "r")

    def __init__(self, name):
        self.name = name
        self.w = None
        self.r = []


class Sched:
    ENG = ("pe", "act", "dve", "pool", "sp")

    def __init__(self, nc, es):
        self.nc = nc
        self.es = es
        self.q = {e: [] for e in self.ENG}
        self.cnt = {e: 0 for e in self.ENG}
        self.sem = {e: es.enter_context(nc.semaphore("sem_" + e)) for e in ("pe", "act", "dve", "pool")}
        self.waited = {e: {} for e in self.ENG}
        self.dsem = {}

    def _tokens(self, reads, writes):
        toks = []
        for r in reads:
            if r.w is not None:
                toks.append(r.w)
        for r in writes:
            if r.w is not None:
                toks.append(r.w)
            toks.extend(r.r)
        return toks

    def _waits(self, eng, toks):
        need = {}
        for (key, sem, val, teng) in toks:
            if teng == eng and eng == "pe":
                continue
            if self.waited[eng].get(key, -1) >= val:
                continue
            if need.get(key, (None, -1))[1] < val:
                need[key] = (sem, val)
        for key, (sem, val) in need.items():
            self.waited[eng][key] = val
        return list(need.values())

    def _post(self, tok, reads, writes):
        ws = set(id(r) for r in writes)
        for r in writes:
            r.w = tok
            r.r = []
        for r in reads:
            if id(r) not in ws:
                r.r.append(tok)
                if len(r.r) > 24:
                    r.r = r.r[-24:] if False else r.r
        return tok

    def op(self, eng, fn, reads=(), writes=()):
        waits = self._waits(eng, self._tokens(reads, writes))
        self.cnt[eng] += 1
        tok = (eng, self.sem[eng], self.cnt[eng], eng)
        self.q[eng].append((waits, fn, self.sem[eng], 1))
        return self._post(tok, reads, writes)

    def dma(self, eng, dkey, fn, reads=(), writes=()):
        if dkey not in self.dsem:
            self.dsem[dkey] = [self.es.enter_context(self.nc.semaphore("d_" + dkey)), 0]
        ent = self.dsem[dkey]
        toks = self._tokens(reads, writes)
        if ent[1] > 0:
            toks.append(("d_" + dkey, ent[0], ent[1], "dma"))
        waits = self._waits(eng, toks)
        ent[1] += 16
        tok = ("d_" + dkey, ent[0], ent[1], "dma")
        self.q[eng].append((waits, fn, ent[0], 16))
        return self._post(tok, reads, writes)

    def emit(self, final_res=()):
        toks = []
        for r in final_res:
            if r.w is not None:
                toks.append(r.w)
            toks.extend(r.r)
        self.q["sp"].append((self._waits("sp", toks), None, None, 0))
        hmap = {"pe": "tensor", "act": "scalar", "dve": "vector", "pool": "gpsimd", "sp": "sync"}
        with self.nc.Block() as block:
            for e in self.ENG:
                items = self.q[e]

                def body(h, items=items):
                    for (waits, fn, sem, inc) in items:
                        for (ws, wv) in waits:
                            h.wait_ge(ws, wv)
                        if fn is not None:
                            fn(h).then_inc(sem, inc)
                getattr(block, hmap[e])(body)


def rev2(ap):
    (ps, pn), (st, n) = ap.ap[0], ap.ap[1]
    return bass.AP(tensor=ap.tensor, offset=ap.offset + (n - 1) * st, ap=[[ps, pn], [-st, n]])


def bcast(ap, shape):
    return ap.broadcast_to(list(shape))


class Builder:
    def __init__(self):
        self.nc = nc = bass.Bass("TRN2", target_bir_lowering=False)
        self.es = es = ExitStack()
        self.S = Sched(nc, es)
        self.out_res = []
        self._decl_io()
        self._alloc()

    def din(self, name, shape):
        return self.nc.dram_tensor(name, list(shape), F32, kind="ExternalInput").ap()

    def dout(self, name, shape):
        return self.nc.dram_tensor(name, list(shape), F32, kind="ExternalOutput").ap()

    def _decl_io(self):
        d = self.din
        self.xp = d("xp", [2, 2048, D])
        self.xs = d("xs", [16, D])
        self.s0re = d("s0re", [32, 64])
        self.s0im = d("s0im", [32, 64])
        self.ck = d("ck", [512, 512])
        self.cv = d("cv", [512, 512])
        self.cc = d("cc", [2, DFF])
        self.g_mix = d("g_mix", [D])
        self.w_in = d("w_in", [D, 4096])
        self.lam_re = d("lam_re", [32, 64])
        self.lam_im = d("lam_im", [32, 64])
        self.log_dt = d("log_dt", [32])
        self.b_re = d("b_re", [32, 64, 16])
        self.b_im = d("b_im", [32, 64, 16])
        self.c_re = d("c_re", [512, 64])
        self.c_im = d("c_im", [512, 64])
        self.ssm_d = d("ssm_d", [512])
        self.w_glu = d("w_glu", [512, 2048])
        self.relb = d("relb", [8, 257])
        self.w_au = d("w_au", [512, D])
        self.w_o = d("w_o", [D, D])
        self.g_ffn = d("g_ffn", [D])
        self.w_up = d("w_up", [D, 2 * DFF])
        self.conv_w = d("conv_w", [3, DFF])
        self.conv_b = d("conv_b", [DFF])
        self.w_down = d("w_down", [DFF, D])
        self.g_fin = d("g_fin", [D])
        o = self.dout
        self.o_yp = o("o_yp", [2, 2048, D])
        self.o_ys = o("o_ys", [16, D])
        self.o_srep = o("o_srep", [2, 32, 64])
        self.o_simp = o("o_simp", [2, 32, 64])
        self.o_kp = o("o_kp", [2, 512, 512])
        self.o_vp = o("o_vp", [2, 512, 512])
        self.o_cp = o("o_cp", [2, 2, DFF])
        self.o_sres = o("o_sres", [32, 64])
        self.o_sims = o("o_sims", [32, 64])
        self.o_ks = o("o_ks", [16, 512])
        self.o_vs = o("o_vs", [16, 512])
        self.o_cs = o("o_cs", [2, DFF])
        self.extb = self.nc.dram_tensor("extb", [8, 768], F32, kind="Internal").ap()
        self.wsc = self.nc.dram_tensor("wsc", [32, 128, 4096], BF16, kind="Internal").ap()
        self.r_wsc = [Res("wsc%d" % i) for i in range(32)]
        if DEBUG_TAPS:
            mk = lambda name, shape, dt: self.nc.dram_tensor(name, list(shape), dt, kind="ExternalOutput").ap()
            self.dbg_y = mk("dbg_y", [128, 2048], BF16)
            self.dbg_a = mk("dbg_a", [128, 2048], BF16)
            self.dbg_u = mk("dbg_u", [128, 2048], BF16)
            self.dbg_m = mk("dbg_m", [128, 4096], BF16)
            self.dbg_h = mk("dbg_h", [128, 4096], F32)

    def sb(self, name, shape, dt):
        return self.es.enter_context(self.nc.sbuf_tensor(name, list(shape), dt))

    def _alloc(self):
        sb = self.sb
        R = Res
        self.bank = [self.es.enter_context(self.nc.psum_tensor("bk%d" % i, [128, 512], F32)) for i in range(8)]
        self.rbank = [R("bk%d" % i) for i in range(8)]
        self.pool_idx = {"mm": 0, "acc": 0}
        self.pools = {"mm": [0, 1, 2, 3], "acc": [4, 5, 6, 7]}
        self.ring = [sb("ring%d" % i, [128, 4096], BF16) for i in range(NSLOT)]
        self.rring = [R("ring%d" % i) for i in range(NSLOT)]
        self.xhs = [sb("xh_%d" % j, [128, 4, D], F32) for j in range(2)]
        self.r_xhs = [[R("xh%d_%d" % (j, i)) for i in range(4)] for j in range(2)]
        self.xnb = sb("xnb", [128, D], BF16)
        self.r_xnb = R("xnb")
        self.junk = self.xnb
        self.r_junk = self.r_xnb
        self.ss = sb("ss", [128, 8], F32)
        self.r_ss = R("ss")
        self.xnT = sb("xnT", [128, 8, TB], BF16)
        self.r_xnT = [R("xnT%d" % i) for i in range(4)]
        self.kT = sb("kTr", [128, 4, 2, TB], BF16)
        self.r_kT = [R("kT0"), R("kT1")]
        self.vr = sb("vr", [128, 2, 4, 512], BF16)
        self.r_vr = [[R("vr%d_%d" % (s, t)) for t in range(4)] for s in range(2)]
        self.ones = sb("ones", [128, 128], BF16)
        self.zerob = sb("zerob", [128, 128], BF16)
        self.kmask = sb("kmask", [128, 1], F32)
        self.r_const = R("const")
        self.EM = sb("EM", [128, 8, 640], BF16)
        self.r_EM = R("EM")
        self.identf = sb("identf", [128, 128], F32)
        self.identb = sb("identb", [128, 128], BF16)
        self.gmix_fm = sb("gmix_fm", [128, 8], F32)
        self.gffn_fm = sb("gffn_fm", [128, 8], F32)
        self.gfin_row = sb("gfin_row", [128, D], F32)
        self.taps = sb("taps", [128, 3, NF], F32)
        self.cbias = sb("cbias", [128, NF], F32)
        self.hist = sb("hist", [128, NF, 2], F32)
        self.r_hist = R("hist")
        self.W1 = sb("W1", [128, 4, 8, 2, 128], BF16)
        self.TK = sb("TK", [128, 4, 8, 128], BF16)
        self.W3 = sb("W3", [128, 16, 9, 2, 32], BF16)
        self.r_s5w = R("s5w")
        self.Ere = sb("Ere", [128, 16, 64], F32)
        self.Eim = sb("Eim", [128, 16, 64], F32)
        self.rho8 = sb("rho8", [128, 16], F32)
        self.car = sb("car", [128, 2, 16], F32)
        self.r_car = R("car")
        self.ml = sb("ml", [128, 40, 16], F32)
        self.r_ml = R("ml")
        self.mli = sb("mli", [128, 16], I32)
        self.Dch = sb("Dch", [128, 4], F32)
        self.RAK = 46
        self.RA = sb("RA", [128, self.RAK * 256], F32)
        self.r_RA = [R("RA%d" % i) for i in range(self.RAK)]

    def av(self, off_kb, nbytes, dt):
        ob = int(round(off_kb * 1024))
        e0 = ob // 4
        e1 = (ob + nbytes + 3) // 4
        ap = self.RA[:, e0:e1]
        if dt == BF16:
            ap = ap.bitcast(BF16)
        res = self.r_RA[ob // 1024:(ob + nbytes - 1) // 1024 + 1]
        return ap, res

    def getbank(self, pool):
        lst = self.pools[pool]
        i = lst[self.pool_idx[pool] % len(lst)]
        self.pool_idx[pool] += 1
        return self.bank[i], self.rbank[i]

    def build_panels(self, nblocks):
        w_in, w_glu, w_au, w_o, w_up, w_down = self.w_in, self.w_glu, self.w_au, self.w_o, self.w_up, self.w_down
        per = []
        for x in range(8):
            per.append([((8, 512, 0, 512), w_in[:, 512 * x:512 * x + 512].rearrange("(k p) n -> p k n", p=128))])
        per.append([((4, 1024, 0, 1024), w_glu[:, 1024:2048].rearrange("(k p) n -> p k n", p=128))])
        per.append([((4, 1024, 0, 1024), w_glu[:, 0:1024].rearrange("(k p) n -> p k n", p=128))])
        per.append([((4, 1024, 0, 1024), w_au.rearrange("(k p) n -> p k n", p=128))])
        for nh in range(2):
            per.append([((8, 512, 0, 512), w_o[:, 512 * nh:512 * nh + 512].rearrange("(k p) n -> p k n", p=128))])
        for pp in range(11):
            per.append([((8, 512, 0, 256), w_up[:, 256 * pp:256 * pp + 256].rearrange("(k p) n -> p k n", p=128)),
                        ((8, 512, 256, 512), w_up[:, DFF + 256 * pp:DFF + 256 * pp + 256].rearrange("(k p) n -> p k n", p=128))])
        for m in range(8):
            per.append([((22, 128, 0, 128), w_down[:, 128 * m:128 * m + 128].rearrange("(f p) n -> p f n", p=128))])
        assert len(per) == 32
        self.panels = per * nblocks
        self.issued = 0
        self.pcur = 0

    def _issue_panel(self, idx):
        s = idx % NSLOT
        p = idx % 32
        if idx >= 32:
            self.S.dma("sp", "ringH%d" % s, lambda h, s=s, p=p: h.dma_start(out=self.ring[s][:], in_=self.wsc[p]),
                       reads=[self.r_wsc[p]], writes=[self.rring[s]])
            return
        for (a, b, c0, c1), src in self.panels[idx]:
            dst = self.ring[s][:, 0:a * b].rearrange("p (a b) -> p a b", a=a)[:, :, c0:c1]
            self.S.dma("pool", "ring%d" % s, lambda h, dst=dst, src=src: h.dma_start(out=dst, in_=src),
                       writes=[self.rring[s]])
        self.S.dma("sp", "wst", lambda h, s=s, p=p: h.dma_start(out=self.wsc[p], in_=self.ring[s][:]),
                   reads=[self.rring[s]], writes=[self.r_wsc[p]])

    def panel(self, a, b):
        idx = self.pcur
        self.pcur += 1
        while self.issued < min(len(self.panels), idx + NSLOT - 1):
            self._issue_panel(self.issued)
            self.issued += 1
        s = idx % NSLOT
        return self.ring[s][:, 0:a * b].rearrange("p (a b) -> p a b", a=a), self.rring[s]

    def setup_consts(self):
        S = self.S
        rc = self.r_const
        iot = self.av(38, 512, F32)[0].bitcast(I32)
        S.op("pool", lambda h: h.iota(iot[:], pattern=[[1, 128]], base=0, channel_multiplier=-1), writes=[rc])
        S.op("dve", lambda h: h.tensor_scalar(out=self.identf[:], in0=iot[:], scalar1=0.0, scalar2=None, op0=ALU.is_equal), reads=[rc], writes=[rc])
        S.op("dve", lambda h: h.tensor_copy(out=self.identb[:], in_=self.identf[:]), reads=[rc], writes=[rc])
        S.op("pool", lambda h: h.memset(self.ones[:], 1.0), writes=[rc])
        S.op("pool", lambda h: h.memset(self.zerob[:], 0.0), writes=[rc])
        S.op("dve", lambda h: h.tensor_reduce(out=self.kmask[:], in_=self.identf[:, 0:16], axis=mybir.AxisListType.X, op=ALU.add), reads=[rc], writes=[rc])
        S.op("pool", lambda h: h.memset(self.hist[:], 0.0), writes=[self.r_hist])
        dm = lambda key, out, in_: S.dma("sp", key, lambda h: h.dma_start(out=out, in_=in_, allow_slow_non_contiguous=True), writes=[rc])
        dm("c0", self.gmix_fm[:], self.g_mix.rearrange("(k p) -> p k", p=128))
        dm("c0", self.gffn_fm[:], self.g_ffn.rearrange("(k p) -> p k", p=128))
        gf = self.g_fin
        dm("c0", self.gfin_row[:], bass.AP(tensor=gf.tensor, offset=gf.offset, ap=[[0, 128], [1, D]]))
        dm("c0", self.taps[:], self.conv_w.rearrange("j (f p) -> p j f", p=128))
        dm("c0", self.cbias[:], self.conv_b.rearrange("(f p) -> p f", p=128))
        dm("c0", self.Dch[:], self.ssm_d.rearrange("(o p) -> p o", p=128))
        ext = self.av(34, 3072, F32)[0][0:8, :]
        tbl = self.av(37, 1028, F32)[0][0:8, :]
        S.dma("sp", "c1", lambda h: h.dma_start(out=tbl[:], in_=self.relb), writes=[rc])
        S.op("dve", lambda h: h.tensor_copy(out=ext[:, 0:511], in_=bcast(tbl[:, 256:257], [8, 511])), reads=[rc], writes=[rc])
        S.op("dve", lambda h: h.tensor_copy(out=ext[:, 511:768], in_=rev2(tbl[:, 0:257])), reads=[rc], writes=[rc])
        r_ext = Res("extb")
        S.dma("sp", "c2", lambda h: h.dma_start(out=self.extb, in_=ext[:]), reads=[rc], writes=[r_ext])
        bmv, bmr = self.av(0, 640 * 4, F32)
        for hh in range(8):
            src = bass.AP(tensor=self.extb.tensor, offset=self.extb.offset + hh * 768, ap=[[1, 128], [1, 640]])
            S.dma("sp", "c3", lambda h, src=src: h.dma_start(out=bmv, in_=src), reads=[r_ext], writes=bmr)
            S.op("act", lambda h, hh=hh: h.activation(out=self.EM[:, hh, :], in_=rev2(bmv), func=AF.Exp), reads=bmr, writes=[self.r_EM])
        S.op("pool", lambda h: h.memset(self.EM[0:64, :, 576:640], 0.0), writes=[self.r_EM])
        S.op("pool", lambda h: h.memset(self.EM[64:128, :, 0:64], 0.0), writes=[self.r_EM])

    def s5_prep(self):
        S = self.S
        ml, rml = self.ml, self.r_ml
        R1 = [rml]

        def dve(fn, reads=R1, writes=R1):
            return S.op("dve", fn, reads, writes)

        def act(fn, reads=R1, writes=R1):
            return S.op("act", fn, reads, writes)

        def tt(out, a, b, op):
            dve(lambda h: h.tensor_tensor(out=out, in0=a, in1=b, op=op))

        def ts(out, a, s1, op0, s2=None, op1=None):
            if op1 is None:
                dve(lambda h: h.tensor_scalar(out=out, in0=a, scalar1=s1, scalar2=None, op0=op0))
            else:
                dve(lambda h: h.tensor_scalar(out=out, in0=a, scalar1=s1, scalar2=s2, op0=op0, op1=op1))

        V = lambda i: ml[:, i, :]
        LRE, LIM, DT, T0, MAG, X, KF, RR, M1, SIN, COS, LBR, LBI, NR, DEN, KRE, KIM, T1, T2, NPI = range(20)
        P0 = 20
        dm = lambda out, in_: S.dma("sp", "p0", lambda h: h.dma_start(out=out, in_=in_, allow_slow_non_contiguous=True), writes=R1)
        dm(V(LRE), self.lam_re.rearrange("(q g) p -> (g p) q", g=2))
        dm(V(LIM), self.lam_im.rearrange("(q g) p -> (g p) q", g=2))
        ld = self.log_dt
        for gl in range(2):
            dm(ml[64 * gl:64 * gl + 64, DT, :], bass.AP(tensor=ld.tensor, offset=ld.offset + gl, ap=[[0, 64], [2, 16]]))
        act(lambda h: h.activation(out=V(DT), in_=V(DT), func=AF.Exp))
        tt(V(T0), V(LRE), V(DT), ALU.mult)
        act(lambda h: h.activation(out=V(MAG), in_=V(T0), func=AF.Exp))
        act(lambda h: h.activation(out=self.rho8[:], in_=V(T0), func=AF.Exp, scale=8.0))
        act(lambda h: h.activation(out=V(38), in_=V(T0), func=AF.Exp, scale=-8.0))
        tt(V(X), V(LIM), V(DT), ALU.mult)
        ts(V(X), V(X), 1.0 / (2 * math.pi), ALU.mult)

        def sin_turns(dst, src_off):
            ts(V(RR), V(X), src_off, ALU.add)
            dve(lambda h: h.tensor_copy(out=self.mli[:], in_=V(RR)))
            dve(lambda h: h.tensor_copy(out=V(KF), in_=self.mli[:]))
            tt(V(RR), V(RR), V(KF), ALU.subtract)
            ts(V(M1), V(RR), 0.5, ALU.is_gt)
            tt(V(RR), V(RR), V(M1), ALU.subtract)
            ts(V(M1), V(RR), -0.5, ALU.is_lt)
            tt(V(RR), V(RR), V(M1), ALU.add)
            act(lambda h: h.activation(out=V(dst), in_=V(RR), func=AF.Sin, scale=2 * math.pi))

        sin_turns(SIN, 0.0)
        sin_turns(COS, 0.25)
        tt(V(LBR), V(MAG), V(COS), ALU.mult)
        tt(V(LBI), V(MAG), V(SIN), ALU.mult)
        ts(V(NR), V(LBR), -1.0, ALU.add)
        tt(V(T1), V(LRE), V(LRE), ALU.mult)
        tt(V(T2), V(LIM), V(LIM), ALU.mult)
        tt(V(DEN), V(T1), V(T2), ALU.add)
        dve(lambda h: h.reciprocal(out=V(DEN), in_=V(DEN)))
        tt(V(T1), V(NR), V(LRE), ALU.mult)
        tt(V(T2), V(LBI), V(LIM), ALU.mult)
        tt(V(T1), V(T1), V(T2), ALU.add)
        tt(V(KRE), V(T1), V(DEN), ALU.mult)
        tt(V(T1), V(LBI), V(LRE), ALU.mult)
        tt(V(T2), V(NR), V(LIM), ALU.mult)
        tt(V(T1), V(T1), V(T2), ALU.subtract)
        tt(V(KIM), V(T1), V(DEN), ALU.mult)
        PR = lambda n: ml[:, P0 + n, :]
        PI_ = lambda n: ml[:, P0 + 9 + n, :]
        dve(lambda h: h.memset(PR(0), 1.0))
        dve(lambda h: h.memset(PI_(0), 0.0))
        for n in range(1, 9):
            tt(V(T1), PR(n - 1), V(LBR), ALU.mult)
            tt(V(T2), PI_(n - 1), V(LBI), ALU.mult)
            tt(PR(n), V(T1), V(T2), ALU.subtract)
            tt(V(T1), PR(n - 1), V(LBI), ALU.mult)
            tt(V(T2), PI_(n - 1), V(LBR), ALU.mult)
            tt(PI_(n), V(T1), V(T2), ALU.add)
        Ere, Eim = self.Ere, self.Eim
        tt(Ere[:, :, 0], PR(8), V(38), ALU.mult)
        tt(Eim[:, :, 0], PI_(8), V(38), ALU.mult)
        tmpA, rA = self.av(40, 16 * 32 * 4, F32)
        tmpB, rB = self.av(42, 16 * 32 * 4, F32)
        RE = R1 + rA + rB
        for k in range(6):
            m = 1 << k
            last_re = bcast(Ere[:, :, m - 1:m], [128, 16, m])
            last_im = bcast(Eim[:, :, m - 1:m], [128, 16, m])
            a3 = tmpA.rearrange("p (q c) -> p q c", q=16)[:, :, 0:m]
            b3 = tmpB.rearrange("p (q c) -> p q c", q=16)[:, :, 0:m]
            src_re, src_im = Ere[:, :, 0:m], Eim[:, :, 0:m]
            dst_re, dst_im = Ere[:, :, m:2 * m], Eim[:, :, m:2 * m]
            S.op("dve", lambda h, a3=a3, s=src_re, l=last_re: h.tensor_tensor(out=a3, in0=s, in1=l, op=ALU.mult), RE, RE)
            S.op("dve", lambda h, b3=b3, s=src_im, l=last_im: h.tensor_tensor(out=b3, in0=s, in1=l, op=ALU.mult), RE, RE)
            S.op("dve", lambda h, a3=a3, b3=b3, d=dst_re: h.tensor_tensor(out=d, in0=a3, in1=b3, op=ALU.subtract), RE, RE)
            S.op("dve", lambda h, a3=a3, s=src_re, l=last_im: h.tensor_tensor(out=a3, in0=s, in1=l, op=ALU.mult), RE, RE)
            S.op("dve", lambda h, b3=b3, s=src_im, l=last_re: h.tensor_tensor(out=b3, in0=s, in1=l, op=ALU.mult), RE, RE)
            S.op("dve", lambda h, a3=a3, b3=b3, d=dst_im: h.tensor_tensor(out=d, in0=a3, in1=b3, op=ALU.add), RE, RE)
        ts(V(NPI), PI_(0), -1.0, ALU.mult)

        bre, r_bre = self.av(0, 1024, F32)
        bim, r_bim = self.av(1, 1024, F32)
        bbr, r_bbr = self.av(2, 1024, F32)
        bbi, r_bbi = self.av(3, 1024, F32)
        t1v, r_t1 = self.av(4, 1024, F32)
        t2v, r_t2 = self.av(5, 1024, F32)
        v3 = lambda ap: ap.rearrange("p (q h) -> p q h", q=16)
        for gl in range(2):
            for (dst, src, rr) in ((bre, self.b_re, r_bre), (bim, self.b_im, r_bim)):
                s_ap = bass.AP(tensor=src.tensor, offset=src.offset + gl * 1024, ap=[[16, 64], [2048, 16], [1, 16]])
                S.dma("sp", "p1", lambda h, d=v3(dst)[64 * gl:64 * gl + 64], s=s_ap: h.dma_start(out=d, in_=s), writes=rr)
        RB = R1 + r_bre + r_bim + r_bbr + r_bbi + r_t1 + r_t2
        kre_b = bcast(ml[:, KRE, :].unsqueeze(2), [128, 16, 16])
        kim_b = bcast(ml[:, KIM, :].unsqueeze(2), [128, 16, 16])
        db = lambda fn: S.op("dve", fn, RB, RB)
        db(lambda h: h.tensor_tensor(out=v3(t1v), in0=v3(bre), in1=kre_b, op=ALU.mult))
        db(lambda h: h.tensor_tensor(out=v3(t2v), in0=v3(bim), in1=kim_b, op=ALU.mult))
        db(lambda h: h.tensor_tensor(out=bbr, in0=t1v, in1=t2v, op=ALU.subtract))
        db(lambda h: h.tensor_tensor(out=v3(t1v), in0=v3(bim), in1=kre_b, op=ALU.mult))
        db(lambda h: h.tensor_tensor(out=v3(t2v), in0=v3(bre), in1=kim_b, op=ALU.mult))
        db(lambda h: h.tensor_tensor(out=bbi, in0=t1v, in1=t2v, op=ALU.add))
        lbr, r_lbr = self.av(6, 8192, F32)
        lbi, r_lbi = self.av(14, 8192, F32)
        u1, r_u1 = self.av(22, 8192, F32)
        u2, r_u2 = self.av(30, 8192, F32)
        v4 = lambda ap: ap.rearrange("p (j q h) -> p j q h", j=8, q=16)
        RL = RB + r_lbr + r_lbi + r_u1 + r_u2
        mlap = self.ml[:, P0 + 7, :]
        prr = bass.AP(tensor=self.ml, offset=mlap.offset, ap=[list(mlap.ap[0]), [-16, 8], [1, 16], [0, 16]])
        mlap2 = self.ml[:, P0 + 9 + 7, :]
        pir = bass.AP(tensor=self.ml, offset=mlap2.offset, ap=[list(mlap2.ap[0]), [-16, 8], [1, 16], [0, 16]])
        bbr4 = bcast(v3(bbr).unsqueeze(1), [128, 8, 16, 16])
        bbi4 = bcast(v3(bbi).unsqueeze(1), [128, 8, 16, 16])
        dl = lambda fn: S.op("dve", fn, RL, RL)
        dl(lambda h: h.tensor_tensor(out=v4(u1), in0=bbr4, in1=prr, op=ALU.mult))
        dl(lambda h: h.tensor_tensor(out=v4(u2), in0=bbi4, in1=pir, op=ALU.mult))
        dl(lambda h: h.tensor_tensor(out=lbr, in0=u1, in1=u2, op=ALU.subtract))
        dl(lambda h: h.tensor_tensor(out=v4(u1), in0=bbr4, in1=pir, op=ALU.mult))
        dl(lambda h: h.tensor_tensor(out=v4(u2), in0=bbi4, in1=prr, op=ALU.mult))
        dl(lambda h: h.tensor_tensor(out=lbi, in0=u1, in1=u2, op=ALU.add))
        tp = [self.av(22 + 2 * ri, 2048, BF16) for ri in range(2)]
        RT = RL + tp[0][1] + tp[1][1]
        lbs = (lbr, lbi)
        for r in range(4):
            for ri in range(2):
                S.op("pool", lambda h, t=tp[ri][0]: h.memset(t, 0.0), RT, RT)
            for o in range(4):
                q = 4 * o + r
                for ri in range(2):
                    t3 = tp[ri][0].rearrange("p (j c) -> p j c", j=8)
                    for gl in range(2):
                        src = v4(lbs[ri])[64 * gl:64 * gl + 64, :, q, :]
                        dst = t3[64 * gl:64 * gl + 64, :, 32 * r + 16 * gl:32 * r + 16 * gl + 16]
                        S.op("dve" if gl == 0 else "act",
                             (lambda h, d=dst, s=src: h.tensor_copy(out=d, in_=s)) if gl == 0 else
                             (lambda h, d=dst, s=src: h.activation(out=d, in_=s, func=AF.Copy)), RT, RT)
                    bk, rbk = self.getbank("mm")
                    bkb = bk[:].bitcast(BF16).rearrange("p (j c) -> p j c", j=8)
                    for j in range(8):
                        S.op("pe", lambda h, j=j, bkb=bkb, t3=t3: h.transpose(out=bkb[:, j, :], in_=t3[:, j, :], identity=self.identb[:]),
                             RT + [self.r_const], [rbk])
                    S.op("act" if ri == 0 else "dve",
                         (lambda h, o=o, r=r, ri=ri, bkb=bkb: h.activation(out=self.W1[32 * r:32 * r + 32, o, :, ri, :], in_=bkb[32 * r:32 * r + 32, :, :], func=AF.Copy)) if ri == 0 else
                         (lambda h, o=o, r=r, ri=ri, bkb=bkb: h.tensor_copy(out=self.W1[32 * r:32 * r + 32, o, :, ri, :], in_=bkb[32 * r:32 * r + 32, :, :])),
                         [rbk], [self.r_s5w])
        cml = [self.av(26 + ri, 1024, F32) for ri in range(2)]
        zst, r_z = self.av(28, 128 * 4, F32)
        RC = R1 + cml[0][1] + cml[1][1] + r_z
        for ri, csrc in enumerate((self.c_re, self.c_im)):
            for t in range(4):
                s_ap = bass.AP(tensor=csrc.tensor, offset=csrc.offset + t * 128 * 64, ap=[[64, 128], [0, 2], [1, 64]])
                S.dma("sp", "p2", lambda h, s=s_ap: h.dma_start(out=zst.rearrange("p (a b) -> p a b", a=2), in_=s), reads=r_z, writes=r_z)
                bk, rbk = self.getbank("mm")
                S.op("pe", lambda h, bk=bk: h.transpose(out=bk[:, 0:128], in_=zst, identity=self.identf[:]), r_z + [self.r_const], [rbk])
                c3 = cml[ri][0].rearrange("p (q h) -> p q h", q=16)
                bk3 = bk[:, 0:128].rearrange("p (q g h) -> p q g h", q=4, g=2)
                for gl in range(2):
                    S.op("dve", lambda h, gl=gl, t=t, c3=c3, bk3=bk3: h.tensor_copy(out=c3[64 * gl:64 * gl + 64, 4 * t:4 * t + 4, :], in_=bk3[64 * gl:64 * gl + 64, :, gl, :]),
                         [rbk], cml[ri][1])
        w1v, r_w1 = self.av(6, 9 * 1024, F32)
        w2v, r_w2 = self.av(15, 9 * 1024, F32)
        v9 = lambda ap: ap.rearrange("p (d q h) -> p d q h", d=9, q=16)
        RW = RC + r_w1 + r_w2 + [self.r_s5w]
        S.op("pool", lambda h: h.memset(self.W3[:], 0.0), [self.r_s5w], [self.r_s5w])
        cre4 = bcast(cml[0][0].rearrange("p (q h) -> p q h", q=16).unsqueeze(1), [128, 9, 16, 16])
        cim4 = bcast(cml[1][0].rearrange("p (q h) -> p q h", q=16).unsqueeze(1), [128, 9, 16, 16])
        pr9 = bcast(self.ml[:, P0:P0 + 9, :].unsqueeze(3), [128, 9, 16, 16])
        pi9 = bcast(self.ml[:, P0 + 9:P0 + 18, :].unsqueeze(3), [128, 9, 16, 16])
        dw = lambda fn: S.op("dve", fn, RW, RW)
        W3v = self.W3
        dw(lambda h: h.tensor_tensor(out=v9(w1v), in0=cre4, in1=pr9, op=ALU.mult))
        dw(lambda h: h.tensor_tensor(out=v9(w2v), in0=cim4, in1=pi9, op=ALU.mult))
        for gl in range(2):
            dw(lambda h, gl=gl: h.tensor_tensor(out=W3v[64 * gl:64 * gl + 64, :, :, 0, 16 * gl:16 * gl + 16].rearrange("p q d h -> p d q h"),
                                                in0=v9(w1v)[64 * gl:64 * gl + 64], in1=v9(w2v)[64 * gl:64 * gl + 64], op=ALU.subtract))
        dw(lambda h: h.tensor_tensor(out=v9(w1v), in0=cre4, in1=pi9, op=ALU.mult))
        dw(lambda h: h.tensor_tensor(out=v9(w2v), in0=cim4, in1=pr9, op=ALU.mult))
        dw(lambda h: h.tensor_tensor(out=w1v, in0=w1v, in1=w2v, op=ALU.add))
        for gl in range(2):
            dw(lambda h, gl=gl: h.tensor_scalar(out=W3v[64 * gl:64 * gl + 64, :, :, 1, 16 * gl:16 * gl + 16].rearrange("p q d h -> p d q h"),
                                                in0=v9(w1v)[64 * gl:64 * gl + 64], scalar1=-1.0, scalar2=None, op0=ALU.mult))
        S.op("pool", lambda h: h.memset(self.TK[:], 0.0), [self.r_s5w], [self.r_s5w])
        tb = [self.av(30 + ri, 256, BF16) for ri in range(2)]
        RTB = RB + tb[0][1] + tb[1][1]
        bbs = (bbr, bbi)
        for r in range(4):
            for ri in range(2):
                S.op("pool", lambda h, t=tb[ri][0]: h.memset(t, 0.0), RTB, RTB)
            for o in range(4):
                q = 4 * o + r
                for ri in range(2):
                    for gl in range(2):
                        src = v3(bbs[ri])[64 * gl:64 * gl + 64, q, :]
                        dst = tb[ri][0][64 * gl:64 * gl + 64, 32 * r + 16 * gl:32 * r + 16 * gl + 16]
                        S.op("dve", lambda h, d=dst, s=src: h.tensor_copy(out=d, in_=s), RTB, RTB)
                bk, rbk = self.getbank("mm")
                for d_ in range(8):
                    for ri in range(2):
                        S.op("pe", lambda h, ri=ri, d_=d_, q=q, bk=bk: h.matmul(bk[:, 32 * d_:32 * d_ + 32], lhsT=tb[ri][0], rhs=W3v[:, q, d_, ri, :],
                                                                               start=(ri == 0), stop=(ri == 1)), RTB + [self.r_s5w], [rbk])
                S.op("act", lambda h, o=o, r=r, bk=bk: h.activation(out=self.TK[32 * r:32 * r + 32, o, :, 32 * r:32 * r + 32],
                                                                    in_=bk[32 * r:32 * r + 32, 0:256].rearrange("p (d c) -> p d c", d=8), func=AF.Copy),
                     [rbk], [self.r_s5w])
        for o in range(4):
            S.op("dve", lambda h, o=o: h.scalar_tensor_tensor(out=self.TK[:, o, 0, :], in0=self.identb[:], scalar=self.Dch[:, o:o + 1], in1=self.TK[:, o, 0, :],
                                                              op0=ALU.mult, op1=ALU.add), [self.r_s5w, self.r_const], [self.r_s5w])

    def norm_T(self, n, gfm, outT, r_outT, xh, r_xh, tts=None):
        S = self.S
        ntt = (n + 127) // 128
        for tt in (range(ntt) if tts is None else tts):
            tn = min(128, n - tt * 128)
            xin = xh[0:tn, tt, :]
            ssv = self.ss[0:tn, tt:tt + 1]
            S.op("act", lambda h, xin=xin, ssv=ssv, tn=tn: h.activation(out=self.junk[0:tn, :], in_=xin, func=AF.Square, accum_out=ssv),
                 [r_xh[tt]], [self.r_junk, self.r_ss])
            S.op("act", lambda h, ssv=ssv: h.activation(out=ssv, in_=ssv, func=AF.Sqrt, scale=1.0 / D, bias=EPS), [self.r_ss], [self.r_ss])
            S.op("dve", lambda h, ssv=ssv: h.reciprocal(out=ssv, in_=ssv), [self.r_ss], [self.r_ss])
            S.op("act", lambda h, xin=xin, ssv=ssv, tn=tn: h.activation(out=self.xnb[0:tn, :], in_=xin, func=AF.Copy, scale=ssv),
                 [r_xh[tt], self.r_ss], [self.r_xnb])
            bk, rbk = self.getbank("mm")
            bkb = bk[:].bitcast(BF16).rearrange("p (k c) -> p k c", k=8)
            for kt in range(8):
                S.op("pe", lambda h, kt=kt, tn=tn, bkb=bkb: h.transpose(out=bkb[:, kt, 0:tn], in_=self.xnb[0:tn, 128 * kt:128 * kt + 128], identity=self.identb[0:tn, 0:tn]),
                     [self.r_xnb, self.r_const], [rbk])
            S.op("dve", lambda h, tt=tt, tn=tn, bkb=bkb: h.tensor_tensor(out=outT[:, :, 128 * tt:128 * tt + tn], in0=bkb[:, :, 0:tn],
                                                                      in1=bcast(gfm[:, :].unsqueeze(2), [128, 8, tn]), op=ALU.mult),
                 [rbk, self.r_const], [r_outT[tt]])

    def stage1(self, seq, b, gi, load_only=False):
        S = self.S
        n = seq["n"]
        nr = seq.get("nreal", n)
        ntt = (n + 127) // 128
        T0 = b * TB
        x_ap = seq["x"]
        xh, r_xh = self.xhs[gi % 2], self.r_xhs[gi % 2]
        for tt in range(ntt):
            tn = min(128, n - tt * 128)
            tnr = min(tn, nr - tt * 128)
            if tnr < tn:
                S.op("pool", lambda h, tt=tt, tn=tn: h.memset(xh[0:tn, tt, :], 0.0), [], [r_xh[tt]])
            S.dma("sp", "xh%d_%d" % (gi % 2, tt), lambda h, tt=tt, tnr=tnr: h.dma_start(out=xh[0:tnr, tt, :], in_=x_ap[T0 + 128 * tt:T0 + 128 * tt + tnr, :]), writes=[r_xh[tt]])
        if not load_only:
            self.norm_T(n, self.gmix_fm, self.xnT, self.r_xnT, xh, r_xh)

    def stage1_norm(self, seq, gi, tt):
        n = seq["n"]
        if tt < (n + 127) // 128:
            self.norm_T(n, self.gmix_fm, self.xnT, self.r_xnT, self.xhs[gi % 2], self.r_xhs[gi % 2], tts=[tt])

    def block(self, seq, b, gi, pre8=None):
        S = self.S
        n = seq["n"]
        nr = seq.get("nreal", n)
        ntt = (n + 127) // 128
        T0 = b * TB
        cur = seq["slot"](b)
        prev = 1 - cur
        has_prev = seq["has_prev"](b)
        is_last = seq["is_last"](b)
        rxnT = self.r_xnT[:ntt]
        xh, r_xh = self.xhs[gi % 2], self.r_xhs[gi % 2]
        chk("s1")
        uT, r_uT = self.av(0, 4096, BF16)
        qT, r_qT = self.av(4, 4096, BF16)
        yT, r_yT = self.av(8, 4096, BF16)
        aT, r_aT = self.av(12, 4096, BF16)
        uT3, qT3, yT3, aT3 = [v.rearrange("p (f t) -> p f t", f=4) for v in (uT, qT, yT, aT)]
        evac_i = [0]

        def evac_copy(out, in_, reads, writes, eng=None):
            evac_i[0] += 1
            if eng == "act" or (eng is None and evac_i[0] % 2):
                S.op("act", lambda h: h.activation(out=out, in_=in_, func=AF.Copy), reads, writes)
            else:
                S.op("dve", lambda h: h.tensor_copy(out=out, in_=in_), reads, writes)

        def proj_fm(pan, rpan, nk, col0, rhs3, r_rhs, width=128):
            bk, rbk = self.getbank("mm")
            for kt in range(nk):
                S.op("pe", lambda h, kt=kt, bk=bk: h.matmul(bk[0:width, 0:n], lhsT=pan[:, kt, col0:col0 + width], rhs=rhs3[:, kt, 0:n],
                                                            start=(kt == 0), stop=(kt == nk - 1)), [rpan] + list(r_rhs), [rbk])
            return bk, rbk

        def proj_tm(pan, rpan, nk, lhs3, r_lhs, tt, tn, c0=0, c1=512):
            bk, rbk = self.getbank("mm")
            for kt in range(nk):
                S.op("pe", lambda h, kt=kt, bk=bk: h.matmul(bk[0:tn, 0:c1 - c0], lhsT=lhs3[:, kt, 128 * tt:128 * tt + tn], rhs=pan[:, kt, c0:c1],
                                                            start=(kt == 0), stop=(kt == nk - 1)), [rpan] + list(r_lhs), [rbk])
            return bk, rbk

        stg, r_stg = self.av(44, 2048, F32)
        pan, rpan = self.panel(8, 512)
        for ft in range(4):
            bk, rbk = proj_fm(pan, rpan, 8, 128 * ft, self.xnT, rxnT)
            evac_copy(uT3[:, ft, 0:n], bk[:, 0:n], [rbk], [r_uT[ft]], eng="act")
        chk("s2u")
        pan, rpan = self.panel(8, 512)
        for ft in range(4):
            bk, rbk = proj_fm(pan, rpan, 8, 128 * ft, self.xnT, rxnT)
            evac_copy(qT3[:, ft, 0:n], bk[:, 0:n], [rbk], [r_qT[ft]], eng="act")
        pan, rpan = self.panel(8, 512)
        for ft in range(4):
            bk, rbk = proj_fm(pan, rpan, 8, 128 * ft, self.xnT, rxnT)
            evac_copy(self.kT[:, ft, cur, 0:n], bk[:, 0:n], [rbk], [self.r_kT[cur]], eng="act")
        chk("s2k")
        if is_last:
            for tt in range(ntt):
                tn = min(128, n - tt * 128)
                bk, rbk = proj_tm(pan, rpan, 8, self.xnT, rxnT, tt, tn)
                evac_copy(stg[0:tn, :], bk[0:tn, :], [rbk], r_stg)
                tnr = min(tn, nr - tt * 128)
                dst = seq["o_k"][128 * tt:128 * tt + tnr, :]
                self.out_res.append(Res("ok"))
                S.dma("sp", "stg", lambda h, dst=dst, tnr=tnr: h.dma_start(out=dst, in_=stg[0:tnr, :]), reads=r_stg, writes=[self.out_res[-1]])
        chk("s2kt")
        pan, rpan = self.panel(8, 512)
        for tt in range(ntt):
            tn = min(128, n - tt * 128)
            bk, rbk = proj_tm(pan, rpan, 8, self.xnT, rxnT, tt, tn)
            evac_copy(self.vr[0:tn, cur, tt, :], bk[0:tn, :], [rbk], [self.r_vr[cur][tt]], eng="act")
            if is_last:
                evac_copy(stg[0:tn, :], bk[0:tn, :], [rbk], r_stg)
                tnr = min(tn, nr - tt * 128)
                dst = seq["o_v"][128 * tt:128 * tt + tnr, :]
                self.out_res.append(Res("ov"))
                S.dma("sp", "stg", lambda h, dst=dst, tnr=tnr: h.dma_start(out=dst, in_=stg[0:tnr, :]), reads=r_stg, writes=[self.out_res[-1]])

        chk("s2")
        NCb = n // 8
        Lv, r_L = self.av(16, 8192, F32)
        Tv, r_T = self.av(24, 8192, F32)
        Spv, r_Sp = self.av(32, 4096, BF16)
        L4 = Lv.rearrange("p (r q c) -> p r q c", r=2, q=16)
        T4 = Tv.rearrange("p (r q c) -> p r q c", r=2, q=16)
        Sp4 = Spv.rearrange("p (r q c) -> p r q c", r=2, q=16)
        sl = [self.getbank("mm") for _ in range(4)]
        sl4 = [bk_[:].rearrange("p (a r c) -> p a r c", a=4, r=2) for (bk_, _) in sl]
        for o in range(4):
            for ri in range(2):
                for j in range(8):
                    for r in range(4):
                        S.op("pe", lambda h, o=o, r=r, ri=ri, j=j: h.matmul(
                            sl4[r][:, o, ri, 0:NCb], lhsT=self.W1[32 * r:32 * r + 32, o, j, ri, :], rhs=uT3[32 * r:32 * r + 32, o, j:n:8],
                            start=(j == 0), stop=(j == 7), tile_position=(32 * r, 0)), [self.r_s5w, r_uT[o]], [sl[r][1]])
        for r in range(4):
            evac_copy(L4[:, :, r:16:4, 0:NCb].rearrange("p r a c -> p a r c"), sl4[r][:, :, :, 0:NCb], [sl[r][1]], r_L, eng="act")
        chk("s3a")
        Er = self.Ere[:, :, 0:NCb]
        Ei = self.Eim[:, :, 0:NCb]
        RS = r_L + r_T + [self.r_s5w]
        Lr, Li = L4[:, 0, :, 0:NCb], L4[:, 1, :, 0:NCb]
        Tr, Ti = T4[:, 0, :, 0:NCb], T4[:, 1, :, 0:NCb]
        chain = []
        dv = lambda fn: chain.append((fn, RS, RS))
        dv(lambda h: h.tensor_tensor(out=Tr, in0=Er, in1=Lr, op=ALU.mult))
        dv(lambda h: h.tensor_tensor(out=Ti, in0=Ei, in1=Li, op=ALU.mult))
        dv(lambda h: h.tensor_tensor(out=Tr, in0=Tr, in1=Ti, op=ALU.add))
        dv(lambda h: h.tensor_tensor(out=Ti, in0=Er, in1=Li, op=ALU.mult))
        dv(lambda h: h.tensor_tensor(out=Lr, in0=Ei, in1=Lr, op=ALU.mult))
        dv(lambda h: h.tensor_tensor(out=Ti, in0=Ti, in1=Lr, op=ALU.subtract))
        for ri in range(2):
            for q in range(16):
                chain.append((lambda h, ri=ri, q=q: h.tensor_tensor_scan(out=L4[:, ri, q, 0:NCb], data0=bcast(self.rho8[:, q:q + 1], [128, NCb]),
                                                                       data1=T4[:, ri, q, 0:NCb], initial=self.car[:, ri, q:q + 1],
                                                                       op0=ALU.mult, op1=ALU.add), RS + [self.r_car], RS))
        chk("s3b")
        dv(lambda h: h.tensor_tensor(out=Tr, in0=Er, in1=Lr, op=ALU.mult))
        dv(lambda h: h.tensor_tensor(out=Ti, in0=Ei, in1=Li, op=ALU.mult))
        dv(lambda h: h.tensor_tensor(out=Tr, in0=Tr, in1=Ti, op=ALU.subtract))
        dv(lambda h: h.tensor_tensor(out=Ti, in0=Er, in1=Li, op=ALU.mult))
        dv(lambda h: h.tensor_tensor(out=Lr, in0=Ei, in1=Lr, op=ALU.mult))
        dv(lambda h: h.tensor_tensor(out=Ti, in0=Ti, in1=Lr, op=ALU.add))
        def s5_y():
            RSP = RS + r_Sp + [self.r_car]
            S.op("act", lambda h: h.activation(out=Sp4[:, :, :, 0:1], in_=self.car[:, :, :].unsqueeze(3), func=AF.Copy), RSP, RSP)
            if NCb > 1:
                S.op("act", lambda h: h.activation(out=Sp4[:, :, :, 1:NCb], in_=T4[:, :, :, 0:NCb - 1], func=AF.Copy), RSP, RSP)
            NCr = nr // 8
            S.op("dve", lambda h: h.tensor_copy(out=self.car[:, :, :].unsqueeze(3), in_=T4[:, :, :, NCr - 1:NCr]), RSP, RSP)
            chk("s3c")
            for o in range(4):
                bk, rbk = self.getbank("mm")
                Y3 = bk[:, 0:8 * NCb].rearrange("p (i c) -> p i c", i=8)
                uj = uT3[:, o, 0:n].rearrange("p (c j) -> p j c", j=8)

                def toep_lag(d, first, last, o=o, Y3=Y3, uj=uj):
                    S.op("pe", lambda h: h.matmul(Y3[:, d:8, 0:NCb], lhsT=self.TK[:, o, d, :], rhs=uj[:, 0:8 - d, :],
                                                  start=first, stop=last), [self.r_s5w, r_uT[o]], [rbk])
                toep_lag(0, True, False)
                for i in range(8):
                    for ri in range(2):
                        for r in range(4):
                            q = 4 * o + r
                            S.op("pe", lambda h, r=r, q=q, ri=ri, i=i, Y3=Y3: h.matmul(Y3[32 * r:32 * r + 32, i, 0:NCb], lhsT=self.W3[:, q, i + 1, ri, :],
                                                                                       rhs=Sp4[:, ri, q, 0:NCb], start=False, stop=False, tile_position=(0, 32 * r)),
                                 [self.r_s5w] + r_Sp, [rbk])
                for d in range(1, 8):
                    toep_lag(d, False, d == 7)
                S.op("act", lambda h, o=o, Y3=Y3: h.activation(out=yT3[:, o, 0:n].rearrange("p (c i) -> p i c", i=8), in_=Y3[:, :, 0:NCb], func=AF.Gelu_apprx_tanh),
                     [rbk], [r_yT[o]])
            if is_last:
                bk, rbk = self.getbank("mm")
                S.op("pe", lambda h, bk=bk: h.transpose(out=bk[0:32, 0:128], in_=self.car[:, :, :].rearrange("p r q -> p (r q)"), identity=self.identf[:]),
                     [self.r_car, self.r_const], [rbk])
                so, r_so = self.av(40, 512, F32)
                evac_copy(so[0:32, 0:128], bk[0:32, 0:128], [rbk], r_so)
                for ri, dst in enumerate((seq["o_sre"], seq["o_sim"])):
                    self.out_res.append(Res("os"))
                    S.dma("sp", "so", lambda h, ri=ri, dst=dst: h.dma_start(out=dst.rearrange("(q g) p -> q g p", g=2),
                                                                            in_=so[16 * ri:16 * ri + 16, 0:128].rearrange("q (g p) -> q g p", g=2)),
                          reads=r_so, writes=[self.out_res[-1]])

        chk("s3")
        tap = DEBUG_TAPS and seq.get("first") and b == 0
        if tap:
            self.out_res.append(Res("dbg"))
            S.dma("sp", "dbg", lambda h: h.dma_start(out=self.dbg_y, in_=yT), reads=r_yT, writes=[self.out_res[-1]])
            self.out_res.append(Res("dbg"))
            S.dma("sp", "dbg", lambda h: h.dma_start(out=self.dbg_u, in_=uT), reads=r_uT, writes=[self.out_res[-1]])
        exv = [self.av(o_, 1024, BF16) for o_ in (36, 37, 42, 43)]
        pmv = [self.av(o_, 1024, BF16) for o_ in (38, 39, 44, 45)]
        recv, r_rec = self.av(40, 2048, F32)
        tiles = [(4 + t, cur, t, min(128, n - 128 * t)) for t in range(ntt)]
        padded_keys = nr < n
        if has_prev:
            tiles += [(t, prev, t, 128) for t in range(4)]
        it = 0
        heads = {}

        def hctx(hh):
            if hh not in heads:
                ob, r_ob = self.getbank("acc")
                sbk, r_sb = self.getbank("acc")
                heads[hh] = (hh // 2, 64 * (hh % 2), ob, r_ob, sbk, r_sb)
            return heads[hh]

        def front(hh, ti):
            nonlocal it
            hp, po, ob, r_ob, sbk, r_sb = hctx(hh)
            (arel, slot, t, kc) = tiles[ti]
            cq_min = max(8, 2 * arel)
            cq_max = min(15, 2 * arel + 9)
            c0 = 64 * (cq_min - 8)
            c1 = min(n, 64 * (cq_max - 7))
            N = c1 - c0
            qq0 = 512 + c0 - 128 * arel
            stb, r_st = self.getbank("mm")
            S.op("pe", lambda h: h.matmul(stb[0:kc, 0:N], lhsT=self.kT[po:po + 64, hp, slot, 128 * t:128 * t + kc], rhs=qT3[po:po + 64, hp, c0:c1], start=True, stop=True),
                 [self.r_kT[slot], r_qT[hp]], [r_st])
            ex, r_ex = exv[it % 4]
            pm, r_pm = pmv[it % 4]
            it += 1
            S.op("act", lambda h: h.activation(out=ex[0:kc, 0:N], in_=stb[0:kc, 0:N], func=AF.Exp, scale=0.125), [r_st], r_ex)
            if padded_keys and slot == cur:
                S.op("dve", lambda h: h.scalar_tensor_tensor(out=pm[0:kc, 0:N], in0=ex[0:kc, 0:N], scalar=self.kmask[0:kc, 0:1], in1=self.EM[0:kc, hh, qq0:qq0 + N],
                                                             op0=ALU.mult, op1=ALU.mult), r_ex + [self.r_EM, self.r_const], r_pm)
            else:
                S.op("pool", lambda h: h.tensor_tensor(out=pm[0:kc, 0:N], in0=ex[0:kc, 0:N], in1=self.EM[0:kc, hh, qq0:qq0 + N], op=ALU.mult), r_ex + [self.r_EM], r_pm)
            return (pm, r_pm, c0, c1, N, kc, slot, t)

        def back(hh, ti, st):
            hp, po, ob, r_ob, sbk, r_sb = hctx(hh)
            (pm, r_pm, c0, c1, N, kc, slot, t) = st
            first = (ti == 0)
            last = (ti == len(tiles) - 1)
            S.op("pe", lambda h: h.matmul(ob[:, c0:c1], lhsT=self.vr[0:kc, slot, t, 128 * hp:128 * hp + 128], rhs=pm[0:kc, 0:N], start=first, stop=last),
                 r_pm + [self.r_vr[slot][t]], [r_ob])
            S.op("pe", lambda h: h.matmul(sbk[:, c0:c1], lhsT=self.ones[0:kc, :], rhs=pm[0:kc, 0:N], start=first, stop=last),
                 r_pm + [self.r_const], [r_sb])

        def finish(hh):
            hp, po, ob, r_ob, sbk, r_sb = hctx(hh)
            S.op("act", lambda h: h.activation(out=recv[po:po + 64, 0:n], in_=sbk[po:po + 64, 0:n], func=AF.Ln), [r_sb], r_rec)
            S.op("act", lambda h: h.activation(out=recv[po:po + 64, 0:n], in_=recv[po:po + 64, 0:n], func=AF.Exp, scale=-1.0), r_rec, r_rec)
            S.op("dve", lambda h: h.tensor_tensor(out=aT3[po:po + 64, hp, 0:n], in0=ob[po:po + 64, 0:n], in1=recv[po:po + 64, 0:n], op=ALU.mult),
                 [r_ob] + r_rec, [r_aT[hp]])
            for _ in range((44 + 7) // 8):
                if chain:
                    fn_, rd_, wr_ = chain.pop(0)
                    S.op("dve", fn_, rd_, wr_)

        items = [(hh, ti) for hh in range(8) for ti in range(len(tiles))]
        SKEW = 2
        pend = [front(*items[k_]) for k_ in range(min(SKEW, len(items)))]
        for k_, (hh, ti) in enumerate(items):
            if k_ + SKEW < len(items):
                pend.append(front(*items[k_ + SKEW]))
            back(hh, ti, pend.pop(0))
            if ti == len(tiles) - 1:
                finish(hh)

        if tap:
            self.out_res.append(Res("dbg"))
            S.dma("sp", "dbg", lambda h: h.dma_start(out=self.dbg_a, in_=aT), reads=r_aT, writes=[self.out_res[-1]])
        while chain:
            fn_, rd_, wr_ = chain.pop(0)
            S.op("dve", fn_, rd_, wr_)
        s5_y()
        chk("s4")
        G1, r_G1 = self.av(16, 8192, BF16)
        G2, r_G2 = self.av(24, 8192, BF16)
        G13 = G1.rearrange("p (m t) -> p m t", m=8)
        G23 = G2.rearrange("p (m t) -> p m t", m=8)
        g3v = [self.av(32 + i, 1024, BF16) for i in range(2)]
        for (G3_, rG) in ((G13, r_G1), (G23, r_G2)):
            for half in range(2):
                pan, rpan = self.panel(8, 512)
                for c in range(4):
                    m = 4 * half + c
                    bk, rbk = proj_fm(pan, rpan, 8, 128 * c, self.xnT, rxnT)
                    S.op("act", lambda h, G3_=G3_, m=m, bk=bk: h.activation(out=G3_[:, m, 0:n], in_=bk[:, 0:n], func=AF.Sigmoid), [rbk], [rG[m]])
        panB, rpanB = self.panel(4, 1024)
        panA, rpanA = self.panel(4, 1024)
        for m in range(8):
            bkb_, rbkb = proj_fm(panB, rpanB, 4, 128 * m, yT3, r_yT)
            g3, r_g3 = g3v[m % 2]
            S.op("act", lambda h, g3=g3, bkb_=bkb_: h.activation(out=g3[:, 0:n], in_=bkb_[:, 0:n], func=AF.Sigmoid), [rbkb], r_g3)
            bka, rbka = proj_fm(panA, rpanA, 4, 128 * m, yT3, r_yT)
            S.op("dve", lambda h, g3=g3, bka=bka: h.tensor_tensor(out=g3[:, 0:n], in0=bka[:, 0:n], in1=g3[:, 0:n], op=ALU.mult), [rbka] + r_g3, r_g3)
            S.op("pool", lambda h, g3=g3, m=m: h.tensor_tensor(out=G13[:, m, 0:n], in0=g3[:, 0:n], in1=G13[:, m, 0:n], op=ALU.mult), r_g3 + [r_G1[m]], [r_G1[m]])
        pan, rpan = self.panel(4, 1024)
        for m in range(8):
            bk, rbk = proj_fm(pan, rpan, 4, 128 * m, aT3, r_aT)
            S.op("dve", lambda h, m=m, bk=bk: h.tensor_tensor(out=G23[:, m, 0:n], in0=bk[:, 0:n], in1=G23[:, m, 0:n], op=ALU.mult), [rbk, r_G2[m]], [r_G2[m]])
            S.op("pool", lambda h, m=m: h.tensor_tensor(out=G13[:, m, 0:n], in0=G13[:, m, 0:n], in1=G23[:, m, 0:n], op=ALU.add), [r_G1[m], r_G2[m]], [r_G1[m]])
        mix3, r_mix = G13, r_G1

        if tap:
            self.out_res.append(Res("dbg"))
            S.dma("sp", "dbg", lambda h: h.dma_start(out=self.dbg_m, in_=G1), reads=r_G1, writes=[self.out_res[-1]])
        chk("s5")
        pw0, rpw0 = self.panel(8, 512)
        pw1, rpw1 = self.panel(8, 512)
        for tt in range(ntt):
            tn = min(128, n - tt * 128)
            for nh, (pw, rpw) in enumerate(((pw0, rpw0), (pw1, rpw1))):
                bk, rbk = proj_tm(pw, rpw, 8, mix3, r_mix, tt, tn)
                S.op("dve", lambda h, tt=tt, tn=tn, nh=nh, bk=bk: h.tensor_tensor(out=xh[0:tn, tt, 512 * nh:512 * nh + 512], in0=bk[0:tn, :],
                                                                               in1=xh[0:tn, tt, 512 * nh:512 * nh + 512], op=ALU.add),
                     [rbk, r_xh[tt]], [r_xh[tt]])
            if tt >= 1:
                self.norm_T(n, self.gffn_fm, self.xnT, self.r_xnT, xh, r_xh, tts=[tt - 1])
        self.norm_T(n, self.gffn_fm, self.xnT, self.r_xnT, xh, r_xh, tts=[ntt - 1])
        hnT = self.xnT

        chk("s6")
        gT, r_gT = self.av(0, NF * 1024, BF16)
        gT3 = gT.rearrange("p (f t) -> p f t", f=NF)
        asb = [self.av(22 + 3 * i, (n + 2) * 4, F32) for i in range(2)]
        csb = [self.av(28 + 2 * i, 2048, F32) for i in range(2)]
        geb = [self.av(32 + 2 * i, 2048, F32) for i in range(2)]
        for pp in range(11):
            pan, rpan = self.panel(8, 512)
            for s_ in range(2):
                f = 2 * pp + s_
                bka, rbka = proj_fm(pan, rpan, 8, 128 * s_, hnT, rxnT)
                bkb_, rbkb = proj_fm(pan, rpan, 8, 256 + 128 * s_, hnT, rxnT)
                a_, r_a = asb[f % 2]
                c_, r_c = csb[f % 2]
                g_, r_g = geb[f % 2]
                S.op("pool", lambda h, a_=a_, f=f: h.tensor_copy(out=a_[:, 0:2], in_=self.hist[:, f, :]), [self.r_hist], r_a)
                S.op("act", lambda h, a_=a_, bka=bka: h.activation(out=a_[:, 2:2 + n], in_=bka[:, 0:n], func=AF.Copy), [rbka], r_a)
                S.op("pool", lambda h, a_=a_, f=f: h.tensor_copy(out=self.hist[:, f, :], in_=a_[:, nr:nr + 2]), r_a, [self.r_hist])
                S.op("act", lambda h, c_=c_, bka=bka, f=f: h.activation(out=c_[:, 0:n], in_=bka[:, 0:n], func=AF.Identity, scale=self.taps[:, 2, f:f + 1],
                                                                       bias=self.cbias[:, f:f + 1]), [rbka, self.r_const], r_c)
                S.op("dve", lambda h, c_=c_, a_=a_, f=f: h.scalar_tensor_tensor(out=c_[:, 0:n], in0=a_[:, 1:1 + n], scalar=self.taps[:, 1, f:f + 1], in1=c_[:, 0:n],
                                                                               op0=ALU.mult, op1=ALU.add), r_a + r_c + [self.r_const], r_c)
                S.op("dve", lambda h, c_=c_, a_=a_, f=f: h.scalar_tensor_tensor(out=c_[:, 0:n], in0=a_[:, 0:n], scalar=self.taps[:, 0, f:f + 1], in1=c_[:, 0:n],
                                                                               op0=ALU.mult, op1=ALU.add), r_a + r_c + [self.r_const], r_c)
                S.op("act", lambda h, c_=c_, g_=g_: h.activation(out=g_[:, 0:n], in_=c_[:, 0:n], func=AF.Gelu_apprx_tanh), r_c, r_g)
                S.op("dve", lambda h, g_=g_, bkb_=bkb_, f=f: h.tensor_tensor(out=gT3[:, f, 0:n], in0=bkb_[:, 0:n], in1=g_[:, 0:n], op=ALU.mult), [rbkb] + r_g, [r_gT[f]])
        if is_last:
            bk, rbk = self.getbank("mm")
            S.op("pe", lambda h, bk=bk: h.transpose(out=bk[0:44, 0:128], in_=self.hist[:, :, :].rearrange("p f j -> p (f j)"), identity=self.identf[:]),
                 [self.r_hist, self.r_const], [rbk])
            co, r_co = self.av(40, 512, F32)
            evac_copy(co[0:44, 0:128], bk[0:44, 0:128], [rbk], r_co)
            for j in range(2):
                self.out_res.append(Res("oc"))
                S.dma("sp", "so", lambda h, j=j: h.dma_start(out=seq["o_c"][j, :].rearrange("(f p) -> f p", p=128), in_=co[j:44:2, 0:128]),
                      reads=r_co, writes=[self.out_res[-1]])

        chk("s7")
        if pre8 is not None:
            pre8[0]()
        ffv = [self.av(36 + 2 * i, 2048, F32) for i in range(2)]
        def down_tail(m, ff, r_ff):
            bk2, rbk2 = self.getbank("acc")
            for tt in range(ntt):
                tn = min(128, n - tt * 128)
                S.op("pe", lambda h, tt=tt, tn=tn: h.transpose(out=bk2[0:tn, 128 * tt:128 * tt + 128], in_=ff[:, 128 * tt:128 * tt + tn], identity=self.identf[:]),
                     r_ff + [self.r_const], [rbk2])
            for tt in range(ntt):
                tn = min(128, n - tt * 128)
                S.op("dve", lambda h, tt=tt, tn=tn: h.tensor_tensor(out=xh[0:tn, tt, 128 * m:128 * m + 128], in0=bk2[0:tn, 128 * tt:128 * tt + 128],
                                                                  in1=xh[0:tn, tt, 128 * m:128 * m + 128], op=ALU.add),
                     [rbk2, r_xh[tt]], [r_xh[tt]])

        prev_m = None
        for m in range(8):
            pan, rpan = self.panel(NF, 128)
            bk, rbk = self.getbank("mm")
            for f in range(NF):
                S.op("pe", lambda h, f=f, bk=bk, pan=pan: h.matmul(bk[:, 0:n], lhsT=pan[:, f, :], rhs=gT3[:, f, 0:n], start=(f == 0), stop=(f == NF - 1)),
                     [rpan, r_gT[f]], [rbk])
            ff, r_ff = ffv[m % 2]
            evac_copy(ff[:, 0:n], bk[:, 0:n], [rbk], r_ff)
            if prev_m is not None:
                down_tail(*prev_m)
            prev_m = (m, ff, r_ff)
            if pre8 is not None and m % 2 == 1:
                pre8[1](m // 2)
        down_tail(*prev_m)
        ost, r_ost = self.av(40, 4096, F32)
        for tt in range(ntt):
            tn = min(128, n - tt * 128)
            xin = xh[0:tn, tt, :]
            ssv = self.ss[0:tn, 4 + tt:5 + tt]
            S.op("act", lambda h, xin=xin, ssv=ssv, tn=tn: h.activation(out=self.junk[0:tn, :], in_=xin, func=AF.Square, accum_out=ssv), [r_xh[tt]], [self.r_junk, self.r_ss])
            S.op("act", lambda h, ssv=ssv: h.activation(out=ssv, in_=ssv, func=AF.Sqrt, scale=1.0 / D, bias=EPS), [self.r_ss], [self.r_ss])
            S.op("dve", lambda h, ssv=ssv: h.reciprocal(out=ssv, in_=ssv), [self.r_ss], [self.r_ss])
            S.op("dve", lambda h, xin=xin, ssv=ssv, tn=tn: h.scalar_tensor_tensor(out=ost[0:tn, :], in0=xin, scalar=ssv, in1=self.gfin_row[0:tn, :], op0=ALU.mult, op1=ALU.mult),
                 [r_xh[tt], self.r_ss, self.r_const], r_ost)
            tnr = min(tn, nr - tt * 128)
            dst = seq["o_y"][T0 + 128 * tt:T0 + 128 * tt + tnr, :]
            self.out_res.append(Res("oy"))
            S.dma("sp", "ost", lambda h, dst=dst, tnr=tnr: h.dma_start(out=dst, in_=ost[0:tnr, :]), reads=r_ost, writes=[self.out_res[-1]])

    def seq_reset(self, seq):
        S = self.S
        if seq["kind"] == "prompt":
            S.op("pool", lambda h: h.memset(self.car[:], 0.0), [self.r_car], [self.r_car])
            S.op("pool", lambda h: h.memset(self.hist[:], 0.0), [self.r_hist], [self.r_hist])
        else:
            for ri, src in enumerate((self.s0re, self.s0im)):
                S.dma("sp", "sq", lambda h, ri=ri, src=src: h.dma_start(out=self.car[:, ri, :], in_=src.rearrange("(q g) p -> (g p) q", g=2), allow_slow_non_contiguous=True),
                      reads=[self.r_car], writes=[self.r_car])
            for j in range(2):
                S.dma("sp", "sq", lambda h, j=j: h.dma_start(out=self.hist[:, :, j], in_=self.cc[j, :].rearrange("(f p) -> p f", p=128), allow_slow_non_contiguous=True),
                      reads=[self.r_hist], writes=[self.r_hist])
            for t in range(4):
                S.dma("pool", "cv%d" % t, lambda h, t=t: h.dma_start(out=self.vr[:, 0, t, :], in_=self.cv[128 * t:128 * t + 128, :]), writes=[self.r_vr[0][t]])
            kst, r_kst = self.av(40, 1024, BF16)
            for t in range(4):
                S.dma("pool", "ckk", lambda h, t=t: h.dma_start(out=kst, in_=self.ck[128 * t:128 * t + 128, :]), reads=r_kst, writes=r_kst)
                bk, rbk = self.getbank("mm")
                bkb = bk[:].bitcast(BF16).rearrange("p (k c) -> p k c", k=8)
                for hp in range(4):
                    S.op("pe", lambda h, hp=hp, bkb=bkb: h.transpose(out=bkb[:, hp, :], in_=kst[:, 128 * hp:128 * hp + 128], identity=self.identb[:]), r_kst + [self.r_const], [rbk])
                S.op("dve", lambda h, t=t, bkb=bkb: h.tensor_copy(out=self.kT[:, :, 0, 128 * t:128 * t + 128], in_=bkb[:, 0:4, :]), [rbk], [self.r_kT[0]])

    def build(self):
        seqs = []
        for i in range(2):
            seqs.append(dict(kind="prompt", first=(i == 0), n=TB, nblocks=4, x=self.xp[i], o_y=self.o_yp[i], o_k=self.o_kp[i], o_v=self.o_vp[i],
                             o_sre=self.o_srep[i], o_sim=self.o_simp[i], o_c=self.o_cp[i],
                             slot=lambda b: b % 2, has_prev=lambda b: b > 0, is_last=lambda b: b == 3))
        seqs.append(dict(kind="sample", n=128, nreal=16, nblocks=1, x=self.xs, o_y=self.o_ys, o_k=self.o_ks, o_v=self.o_vs,
                         o_sre=self.o_sres, o_sim=self.o_sims, o_c=self.o_cs,
                         slot=lambda b: 1, has_prev=lambda b: True, is_last=lambda b: True))
        self.build_panels(sum(s["nblocks"] for s in seqs))
        try:
            self.setup_consts()
            chk("c")
            self.s5_prep()
            chk("p")
            allb = [(si, seq, b) for si, seq in enumerate(seqs) for b in range(seq["nblocks"])]
            self.stage1(allb[0][1], allb[0][2], 0)
            for gi, (si, seq, b) in enumerate(allb):
                CURSEQ[0] = si
                if b == 0:
                    self.seq_reset(seq)
                    chk("r")
                nxt = allb[gi + 1] if gi + 1 < len(allb) else None
                pre8 = ((lambda nxt=nxt, gi=gi: self.stage1(nxt[1], nxt[2], gi + 1, load_only=True)),
                        (lambda tt, nxt=nxt, gi=gi: self.stage1_norm(nxt[1], gi + 1, tt))) if nxt is not None else None
                self.block(seq, b, gi, pre8)
                chk("b%d_%d" % (si, b))
        except StopBuild:
            pass
        self.S.emit(final_res=self.out_res)
        return self.nc


_CACHE = {}


def kernel(**inp):
    f = lambda a: np.ascontiguousarray(np.asarray(a, dtype=np.float32))
    if "nc" not in _CACHE:
        _CACHE["nc"] = Builder().build()
    nc = _CACHE["nc"]
    shared = {
        "g_mix": f(inp["g_mix"][0]), "w_in": f(inp["w_in"][0]),
        "lam_re": f(inp["ssm_lambda_re"][0]), "lam_im": f(inp["ssm_lambda_im"][0]), "log_dt": f(inp["ssm_log_dt"][0]),
        "b_re": f(inp["ssm_b_re"][0]), "b_im": f(inp["ssm_b_im"][0]),
        "c_re": f(inp["ssm_c_re"][0]).reshape(512, 64), "c_im": f(inp["ssm_c_im"][0]).reshape(512, 64),
        "ssm_d": f(inp["ssm_d"][0]).reshape(512), "w_glu": f(inp["w_ssm_glu"][0]), "relb": f(inp["attn_rel_bias"][0]),
        "w_au": f(inp["w_attn_up"][0]), "w_o": f(inp["w_o"][0]), "g_ffn": f(inp["g_ffn"][0]), "w_up": f(inp["w_up"][0]),
        "conv_w": f(inp["conv_w"][0]), "conv_b": f(inp["conv_b"][0]), "w_down": f(inp["w_down"][0]), "g_fin": f(inp["g_final"]),
    }
    xp = f(inp["x_prompt"])
    xs = f(inp["x_sample"])
    in_maps = []
    for c in range(NCORES):
        m = dict(shared)
        m["xp"] = xp[2 * c:2 * c + 2]
        m["xs"] = xs[c]
        m["s0re"] = f(inp["state_ssm_re"][0, c])
        m["s0im"] = f(inp["state_ssm_im"][0, c])
        m["ck"] = f(inp["cache_attn_k"][0, c]).reshape(512, 512)
        m["cv"] = f(inp["cache_attn_v"][0, c]).reshape(512, 512)
        m["cc"] = f(inp["cache_conv"][0, c])
        in_maps.append(m)
    res = run_bass_kernel_spmd(nc, in_maps, core_ids=list(range(NCORES)))
    R = res.results
    cat = lambda k: np.concatenate([np.asarray(R[c][k]) for c in range(NCORES)], axis=0)
    stk = lambda k: np.stack([np.asarray(R[c][k]) for c in range(NCORES)], axis=0)
    y_prompt = cat("o_yp").astype(np.float32)
    y_sample = stk("o_ys").astype(np.float32)
    outs = (
        y_prompt, y_sample,
        cat("o_srep")[None].astype(np.float32), cat("o_simp")[None].astype(np.float32),
        cat("o_kp").reshape(1, 16, 512, 8, 64).astype(np.float32), cat("o_vp").reshape(1, 16, 512, 8, 64).astype(np.float32),
        cat("o_cp")[None].astype(np.float32),
        stk("o_sres")[None].astype(np.float32), stk("o_sims")[None].astype(np.float32),
        stk("o_ks").reshape(1, 8, 16, 8, 64).astype(np.float32), stk("o_vs").reshape(1, 8, 16, 8, 64).astype(np.float32),
        stk("o_cs")[None].astype(np.float32),
    )
    return outs
```

```python
import math
import numpy as np
from contextlib import ExitStack
import concourse.bass as bass
import concourse.mybir as mybir
from concourse.bass_utils import run_bass_kernel_spmd

F32 = mybir.dt.float32
BF16 = mybir.dt.bfloat16
I32 = mybir.dt.int32
AF = mybir.ActivationFunctionType
ALU = mybir.AluOpType

NCORES = 8
D = 1024
DFF = 2816
NF = 22
TB = 512
EPS = 1e-6
NSLOT = 4
DEBUG_TAPS = False
NEGB = -240000.0
LIMIT = None


class StopBuild(Exception):
    pass


CURSEQ = [0]


def chk(tag):
    if LIMIT is not None and (tag == LIMIT or ("%d:%s" % (CURSEQ[0], tag)) == LIMIT):
        raise StopBuild(tag)


class Res:
    __slots__ = ("name", "w", "r")

    def __init__(self, name):
        self.name = name
        self.w = None
        self.r = []


class Sched:
    ENG = ("pe", "act", "dve", "pool", "sp")

    def __init__(self, nc, es):
        self.nc = nc
        self.es = es
        self.q = {e: [] for e in self.ENG}
        self.cnt = {e: 0 for e in self.ENG}
        self.sem = {e: es.enter_context(nc.semaphore("sem_" + e)) for e in ("pe", "act", "dve", "pool")}
        self.waited = {e: {} for e in self.ENG}
        self.dsem = {}

    def _tokens(self, reads, writes):
        toks = []
        for r in reads:
            if r.w is not None:
                toks.append(r.w)
        for r in writes:
            if r.w is not None:
                toks.append(r.w)
            toks.extend(r.r)
        return toks

    def _waits(self, eng, toks):
        need = {}
        for (key, sem, val, teng) in toks:
            if teng == eng and eng == "pe":
                continue
            if self.waited[eng].get(key, -1) >= val:
                continue
            if need.get(key, (None, -1))[1] < val:
                need[key] = (sem, val)
        for key, (sem, val) in need.items():
            self.waited[eng][key] = val
        return list(need.values())

    def _post(self, tok, reads, writes):
        ws = set(id(r) for r in writes)
        for r in writes:
            r.w = tok
            r.r = []
        for r in reads:
            if id(r) not in ws:
                r.r.append(tok)
                if len(r.r) > 24:
                    r.r = r.r[-24:] if False else r.r
        return tok

    def op(self, eng, fn, reads=(), writes=()):
        waits = self._waits(eng, self._tokens(reads, writes))
        self.cnt[eng] += 1
        tok = (eng, self.sem[eng], self.cnt[eng], eng)
        self.q[eng].append((waits, fn, self.sem[eng], 1))
        return self._post(tok, reads, writes)

    def dma(self, eng, dkey, fn, reads=(), writes=()):
        if dkey not in self.dsem:
            self.dsem[dkey] = [self.es.enter_context(self.nc.semaphore("d_" + dkey)), 0]
        ent = self.dsem[dkey]
        toks = self._tokens(reads, writes)
        if ent[1] > 0:
            toks.append(("d_" + dkey, ent[0], ent[1], "dma"))
        waits = self._waits(eng, toks)
        ent[1] += 16
        tok = ("d_" + dkey, ent[0], ent[1], "dma")
        self.q[eng].append((waits, fn, ent[0], 16))
        return self._post(tok, reads, writes)

    def emit(self, final_res=()):
        toks = []
        for r in final_res:
            if r.w is not None:
                toks.append(r.w)
            toks.extend(r.r)
        self.q["sp"].append((self._waits("sp", toks), None, None, 0))
        hmap = {"pe": "tensor", "act": "scalar", "dve": "vector", "pool": "gpsimd", "sp": "sync"}
        with self.nc.Block() as block:
            for e in self.ENG:
                items = self.q[e]

                def body(h, items=items):
                    for (waits, fn, sem, inc) in items:
                        for (ws, wv) in waits:
                            h.wait_ge(ws, wv)
                        if fn is not None:
                            fn(h).then_inc(sem, inc)
                getattr(block, hmap[e])(body)


def rev2(ap):
    (ps, pn), (st, n) = ap.ap[0], ap.ap[1]
    return bass.AP(tensor=ap.tensor, offset=ap.offset + (n - 1) * st, ap=[[ps, pn], [-st, n]])


def bcast(ap, shape):
    return ap.broadcast_to(list(shape))


class Builder:
    def __init__(self):
        self.nc = nc = bass.Bass("TRN2", target_bir_lowering=False)
        self.es = es = ExitStack()
        self.S = Sched(nc, es)
        self.out_res = []
        self._decl_io()
        self._alloc()

    def din(self, name, shape):
        return self.nc.dram_tensor(name, list(shape), F32, kind="ExternalInput").ap()

    def dout(self, name, shape):
        return self.nc.dram_tensor(name, list(shape), F32, kind="ExternalOutput").ap()

    def _decl_io(self):
        d = self.din
        self.xp = d("xp", [2, 2048, D])
        self.xs = d("xs", [16, D])
        self.s0re = d("s0re", [32, 64])
        self.s0im = d("s0im", [32, 64])
        self.ck = d("ck", [512, 512])
        self.cv = d("cv", [512, 512])
        self.cc = d("cc", [2, DFF])
        self.g_mix = d("g_mix", [D])
        self.w_in = d("w_in", [D, 4096])
        self.lam_re = d("lam_re", [32, 64])
        self.lam_im = d("lam_im", [32, 64])
        self.log_dt = d("log_dt", [32])
        self.b_re = d("b_re", [32, 64, 16])
        self.b_im = d("b_im", [32, 64, 16])
        self.c_re = d("c_re", [512, 64])
        self.c_im = d("c_im", [512, 64])
        self.ssm_d = d("ssm_d", [512])
        self.w_glu = d("w_glu", [512, 2048])
        self.relb = d("relb", [8, 257])
        self.w_au = d("w_au", [512, D])
        self.w_o = d("w_o", [D, D])
        self.g_ffn = d("g_ffn", [D])
        self.w_up = d("w_up", [D, 2 * DFF])
        self.conv_w = d("conv_w", [3, DFF])
        self.conv_b = d("conv_b", [DFF])
        self.w_down = d("w_down", [DFF, D])
        self.g_fin = d("g_fin", [D])
        o = self.dout
        self.o_yp = o("o_yp", [2, 2048, D])
        self.o_ys = o("o_ys", [16, D])
        self.o_srep = o("o_srep", [2, 32, 64])
        self.o_simp = o("o_simp", [2, 32, 64])
        self.o_kp = o("o_kp", [2, 512, 512])
        self.o_vp = o("o_vp", [2, 512, 512])
        self.o_cp = o("o_cp", [2, 2, DFF])
        self.o_sres = o("o_sres", [32, 64])
        self.o_sims = o("o_sims", [32, 64])
        self.o_ks = o("o_ks", [16, 512])
        self.o_vs = o("o_vs", [16, 512])
        self.o_cs = o("o_cs", [2, DFF])
        self.extb = self.nc.dram_tensor("extb", [8, 768], F32, kind="Internal").ap()
        self.wsc = self.nc.dram_tensor("wsc", [32, 128, 4096], BF16, kind="Internal").ap()
        self.r_wsc = [Res("wsc%d" % i) for i in range(32)]
        if DEBUG_TAPS:
            mk = lambda name, shape, dt: self.nc.dram_tensor(name, list(shape), dt, kind="ExternalOutput").ap()
            self.dbg_y = mk("dbg_y", [128, 2048], BF16)
            self.dbg_a = mk("dbg_a", [128, 2048], BF16)
            self.dbg_u = mk("dbg_u", [128, 2048], BF16)
            self.dbg_m = mk("dbg_m", [128, 4096], BF16)
            self.dbg_h = mk("dbg_h", [128, 4096], F32)

    def sb(self, name, shape, dt):
        return self.es.enter_context(self.nc.sbuf_tensor(name, list(shape), dt))

    def _alloc(self):
        sb = self.sb
        R = Res
        self.bank = [self.es.enter_context(self.nc.psum_tensor("bk%d" % i, [128, 512], F32)) for i in range(8)]
        self.rbank = [R("bk%d" % i) for i in range(8)]
        self.pool_idx = {"mm": 0, "acc": 0}
        self.pools = {"mm": [0, 1, 2, 3], "acc": [4, 5, 6, 7]}
        self.ring = [sb("ring%d" % i, [128, 4096], BF16) for i in range(NSLOT)]
        self.rring = [R("ring%d" % i) for i in range(NSLOT)]
        self.xhs = [sb("xh_%d" % j, [128, 4, D], F32) for j in range(2)]
        self.r_xhs = [[R("xh%d_%d" % (j, i)) for i in range(4)] for j in range(2)]
        self.xnb = sb("xnb", [128, D], BF16)
        self.r_xnb = R("xnb")
        self.junk = self.xnb
        self.r_junk = self.r_xnb
        self.ss = sb("ss", [128, 8], F32)
        self.r_ss = R("ss")
        self.xnT = sb("xnT", [128, 8, TB], BF16)
        self.r_xnT = [R("xnT%d" % i) for i in range(4)]
        self.kT = sb("kTr", [128, 4, 2, TB], BF16)
        self.r_kT = [R("kT0"), R("kT1")]
        self.vr = sb("vr", [128, 2, 4, 512], BF16)
        self.r_vr = [[R("vr%d_%d" % (s, t)) for t in range(4)] for s in range(2)]
        self.ones = sb("ones", [128, 128], BF16)
        self.zerob = sb("zerob", [128, 128], BF16)
        self.kmask = sb("kmask", [128, 1], F32)
        self.padrow = sb("padrow", [32, 128], BF16)
        self.r_const = R("const")
        self.EM = sb("EM", [128, 8, 640], BF16)
        self.r_EM = R("EM")
        self.identf = sb("identf", [128, 128], F32)
        self.identb = sb("identb", [128, 128], BF16)
        self.gmix_fm = sb("gmix_fm", [128, 8], F32)
        self.gffn_fm = sb("gffn_fm", [128, 8], F32)
        self.gfin_row = sb("gfin_row", [128, D], F32)
        self.taps = sb("taps", [128, 3, NF], F32)
        self.cbias = sb("cbias", [128, NF], F32)
        self.hist = sb("hist", [128, NF, 2], F32)
        self.r_hist = R("hist")
        self.W1 = sb("W1", [128, 4, 8, 2, 128], BF16)
        self.TK = sb("TK", [128, 4, 8, 128], BF16)
        self.W3 = sb("W3", [128, 16, 9, 2, 32], BF16)
        self.r_s5w = R("s5w")
        self.Ere = sb("Ere", [128, 16, 64], F32)
        self.Eim = sb("Eim", [128, 16, 64], F32)
        self.rho8 = sb("rho8", [128, 16], F32)
        self.car = sb("car", [128, 2, 16], F32)
        self.r_car = R("car")
        self.ml = sb("ml", [128, 40, 16], F32)
        self.r_ml = R("ml")
        self.mli = sb("mli", [128, 16], I32)
        self.Dch = sb("Dch", [128, 4], F32)
        self.RAK = 46
        self.RA = sb("RA", [128, self.RAK * 256], F32)
        self.r_RA = [R("RA%d" % i) for i in range(self.RAK)]

    def av(self, off_kb, nbytes, dt):
        ob = int(round(off_kb * 1024))
        e0 = ob // 4
        e1 = (ob + nbytes + 3) // 4
        ap = self.RA[:, e0:e1]
        if dt == BF16:
            ap = ap.bitcast(BF16)
        res = self.r_RA[ob // 1024:(ob + nbytes - 1) // 1024 + 1]
        return ap, res

    def getbank(self, pool):
        lst = self.pools[pool]
        i = lst[self.pool_idx[pool] % len(lst)]
        self.pool_idx[pool] += 1
        return self.bank[i], self.rbank[i]

    def build_panels(self, nblocks):
        w_in, w_glu, w_au, w_o, w_up, w_down = self.w_in, self.w_glu, self.w_au, self.w_o, self.w_up, self.w_down
        per = []
        for x in range(8):
            per.append([((8, 512, 0, 512), w_in[:, 512 * x:512 * x + 512].rearrange("(k p) n -> p k n", p=128))])
        per.append([((4, 1024, 0, 1024), w_glu[:, 1024:2048].rearrange("(k p) n -> p k n", p=128))])
        per.append([((4, 1024, 0, 1024), w_glu[:, 0:1024].rearrange("(k p) n -> p k n", p=128))])
        per.append([((4, 1024, 0, 1024), w_au.rearrange("(k p) n -> p k n", p=128))])
        for nh in range(2):
            per.append([((8, 512, 0, 512), w_o[:, 512 * nh:512 * nh + 512].rearrange("(k p) n -> p k n", p=128))])
        for pp in range(11):
            per.append([((8, 512, 0, 256), w_up[:, 256 * pp:256 * pp + 256].rearrange("(k p) n -> p k n", p=128)),
                        ((8, 512, 256, 512), w_up[:, DFF + 256 * pp:DFF + 256 * pp + 256].rearrange("(k p) n -> p k n", p=128))])
        for m in range(8):
            per.append([((22, 128, 0, 128), w_down[:, 128 * m:128 * m + 128].rearrange("(f p) n -> p f n", p=128))])
        assert len(per) == 32
        self.panels = per * nblocks
        self.issued = 0
        self.pcur = 0

    def _issue_panel(self, idx):
        s = idx % NSLOT
        p = idx % 32
        if idx >= 32:
            self.S.dma("sp", "ringH%d" % s, lambda h, s=s, p=p: h.dma_start(out=self.ring[s][:], in_=self.wsc[p]),
                       reads=[self.r_wsc[p]], writes=[self.rring[s]])
            return
        for (a, b, c0, c1), src in self.panels[idx]:
            dst = self.ring[s][:, 0:a * b].rearrange("p (a b) -> p a b", a=a)[:, :, c0:c1]
            self.S.dma("pool", "ring%d" % s, lambda h, dst=dst, src=src: h.dma_start(out=dst, in_=src),
                       writes=[self.rring[s]])
        self.S.dma("sp", "wst", lambda h, s=s, p=p: h.dma_start(out=self.wsc[p], in_=self.ring[s][:]),
                   reads=[self.rring[s]], writes=[self.r_wsc[p]])

    def panel(self, a, b):
        idx = self.pcur
        self.pcur += 1
        while self.issued < min(len(self.panels), idx + NSLOT - 1):
            self._issue_panel(self.issued)
            self.issued += 1
        s = idx % NSLOT
        return self.ring[s][:, 0:a * b].rearrange("p (a b) -> p a b", a=a), self.rring[s]

    def setup_consts(self):
        S = self.S
        rc = self.r_const
        iot = self.av(38, 512, F32)[0].bitcast(I32)
        S.op("pool", lambda h: h.iota(iot[:], pattern=[[1, 128]], base=0, channel_multiplier=-1), writes=[rc])
        S.op("dve", lambda h: h.tensor_scalar(out=self.identf[:], in0=iot[:], scalar1=0.0, scalar2=None, op0=ALU.is_equal), reads=[rc], writes=[rc])
        S.op("dve", lambda h: h.tensor_copy(out=self.identb[:], in_=self.identf[:]), reads=[rc], writes=[rc])
        S.op("pool", lambda h: h.memset(self.ones[:], 1.0), writes=[rc])
        S.op("pool", lambda h: h.memset(self.zerob[:], 0.0), writes=[rc])
        S.op("dve", lambda h: h.tensor_reduce(out=self.kmask[:], in_=self.identf[:, 0:16], axis=mybir.AxisListType.X, op=ALU.add), reads=[rc], writes=[rc])
        S.op("pool", lambda h: h.memset(self.hist[:], 0.0), writes=[self.r_hist])
        dm = lambda key, out, in_: S.dma("sp", key, lambda h: h.dma_start(out=out, in_=in_, allow_slow_non_contiguous=True), writes=[rc])
        dm("c0", self.gmix_fm[:], self.g_mix.rearrange("(k p) -> p k", p=128))
        dm("c0", self.gffn_fm[:], self.g_ffn.rearrange("(k p) -> p k", p=128))
        gf = self.g_fin
        dm("c0", self.gfin_row[:], bass.AP(tensor=gf.tensor, offset=gf.offset, ap=[[0, 128], [1, D]]))
        dm("c0", self.taps[:], self.conv_w.rearrange("j (f p) -> p j f", p=128))
        dm("c0", self.cbias[:], self.conv_b.rearrange("(f p) -> p f", p=128))
        dm("c0", self.Dch[:], self.ssm_d.rearrange("(o p) -> p o", p=128))
        ext = self.av(34, 3072, F32)[0][0:8, :]
        tbl = self.av(37, 1028, F32)[0][0:8, :]
        S.dma("sp", "c1", lambda h: h.dma_start(out=tbl[:], in_=self.relb), writes=[rc])
        S.op("dve", lambda h: h.tensor_copy(out=ext[:, 0:511], in_=bcast(tbl[:, 256:257], [8, 511])), reads=[rc], writes=[rc])
        S.op("dve", lambda h: h.tensor_copy(out=ext[:, 511:768], in_=rev2(tbl[:, 0:257])), reads=[rc], writes=[rc])
        r_ext = Res("extb")
        S.dma("sp", "c2", lambda h: h.dma_start(out=self.extb, in_=ext[:]), reads=[rc], writes=[r_ext])
        bmv, bmr = self.av(0, 640 * 4, F32)
        for hh in range(8):
            src = bass.AP(tensor=self.extb.tensor, offset=self.extb.offset + hh * 768, ap=[[1, 128], [1, 640]])
            S.dma("sp", "c3", lambda h, src=src: h.dma_start(out=bmv, in_=src), reads=[r_ext], writes=bmr)
            S.op("act", lambda h, hh=hh: h.activation(out=self.EM[:, hh, :], in_=rev2(bmv), func=AF.Copy, scale=8.0), reads=bmr, writes=[self.r_EM])
        S.op("pool", lambda h: h.memset(self.EM[0:64, :, 576:640], NEGB), writes=[self.r_EM])
        S.op("pool", lambda h: h.memset(self.EM[64:128, :, 0:64], NEGB), writes=[self.r_EM])
        S.op("pool", lambda h: h.memset(self.padrow[:], 0.0), writes=[rc])
        S.op("pool", lambda h: h.memset(self.padrow[0:1, 16:128], NEGB), reads=[rc], writes=[rc])

    def s5_prep(self):
        S = self.S
        ml, rml = self.ml, self.r_ml
        R1 = [rml]

        def dve(fn, reads=R1, writes=R1):
            return S.op("dve", fn, reads, writes)

        def act(fn, reads=R1, writes=R1):
            return S.op("act", fn, reads, writes)

        def tt(out, a, b, op):
            dve(lambda h: h.tensor_tensor(out=out, in0=a, in1=b, op=op))

        def ts(out, a, s1, op0, s2=None, op1=None):
            if op1 is None:
                dve(lambda h: h.tensor_scalar(out=out, in0=a, scalar1=s1, scalar2=None, op0=op0))
            else:
                dve(lambda h: h.tensor_scalar(out=out, in0=a, scalar1=s1, scalar2=s2, op0=op0, op1=op1))

        V = lambda i: ml[:, i, :]
        LRE, LIM, DT, T0, MAG, X, KF, RR, M1, SIN, COS, LBR, LBI, NR, DEN, KRE, KIM, T1, T2, NPI = range(20)
        P0 = 20
        dm = lambda out, in_: S.dma("sp", "p0", lambda h: h.dma_start(out=out, in_=in_, allow_slow_non_contiguous=True), writes=R1)
        dm(V(LRE), self.lam_re.rearrange("(q g) p -> (g p) q", g=2))
        dm(V(LIM), self.lam_im.rearrange("(q g) p -> (g p) q", g=2))
        ld = self.log_dt
        for gl in range(2):
            dm(ml[64 * gl:64 * gl + 64, DT, :], bass.AP(tensor=ld.tensor, offset=ld.offset + gl, ap=[[0, 64], [2, 16]]))
        act(lambda h: h.activation(out=V(DT), in_=V(DT), func=AF.Exp))
        tt(V(T0), V(LRE), V(DT), ALU.mult)
        act(lambda h: h.activation(out=V(MAG), in_=V(T0), func=AF.Exp))
        act(lambda h: h.activation(out=self.rho8[:], in_=V(T0), func=AF.Exp, scale=8.0))
        act(lambda h: h.activation(out=V(38), in_=V(T0), func=AF.Exp, scale=-8.0))
        tt(V(X), V(LIM), V(DT), ALU.mult)
        ts(V(X), V(X), 1.0 / (2 * math.pi), ALU.mult)

        def sin_turns(dst, src_off):
            ts(V(RR), V(X), src_off, ALU.add)
            dve(lambda h: h.tensor_copy(out=self.mli[:], in_=V(RR)))
            dve(lambda h: h.tensor_copy(out=V(KF), in_=self.mli[:]))
            tt(V(RR), V(RR), V(KF), ALU.subtract)
            ts(V(M1), V(RR), 0.5, ALU.is_gt)
            tt(V(RR), V(RR), V(M1), ALU.subtract)
            ts(V(M1), V(RR), -0.5, ALU.is_lt)
            tt(V(RR), V(RR), V(M1), ALU.add)
            act(lambda h: h.activation(out=V(dst), in_=V(RR), func=AF.Sin, scale=2 * math.pi))

        sin_turns(SIN, 0.0)
        sin_turns(COS, 0.25)
        tt(V(LBR), V(MAG), V(COS), ALU.mult)
        tt(V(LBI), V(MAG), V(SIN), ALU.mult)
        ts(V(NR), V(LBR), -1.0, ALU.add)
        tt(V(T1), V(LRE), V(LRE), ALU.mult)
        tt(V(T2), V(LIM), V(LIM), ALU.mult)
        tt(V(DEN), V(T1), V(T2), ALU.add)
        dve(lambda h: h.reciprocal(out=V(DEN), in_=V(DEN)))
        tt(V(T1), V(NR), V(LRE), ALU.mult)
        tt(V(T2), V(LBI), V(LIM), ALU.mult)
        tt(V(T1), V(T1), V(T2), ALU.add)
        tt(V(KRE), V(T1), V(DEN), ALU.mult)
        tt(V(T1), V(LBI), V(LRE), ALU.mult)
        tt(V(T2), V(NR), V(LIM), ALU.mult)
        tt(V(T1), V(T1), V(T2), ALU.subtract)
        tt(V(KIM), V(T1), V(DEN), ALU.mult)
        PR = lambda n: ml[:, P0 + n, :]
        PI_ = lambda n: ml[:, P0 + 9 + n, :]
        dve(lambda h: h.memset(PR(0), 1.0))
        dve(lambda h: h.memset(PI_(0), 0.0))
        for n in range(1, 9):
            tt(V(T1), PR(n - 1), V(LBR), ALU.mult)
            tt(V(T2), PI_(n - 1), V(LBI), ALU.mult)
            tt(PR(n), V(T1), V(T2), ALU.subtract)
            tt(V(T1), PR(n - 1), V(LBI), ALU.mult)
            tt(V(T2), PI_(n - 1), V(LBR), ALU.mult)
            tt(PI_(n), V(T1), V(T2), ALU.add)
        Ere, Eim = self.Ere, self.Eim
        tt(Ere[:, :, 0], PR(8), V(38), ALU.mult)
        tt(Eim[:, :, 0], PI_(8), V(38), ALU.mult)
        tmpA, rA = self.av(40, 16 * 32 * 4, F32)
        tmpB, rB = self.av(42, 16 * 32 * 4, F32)
        RE = R1 + rA + rB
        for k in range(6):
            m = 1 << k
            last_re = bcast(Ere[:, :, m - 1:m], [128, 16, m])
            last_im = bcast(Eim[:, :, m - 1:m], [128, 16, m])
            a3 = tmpA.rearrange("p (q c) -> p q c", q=16)[:, :, 0:m]
            b3 = tmpB.rearrange("p (q c) -> p q c", q=16)[:, :, 0:m]
            src_re, src_im = Ere[:, :, 0:m], Eim[:, :, 0:m]
            dst_re, dst_im = Ere[:, :, m:2 * m], Eim[:, :, m:2 * m]
            S.op("dve", lambda h, a3=a3, s=src_re, l=last_re: h.tensor_tensor(out=a3, in0=s, in1=l, op=ALU.mult), RE, RE)
            S.op("dve", lambda h, b3=b3, s=src_im, l=last_im: h.tensor_tensor(out=b3, in0=s, in1=l, op=ALU.mult), RE, RE)
            S.op("dve", lambda h, a3=a3, b3=b3, d=dst_re: h.tensor_tensor(out=d, in0=a3, in1=b3, op=ALU.subtract), RE, RE)
            S.op("dve", lambda h, a3=a3, s=src_re, l=last_im: h.tensor_tensor(out=a3, in0=s, in1=l, op=ALU.mult), RE, RE)
            S.op("dve", lambda h, b3=b3, s=src_im, l=last_re: h.tensor_tensor(out=b3, in0=s, in1=l, op=ALU.mult), RE, RE)
            S.op("dve", lambda h, a3=a3, b3=b3, d=dst_im: h.tensor_tensor(out=d, in0=a3, in1=b3, op=ALU.add), RE, RE)
        ts(V(NPI), PI_(0), -1.0, ALU.mult)

        bre, r_bre = self.av(0, 1024, F32)
        bim, r_bim = self.av(1, 1024, F32)
        bbr, r_bbr = self.av(2, 1024, F32)
        bbi, r_bbi = self.av(3, 1024, F32)
        t1v, r_t1 = self.av(4, 1024, F32)
        t2v, r_t2 = self.av(5, 1024, F32)
        v3 = lambda ap: ap.rearrange("p (q h) -> p q h", q=16)
        for gl in range(2):
            for (dst, src, rr) in ((bre, self.b_re, r_bre), (bim, self.b_im, r_bim)):
                s_ap = bass.AP(tensor=src.tensor, offset=src.offset + gl * 1024, ap=[[16, 64], [2048, 16], [1, 16]])
                S.dma("sp", "p1", lambda h, d=v3(dst)[64 * gl:64 * gl + 64], s=s_ap: h.dma_start(out=d, in_=s), writes=rr)
        RB = R1 + r_bre + r_bim + r_bbr + r_bbi + r_t1 + r_t2
        kre_b = bcast(ml[:, KRE, :].unsqueeze(2), [128, 16, 16])
        kim_b = bcast(ml[:, KIM, :].unsqueeze(2), [128, 16, 16])
        db = lambda fn: S.op("dve", fn, RB, RB)
        db(lambda h: h.tensor_tensor(out=v3(t1v), in0=v3(bre), in1=kre_b, op=ALU.mult))
        db(lambda h: h.tensor_tensor(out=v3(t2v), in0=v3(bim), in1=kim_b, op=ALU.mult))
        db(lambda h: h.tensor_tensor(out=bbr, in0=t1v, in1=t2v, op=ALU.subtract))
        db(lambda h: h.tensor_tensor(out=v3(t1v), in0=v3(bim), in1=kre_b, op=ALU.mult))
        db(lambda h: h.tensor_tensor(out=v3(t2v), in0=v3(bre), in1=kim_b, op=ALU.mult))
        db(lambda h: h.tensor_tensor(out=bbi, in0=t1v, in1=t2v, op=ALU.add))
        lbr, r_lbr = self.av(6, 8192, F32)
        lbi, r_lbi = self.av(14, 8192, F32)
        u1, r_u1 = self.av(22, 8192, F32)
        u2, r_u2 = self.av(30, 8192, F32)
        v4 = lambda ap: ap.rearrange("p (j q h) -> p j q h", j=8, q=16)
        RL = RB + r_lbr + r_lbi + r_u1 + r_u2
        mlap = self.ml[:, P0 + 7, :]
        prr = bass.AP(tensor=self.ml, offset=mlap.offset, ap=[list(mlap.ap[0]), [-16, 8], [1, 16], [0, 16]])
        mlap2 = self.ml[:, P0 + 9 + 7, :]
        pir = bass.AP(tensor=self.ml, offset=mlap2.offset, ap=[list(mlap2.ap[0]), [-16, 8], [1, 16], [0, 16]])
        bbr4 = bcast(v3(bbr).unsqueeze(1), [128, 8, 16, 16])
        bbi4 = bcast(v3(bbi).unsqueeze(1), [128, 8, 16, 16])
        dl = lambda fn: S.op("dve", fn, RL, RL)
        dl(lambda h: h.tensor_tensor(out=v4(u1), in0=bbr4, in1=prr, op=ALU.mult))
        dl(lambda h: h.tensor_tensor(out=v4(u2), in0=bbi4, in1=pir, op=ALU.mult))
        dl(lambda h: h.tensor_tensor(out=lbr, in0=u1, in1=u2, op=ALU.subtract))
        dl(lambda h: h.tensor_tensor(out=v4(u1), in0=bbr4, in1=pir, op=ALU.mult))
        dl(lambda h: h.tensor_tensor(out=v4(u2), in0=bbi4, in1=prr, op=ALU.mult))
        dl(lambda h: h.tensor_tensor(out=lbi, in0=u1, in1=u2, op=ALU.add))
        tp = [self.av(22 + 2 * ri, 2048, BF16) for ri in range(2)]
        RT = RL + tp[0][1] + tp[1][1]
        lbs = (lbr, lbi)
        for r in range(4):
            for ri in range(2):
                S.op("pool", lambda h, t=tp[ri][0]: h.memset(t, 0.0), RT, RT)
            for o in range(4):
                q = 4 * o + r
                for ri in range(2):
                    t3 = tp[ri][0].rearrange("p (j c) -> p j c", j=8)
                    for gl in range(2):
                        src = v4(lbs[ri])[64 * gl:64 * gl + 64, :, q, :]
                        dst = t3[64 * gl:64 * gl + 64, :, 32 * r + 16 * gl:32 * r + 16 * gl + 16]
                        S.op("dve" if gl == 0 else "act",
                             (lambda h, d=dst, s=src: h.tensor_copy(out=d, in_=s)) if gl == 0 else
                             (lambda h, d=dst, s=src: h.activation(out=d, in_=s, func=AF.Copy)), RT, RT)
                    bk, rbk = self.getbank("mm")
                    bkb = bk[:].bitcast(BF16).rearrange("p (j c) -> p j c", j=8)
                    for j in range(8):
                        S.op("pe", lambda h, j=j, bkb=bkb, t3=t3: h.transpose(out=bkb[:, j, :], in_=t3[:, j, :], identity=self.identb[:]),
                             RT + [self.r_const], [rbk])
                    S.op("act" if ri == 0 else "dve",
                         (lambda h, o=o, r=r, ri=ri, bkb=bkb: h.activation(out=self.W1[32 * r:32 * r + 32, o, :, ri, :], in_=bkb[32 * r:32 * r + 32, :, :], func=AF.Copy)) if ri == 0 else
                         (lambda h, o=o, r=r, ri=ri, bkb=bkb: h.tensor_copy(out=self.W1[32 * r:32 * r + 32, o, :, ri, :], in_=bkb[32 * r:32 * r + 32, :, :])),
                         [rbk], [self.r_s5w])
        cml = [self.av(26 + ri, 1024, F32) for ri in range(2)]
        zst, r_z = self.av(28, 128 * 4, F32)
        RC = R1 + cml[0][1] + cml[1][1] + r_z
        for ri, csrc in enumerate((self.c_re, self.c_im)):
            for t in range(4):
                s_ap = bass.AP(tensor=csrc.tensor, offset=csrc.offset + t * 128 * 64, ap=[[64, 128], [0, 2], [1, 64]])
                S.dma("sp", "p2", lambda h, s=s_ap: h.dma_start(out=zst.rearrange("p (a b) -> p a b", a=2), in_=s), reads=r_z, writes=r_z)
                bk, rbk = self.getbank("mm")
                S.op("pe", lambda h, bk=bk: h.transpose(out=bk[:, 0:128], in_=zst, identity=self.identf[:]), r_z + [self.r_const], [rbk])
                c3 = cml[ri][0].rearrange("p (q h) -> p q h", q=16)
                bk3 = bk[:, 0:128].rearrange("p (q g h) -> p q g h", q=4, g=2)
                for gl in range(2):
                    S.op("dve", lambda h, gl=gl, t=t, c3=c3, bk3=bk3: h.tensor_copy(out=c3[64 * gl:64 * gl + 64, 4 * t:4 * t + 4, :], in_=bk3[64 * gl:64 * gl + 64, :, gl, :]),
                         [rbk], cml[ri][1])
        w1v, r_w1 = self.av(6, 9 * 1024, F32)
        w2v, r_w2 = self.av(15, 9 * 1024, F32)
        v9 = lambda ap: ap.rearrange("p (d q h) -> p d q h", d=9, q=16)
        RW = RC + r_w1 + r_w2 + [self.r_s5w]
        S.op("pool", lambda h: h.memset(self.W3[:], 0.0), [self.r_s5w], [self.r_s5w])
        cre4 = bcast(cml[0][0].rearrange("p (q h) -> p q h", q=16).unsqueeze(1), [128, 9, 16, 16])
        cim4 = bcast(cml[1][0].rearrange("p (q h) -> p q h", q=16).unsqueeze(1), [128, 9, 16, 16])
        pr9 = bcast(self.ml[:, P0:P0 + 9, :].unsqueeze(3), [128, 9, 16, 16])
        pi9 = bcast(self.ml[:, P0 + 9:P0 + 18, :].unsqueeze(3), [128, 9, 16, 16])
        dw = lambda fn: S.op("dve", fn, RW, RW)
        W3v = self.W3
        dw(lambda h: h.tensor_tensor(out=v9(w1v), in0=cre4, in1=pr9, op=ALU.mult))
        dw(lambda h: h.tensor_tensor(out=v9(w2v), in0=cim4, in1=pi9, op=ALU.mult))
        for gl in range(2):
            dw(lambda h, gl=gl: h.tensor_tensor(out=W3v[64 * gl:64 * gl + 64, :, :, 0, 16 * gl:16 * gl + 16].rearrange("p q d h -> p d q h"),
                                                in0=v9(w1v)[64 * gl:64 * gl + 64], in1=v9(w2v)[64 * gl:64 * gl + 64], op=ALU.subtract))
        dw(lambda h: h.tensor_tensor(out=v9(w1v), in0=cre4, in1=pi9, op=ALU.mult))
        dw(lambda h: h.tensor_tensor(out=v9(w2v), in0=cim4, in1=pr9, op=ALU.mult))
        dw(lambda h: h.tensor_tensor(out=w1v, in0=w1v, in1=w2v, op=ALU.add))
        for gl in range(2):
            dw(lambda h, gl=gl: h.tensor_scalar(out=W3v[64 * gl:64 * gl + 64, :, :, 1, 16 * gl:16 * gl + 16].rearrange("p q d h -> p d q h"),
                                                in0=v9(w1v)[64 * gl:64 * gl + 64], scalar1=-1.0, scalar2=None, op0=ALU.mult))
        S.op("pool", lambda h: h.memset(self.TK[:], 0.0), [self.r_s5w], [self.r_s5w])
        tb = [self.av(30 + ri, 256, BF16) for ri in range(2)]
        RTB = RB + tb[0][1] + tb[1][1]
        bbs = (bbr, bbi)
        for r in range(4):
            for ri in range(2):
                S.op("pool", lambda h, t=tb[ri][0]: h.memset(t, 0.0), RTB, RTB)
            for o in range(4):
                q = 4 * o + r
                for ri in range(2):
                    for gl in range(2):
                        src = v3(bbs[ri])[64 * gl:64 * gl + 64, q, :]
                        dst = tb[ri][0][64 * gl:64 * gl + 64, 32 * r + 16 * gl:32 * r + 16 * gl + 16]
                        S.op("dve", lambda h, d=dst, s=src: h.tensor_copy(out=d, in_=s), RTB, RTB)
                bk, rbk = self.getbank("mm")
                for d_ in range(8):
                    for ri in range(2):
                        S.op("pe", lambda h, ri=ri, d_=d_, q=q, bk=bk: h.matmul(bk[:, 32 * d_:32 * d_ + 32], lhsT=tb[ri][0], rhs=W3v[:, q, d_, ri, :],
                                                                               start=(ri == 0), stop=(ri == 1)), RTB + [self.r_s5w], [rbk])
                S.op("act", lambda h, o=o, r=r, bk=bk: h.activation(out=self.TK[32 * r:32 * r + 32, o, :, 32 * r:32 * r + 32],
                                                                    in_=bk[32 * r:32 * r + 32, 0:256].rearrange("p (d c) -> p d c", d=8), func=AF.Copy),
                     [rbk], [self.r_s5w])
        for o in range(4):
            S.op("dve", lambda h, o=o: h.scalar_tensor_tensor(out=self.TK[:, o, 0, :], in0=self.identb[:], scalar=self.Dch[:, o:o + 1], in1=self.TK[:, o, 0, :],
                                                              op0=ALU.mult, op1=ALU.add), [self.r_s5w, self.r_const], [self.r_s5w])

    def norm_T(self, n, gfm, outT, r_outT, xh, r_xh, tts=None):
        S = self.S
        ntt = (n + 127) // 128
        for tt in (range(ntt) if tts is None else tts):
            tn = min(128, n - tt * 128)
            xin = xh[0:tn, tt, :]
            ssv = self.ss[0:tn, tt:tt + 1]
            S.op("act", lambda h, xin=xin, ssv=ssv, tn=tn: h.activation(out=self.junk[0:tn, :], in_=xin, func=AF.Square, accum_out=ssv),
                 [r_xh[tt]], [self.r_junk, self.r_ss])
            S.op("act", lambda h, ssv=ssv: h.activation(out=ssv, in_=ssv, func=AF.Sqrt, scale=1.0 / D, bias=EPS), [self.r_ss], [self.r_ss])
            S.op("dve", lambda h, ssv=ssv: h.reciprocal(out=ssv, in_=ssv), [self.r_ss], [self.r_ss])
            S.op("act", lambda h, xin=xin, ssv=ssv, tn=tn: h.activation(out=self.xnb[0:tn, :], in_=xin, func=AF.Copy, scale=ssv),
                 [r_xh[tt], self.r_ss], [self.r_xnb])
            bk, rbk = self.getbank("mm")
            bkb = bk[:].bitcast(BF16).rearrange("p (k c) -> p k c", k=8)
            for kt in range(8):
                S.op("pe", lambda h, kt=kt, tn=tn, bkb=bkb: h.transpose(out=bkb[:, kt, 0:tn], in_=self.xnb[0:tn, 128 * kt:128 * kt + 128], identity=self.identb[0:tn, 0:tn]),
                     [self.r_xnb, self.r_const], [rbk])
            S.op("dve", lambda h, tt=tt, tn=tn, bkb=bkb: h.tensor_tensor(out=outT[:, :, 128 * tt:128 * tt + tn], in0=bkb[:, :, 0:tn],
                                                                      in1=bcast(gfm[:, :].unsqueeze(2), [128, 8, tn]), op=ALU.mult),
                 [rbk, self.r_const], [r_outT[tt]])

    def stage1(self, seq, b, gi, load_only=False):
        S = self.S
        n = seq["n"]
        nr = seq.get("nreal", n)
        ntt = (n + 127) // 128
        T0 = b * TB
        x_ap = seq["x"]
        xh, r_xh = self.xhs[gi % 2], self.r_xhs[gi % 2]
        for tt in range(ntt):
            tn = min(128, n - tt * 128)
            tnr = min(tn, nr - tt * 128)
            if tnr < tn:
                S.op("pool", lambda h, tt=tt, tn=tn: h.memset(xh[0:tn, tt, :], 0.0), [], [r_xh[tt]])
            S.dma("sp", "xh%d_%d" % (gi % 2, tt), lambda h, tt=tt, tnr=tnr: h.dma_start(out=xh[0:tnr, tt, :], in_=x_ap[T0 + 128 * tt:T0 + 128 * tt + tnr, :]), writes=[r_xh[tt]])
        if not load_only:
            self.norm_T(n, self.gmix_fm, self.xnT, self.r_xnT, xh, r_xh)

    def stage1_norm(self, seq, gi, tt):
        n = seq["n"]
        if tt < (n + 127) // 128:
            self.norm_T(n, self.gmix_fm, self.xnT, self.r_xnT, self.xhs[gi % 2], self.r_xhs[gi % 2], tts=[tt])

    def block(self, seq, b, gi, pre8=None):
        S = self.S
        n = seq["n"]
        nr = seq.get("nreal", n)
        ntt = (n + 127) // 128
        T0 = b * TB
        cur = seq["slot"](b)
        prev = 1 - cur
        has_prev = seq["has_prev"](b)
        is_last = seq["is_last"](b)
        rxnT = self.r_xnT[:ntt]
        xh, r_xh = self.xhs[gi % 2], self.r_xhs[gi % 2]
        chk("s1")
        uT, r_uT = self.av(0, 4096, BF16)
        qT, r_qT = self.av(4, 4096, BF16)
        yT, r_yT = self.av(8, 4096, BF16)
        aT, r_aT = self.av(12, 4096, BF16)
        uT3, qT3, yT3, aT3 = [v.rearrange("p (f t) -> p f t", f=4) for v in (uT, qT, yT, aT)]
        evac_i = [0]

        def evac_copy(out, in_, reads, writes, eng=None):
            evac_i[0] += 1
            if eng == "act" or (eng is None and evac_i[0] % 2):
                S.op("act", lambda h: h.activation(out=out, in_=in_, func=AF.Copy), reads, writes)
            else:
                S.op("dve", lambda h: h.tensor_copy(out=out, in_=in_), reads, writes)

        def proj_fm(pan, rpan, nk, col0, rhs3, r_rhs, width=128):
            bk, rbk = self.getbank("mm")
            for kt in range(nk):
                S.op("pe", lambda h, kt=kt, bk=bk: h.matmul(bk[0:width, 0:n], lhsT=pan[:, kt, col0:col0 + width], rhs=rhs3[:, kt, 0:n],
                                                            start=(kt == 0), stop=(kt == nk - 1)), [rpan] + list(r_rhs), [rbk])
            return bk, rbk

        def proj_tm(pan, rpan, nk, lhs3, r_lhs, tt, tn, c0=0, c1=512):
            bk, rbk = self.getbank("mm")
            for kt in range(nk):
                S.op("pe", lambda h, kt=kt, bk=bk: h.matmul(bk[0:tn, 0:c1 - c0], lhsT=lhs3[:, kt, 128 * tt:128 * tt + tn], rhs=pan[:, kt, c0:c1],
                                                            start=(kt == 0), stop=(kt == nk - 1)), [rpan] + list(r_lhs), [rbk])
            return bk, rbk

        stg, r_stg = self.av(44, 2048, F32)
        pan, rpan = self.panel(8, 512)
        for ft in range(4):
            bk, rbk = proj_fm(pan, rpan, 8, 128 * ft, self.xnT, rxnT)
            evac_copy(uT3[:, ft, 0:n], bk[:, 0:n], [rbk], [r_uT[ft]], eng="act")
        chk("s2u")
        pan, rpan = self.panel(8, 512)
        for ft in range(4):
            bk, rbk = proj_fm(pan, rpan, 8, 128 * ft, self.xnT, rxnT)
            evac_copy(qT3[:, ft, 0:n], bk[:, 0:n], [rbk], [r_qT[ft]], eng="act")
        pan, rpan = self.panel(8, 512)
        for ft in range(4):
            bk, rbk = proj_fm(pan, rpan, 8, 128 * ft, self.xnT, rxnT)
            evac_copy(self.kT[:, ft, cur, 0:n], bk[:, 0:n], [rbk], [self.r_kT[cur]], eng="act")
        chk("s2k")
        if is_last:
            for tt in range(ntt):
                tn = min(128, n - tt * 128)
                bk, rbk = proj_tm(pan, rpan, 8, self.xnT, rxnT, tt, tn)
                evac_copy(stg[0:tn, :], bk[0:tn, :], [rbk], r_stg)
                tnr = min(tn, nr - tt * 128)
                dst = seq["o_k"][128 * tt:128 * tt + tnr, :]
                self.out_res.append(Res("ok"))
                S.dma("sp", "stg", lambda h, dst=dst, tnr=tnr: h.dma_start(out=dst, in_=stg[0:tnr, :]), reads=r_stg, writes=[self.out_res[-1]])
        chk("s2kt")
        pan, rpan = self.panel(8, 512)
        for tt in range(ntt):
            tn = min(128, n - tt * 128)
            bk, rbk = proj_tm(pan, rpan, 8, self.xnT, rxnT, tt, tn)
            evac_copy(self.vr[0:tn, cur, tt, :], bk[0:tn, :], [rbk], [self.r_vr[cur][tt]], eng="act")
            if is_last:
                evac_copy(stg[0:tn, :], bk[0:tn, :], [rbk], r_stg)
                tnr = min(tn, nr - tt * 128)
                dst = seq["o_v"][128 * tt:128 * tt + tnr, :]
                self.out_res.append(Res("ov"))
                S.dma("sp", "stg", lambda h, dst=dst, tnr=tnr: h.dma_start(out=dst, in_=stg[0:tnr, :]), reads=r_stg, writes=[self.out_res[-1]])

        chk("s2")
        NCb = n // 8
        Lv, r_L = self.av(16, 8192, F32)
        Tv, r_T = self.av(24, 8192, F32)
        Spv, r_Sp = self.av(32, 4096, BF16)
        L4 = Lv.rearrange("p (r q c) -> p r q c", r=2, q=16)
        T4 = Tv.rearrange("p (r q c) -> p r q c", r=2, q=16)
        Sp4 = Spv.rearrange("p (r q c) -> p r q c", r=2, q=16)
        sl = [self.getbank("mm") for _ in range(4)]
        sl4 = [bk_[:].rearrange("p (a r c) -> p a r c", a=4, r=2) for (bk_, _) in sl]
        for o in range(4):
            for ri in range(2):
                for j in range(8):
                    for r in range(4):
                        S.op("pe", lambda h, o=o, r=r, ri=ri, j=j: h.matmul(
                            sl4[r][:, o, ri, 0:NCb], lhsT=self.W1[32 * r:32 * r + 32, o, j, ri, :], rhs=uT3[32 * r:32 * r + 32, o, j:n:8],
                            start=(j == 0), stop=(j == 7), tile_position=(32 * r, 0)), [self.r_s5w, r_uT[o]], [sl[r][1]])
        for r in range(4):
            evac_copy(L4[:, :, r:16:4, 0:NCb].rearrange("p r a c -> p a r c"), sl4[r][:, :, :, 0:NCb], [sl[r][1]], r_L, eng="act")
        chk("s3a")
        Er = self.Ere[:, :, 0:NCb]
        Ei = self.Eim[:, :, 0:NCb]
        RS = r_L + r_T + [self.r_s5w]
        Lr, Li = L4[:, 0, :, 0:NCb], L4[:, 1, :, 0:NCb]
        Tr, Ti = T4[:, 0, :, 0:NCb], T4[:, 1, :, 0:NCb]
        chain = []
        dv = lambda fn: chain.append((fn, RS, RS))
        dv(lambda h: h.tensor_tensor(out=Tr, in0=Er, in1=Lr, op=ALU.mult))
        dv(lambda h: h.tensor_tensor(out=Ti, in0=Ei, in1=Li, op=ALU.mult))
        dv(lambda h: h.tensor_tensor(out=Tr, in0=Tr, in1=Ti, op=ALU.add))
        dv(lambda h: h.tensor_tensor(out=Ti, in0=Er, in1=Li, op=ALU.mult))
        dv(lambda h: h.tensor_tensor(out=Lr, in0=Ei, in1=Lr, op=ALU.mult))
        dv(lambda h: h.tensor_tensor(out=Ti, in0=Ti, in1=Lr, op=ALU.subtract))
        for ri in range(2):
            for q in range(16):
                chain.append((lambda h, ri=ri, q=q: h.tensor_tensor_scan(out=L4[:, ri, q, 0:NCb], data0=bcast(self.rho8[:, q:q + 1], [128, NCb]),
                                                                       data1=T4[:, ri, q, 0:NCb], initial=self.car[:, ri, q:q + 1],
                                                                       op0=ALU.mult, op1=ALU.add), RS + [self.r_car], RS))
        chk("s3b")
        dv(lambda h: h.tensor_tensor(out=Tr, in0=Er, in1=Lr, op=ALU.mult))
        dv(lambda h: h.tensor_tensor(out=Ti, in0=Ei, in1=Li, op=ALU.mult))
        dv(lambda h: h.tensor_tensor(out=Tr, in0=Tr, in1=Ti, op=ALU.subtract))
        dv(lambda h: h.tensor_tensor(out=Ti, in0=Er, in1=Li, op=ALU.mult))
        dv(lambda h: h.tensor_tensor(out=Lr, in0=Ei, in1=Lr, op=ALU.mult))
        dv(lambda h: h.tensor_tensor(out=Ti, in0=Ti, in1=Lr, op=ALU.add))
        def s5_y():
            RSP = RS + r_Sp + [self.r_car]
            S.op("act", lambda h: h.activation(out=Sp4[:, :, :, 0:1], in_=self.car[:, :, :].unsqueeze(3), func=AF.Copy), RSP, RSP)
            if NCb > 1:
                S.op("act", lambda h: h.activation(out=Sp4[:, :, :, 1:NCb], in_=T4[:, :, :, 0:NCb - 1], func=AF.Copy), RSP, RSP)
            NCr = nr // 8
            S.op("dve", lambda h: h.tensor_copy(out=self.car[:, :, :].unsqueeze(3), in_=T4[:, :, :, NCr - 1:NCr]), RSP, RSP)
            chk("s3c")
            for o in range(4):
                bk, rbk = self.getbank("mm")
                Y3 = bk[:, 0:8 * NCb].rearrange("p (i c) -> p i c", i=8)
                uj = uT3[:, o, 0:n].rearrange("p (c j) -> p j c", j=8)

                def toep_lag(d, first, last, o=o, Y3=Y3, uj=uj):
                    S.op("pe", lambda h: h.matmul(Y3[:, d:8, 0:NCb], lhsT=self.TK[:, o, d, :], rhs=uj[:, 0:8 - d, :],
                                                  start=first, stop=last), [self.r_s5w, r_uT[o]], [rbk])
                toep_lag(0, True, False)
                for i in range(8):
                    for ri in range(2):
                        for r in range(4):
                            q = 4 * o + r
                            S.op("pe", lambda h, r=r, q=q, ri=ri, i=i, Y3=Y3: h.matmul(Y3[32 * r:32 * r + 32, i, 0:NCb], lhsT=self.W3[:, q, i + 1, ri, :],
                                                                                       rhs=Sp4[:, ri, q, 0:NCb], start=False, stop=False, tile_position=(0, 32 * r)),
                                 [self.r_s5w] + r_Sp, [rbk])
                for d in range(1, 8):
                    toep_lag(d, False, d == 7)
                S.op("act", lambda h, o=o, Y3=Y3: h.activation(out=yT3[:, o, 0:n].rearrange("p (c i) -> p i c", i=8), in_=Y3[:, :, 0:NCb], func=AF.Gelu_apprx_tanh),
                     [rbk], [r_yT[o]])
            if is_last:
                bk, rbk = self.getbank("mm")
                S.op("pe", lambda h, bk=bk: h.transpose(out=bk[0:32, 0:128], in_=self.car[:, :, :].rearrange("p r q -> p (r q)"), identity=self.identf[:]),
                     [self.r_car, self.r_const], [rbk])
                so, r_so = self.av(40, 512, F32)
                evac_copy(so[0:32, 0:128], bk[0:32, 0:128], [rbk], r_so)
                for ri, dst in enumerate((seq["o_sre"], seq["o_sim"])):
                    self.out_res.append(Res("os"))
                    S.dma("sp", "so", lambda h, ri=ri, dst=dst: h.dma_start(out=dst.rearrange("(q g) p -> q g p", g=2),
                                                                            in_=so[16 * ri:16 * ri + 16, 0:128].rearrange("q (g p) -> q g p", g=2)),
                          reads=r_so, writes=[self.out_res[-1]])

        chk("s3")
        tap = DEBUG_TAPS and seq.get("first") and b == 0
        if tap:
            self.out_res.append(Res("dbg"))
            S.dma("sp", "dbg", lambda h: h.dma_start(out=self.dbg_y, in_=yT), reads=r_yT, writes=[self.out_res[-1]])
            self.out_res.append(Res("dbg"))
            S.dma("sp", "dbg", lambda h: h.dma_start(out=self.dbg_u, in_=uT), reads=r_uT, writes=[self.out_res[-1]])
        exv = [self.av(o_, 1024, BF16) for o_ in (36, 37, 42, 43)]
        pmv = [self.av(o_, 1024, BF16) for o_ in (38, 39, 44, 45)]
        recv, r_rec = self.av(40, 2048, F32)
        tiles = [(4 + t, cur, t, min(128, n - 128 * t)) for t in range(ntt)]
        padded_keys = nr < n
        if has_prev:
            tiles += [(t, prev, t, 128) for t in range(4)]
        it = 0
        heads = {}

        def hctx(hh):
            if hh not in heads:
                ob, r_ob = self.getbank("acc")
                sbk, r_sb = self.getbank("acc")
                heads[hh] = (hh // 2, 64 * (hh % 2), ob, r_ob, sbk, r_sb)
            return heads[hh]

        def front(hh, ti):
            nonlocal it
            hp, po, ob, r_ob, sbk, r_sb = hctx(hh)
            (arel, slot, t, kc) = tiles[ti]
            cq_min = max(8, 2 * arel)
            cq_max = min(15, 2 * arel + 9)
            c0 = 64 * (cq_min - 8)
            c1 = min(n, 64 * (cq_max - 7))
            N = c1 - c0
            qq0 = 512 + c0 - 128 * arel
            stb, r_st = self.getbank("mm")
            pad = padded_keys and slot == cur
            S.op("pe", lambda h: h.matmul(stb[0:kc, 0:N], lhsT=self.kT[po:po + 64, hp, slot, 128 * t:128 * t + kc], rhs=qT3[po:po + 64, hp, c0:c1], start=True, stop=False),
                 [self.r_kT[slot], r_qT[hp]], [r_st])
            S.op("pe", lambda h: h.matmul(stb[0:kc, 0:N], lhsT=self.identb[0:kc, 0:kc], rhs=self.EM[0:kc, hh, qq0:qq0 + N], start=False, stop=not pad),
                 [self.r_EM, self.r_const], [r_st])
            if pad:
                S.op("pe", lambda h: h.matmul(stb[0:kc, 0:N], lhsT=self.padrow[0:32, 0:kc], rhs=self.ones[0:32, 0:N], start=False, stop=True),
                     [self.r_const], [r_st])
            pm, r_pm = pmv[it % 4]
            it += 1
            S.op("act", lambda h: h.activation(out=pm[0:kc, 0:N], in_=stb[0:kc, 0:N], func=AF.Exp, scale=0.125), [r_st], r_pm)
            return (pm, r_pm, c0, c1, N, kc, slot, t)

        def back(hh, ti, st):
            hp, po, ob, r_ob, sbk, r_sb = hctx(hh)
            (pm, r_pm, c0, c1, N, kc, slot, t) = st
            first = (ti == 0)
            last = (ti == len(tiles) - 1)
            S.op("pe", lambda h: h.matmul(ob[:, c0:c1], lhsT=self.vr[0:kc, slot, t, 128 * hp:128 * hp + 128], rhs=pm[0:kc, 0:N], start=first, stop=last),
                 r_pm + [self.r_vr[slot][t]], [r_ob])
            S.op("pe", lambda h: h.matmul(sbk[:, c0:c1], lhsT=self.ones[0:kc, :], rhs=pm[0:kc, 0:N], start=first, stop=last),
                 r_pm + [self.r_const], [r_sb])

        def finish(hh):
            hp, po, ob, r_ob, sbk, r_sb = hctx(hh)
            S.op("act", lambda h: h.activation(out=recv[po:po + 64, 0:n], in_=sbk[po:po + 64, 0:n], func=AF.Ln), [r_sb], r_rec)
            S.op("act", lambda h: h.activation(out=recv[po:po + 64, 0:n], in_=recv[po:po + 64, 0:n], func=AF.Exp, scale=-1.0), r_rec, r_rec)
            S.op("dve", lambda h: h.tensor_tensor(out=aT3[po:po + 64, hp, 0:n], in0=ob[po:po + 64, 0:n], in1=recv[po:po + 64, 0:n], op=ALU.mult),
                 [r_ob] + r_rec, [r_aT[hp]])
            for _ in range((44 + 7) // 8):
                if chain:
                    fn_, rd_, wr_ = chain.pop(0)
                    S.op("dve", fn_, rd_, wr_)

        items = [(hh, ti) for hh in range(8) for ti in range(len(tiles))]
        SKEW = 2
        pend = [front(*items[k_]) for k_ in range(min(SKEW, len(items)))]
        for k_, (hh, ti) in enumerate(items):
            if k_ + SKEW < len(items):
                pend.append(front(*items[k_ + SKEW]))
            back(hh, ti, pend.pop(0))
            if ti == len(tiles) - 1:
                finish(hh)

        if tap:
            self.out_res.append(Res("dbg"))
            S.dma("sp", "dbg", lambda h: h.dma_start(out=self.dbg_a, in_=aT), reads=r_aT, writes=[self.out_res[-1]])
        while chain:
            fn_, rd_, wr_ = chain.pop(0)
            S.op("dve", fn_, rd_, wr_)
        s5_y()
        chk("s4")
        G1, r_G1 = self.av(16, 8192, BF16)
        G2, r_G2 = self.av(24, 8192, BF16)
        G13 = G1.rearrange("p (m t) -> p m t", m=8)
        G23 = G2.rearrange("p (m t) -> p m t", m=8)
        g3v = [self.av(32 + i, 1024, BF16) for i in range(2)]
        for (G3_, rG) in ((G13, r_G1), (G23, r_G2)):
            for half in range(2):
                pan, rpan = self.panel(8, 512)
                for c in range(4):
                    m = 4 * half + c
                    bk, rbk = proj_fm(pan, rpan, 8, 128 * c, self.xnT, rxnT)
                    S.op("act", lambda h, G3_=G3_, m=m, bk=bk: h.activation(out=G3_[:, m, 0:n], in_=bk[:, 0:n], func=AF.Sigmoid), [rbk], [rG[m]])
        panB, rpanB = self.panel(4, 1024)
        panA, rpanA = self.panel(4, 1024)
        for m in range(8):
            bkb_, rbkb = proj_fm(panB, rpanB, 4, 128 * m, yT3, r_yT)
            g3, r_g3 = g3v[m % 2]
            S.op("act", lambda h, g3=g3, bkb_=bkb_: h.activation(out=g3[:, 0:n], in_=bkb_[:, 0:n], func=AF.Sigmoid), [rbkb], r_g3)
            bka, rbka = proj_fm(panA, rpanA, 4, 128 * m, yT3, r_yT)
            S.op("dve", lambda h, g3=g3, bka=bka: h.tensor_tensor(out=g3[:, 0:n], in0=bka[:, 0:n], in1=g3[:, 0:n], op=ALU.mult), [rbka] + r_g3, r_g3)
            S.op("pool", lambda h, g3=g3, m=m: h.tensor_tensor(out=G13[:, m, 0:n], in0=g3[:, 0:n], in1=G13[:, m, 0:n], op=ALU.mult), r_g3 + [r_G1[m]], [r_G1[m]])
        pan, rpan = self.panel(4, 1024)
        for m in range(8):
            bk, rbk = proj_fm(pan, rpan, 4, 128 * m, aT3, r_aT)
            S.op("dve", lambda h, m=m, bk=bk: h.tensor_tensor(out=G23[:, m, 0:n], in0=bk[:, 0:n], in1=G23[:, m, 0:n], op=ALU.mult), [rbk, r_G2[m]], [r_G2[m]])
            S.op("pool", lambda h, m=m: h.tensor_tensor(out=G13[:, m, 0:n], in0=G13[:, m, 0:n], in1=G23[:, m, 0:n], op=ALU.add), [r_G1[m], r_G2[m]], [r_G1[m]])
        mix3, r_mix = G13, r_G1

        if tap:
            self.out_res.append(Res("dbg"))
            S.dma("sp", "dbg", lambda h: h.dma_start(out=self.dbg_m, in_=G1), reads=r_G1, writes=[self.out_res[-1]])
        chk("s5")
        pw0, rpw0 = self.panel(8, 512)
        pw1, rpw1 = self.panel(8, 512)
        for tt in range(ntt):
            tn = min(128, n - tt * 128)
            for nh, (pw, rpw) in enumerate(((pw0, rpw0), (pw1, rpw1))):
                bk, rbk = proj_tm(pw, rpw, 8, mix3, r_mix, tt, tn)
                S.op("dve", lambda h, tt=tt, tn=tn, nh=nh, bk=bk: h.tensor_tensor(out=xh[0:tn, tt, 512 * nh:512 * nh + 512], in0=bk[0:tn, :],
                                                                               in1=xh[0:tn, tt, 512 * nh:512 * nh + 512], op=ALU.add),
                     [rbk, r_xh[tt]], [r_xh[tt]])
            if tt >= 1:
                self.norm_T(n, self.gffn_fm, self.xnT, self.r_xnT, xh, r_xh, tts=[tt - 1])
        self.norm_T(n, self.gffn_fm, self.xnT, self.r_xnT, xh, r_xh, tts=[ntt - 1])
        hnT = self.xnT

        chk("s6")
        gT, r_gT = self.av(0, NF * 1024, BF16)
        gT3 = gT.rearrange("p (f t) -> p f t", f=NF)
        asb = [self.av(22 + 3 * i, (n + 2) * 4, F32) for i in range(2)]
        csb = [self.av(28 + 2 * i, 2048, F32) for i in range(2)]
        geb = [self.av(32 + 2 * i, 2048, F32) for i in range(2)]
        for pp in range(11):
            pan, rpan = self.panel(8, 512)
            for s_ in range(2):
                f = 2 * pp + s_
                bka, rbka = proj_fm(pan, rpan, 8, 128 * s_, hnT, rxnT)
                bkb_, rbkb = proj_fm(pan, rpan, 8, 256 + 128 * s_, hnT, rxnT)
                a_, r_a = asb[f % 2]
                c_, r_c = csb[f % 2]
                g_, r_g = geb[f % 2]
                S.op("pool", lambda h, a_=a_, f=f: h.tensor_copy(out=a_[:, 0:2], in_=self.hist[:, f, :]), [self.r_hist], r_a)
                S.op("act", lambda h, a_=a_, bka=bka: h.activation(out=a_[:, 2:2 + n], in_=bka[:, 0:n], func=AF.Copy), [rbka], r_a)
                S.op("pool", lambda h, a_=a_, f=f: h.tensor_copy(out=self.hist[:, f, :], in_=a_[:, nr:nr + 2]), r_a, [self.r_hist])
                S.op("act", lambda h, c_=c_, bka=bka, f=f: h.activation(out=c_[:, 0:n], in_=bka[:, 0:n], func=AF.Identity, scale=self.taps[:, 2, f:f + 1],
                                                                       bias=self.cbias[:, f:f + 1]), [rbka, self.r_const], r_c)
                S.op("dve", lambda h, c_=c_, a_=a_, f=f: h.scalar_tensor_tensor(out=c_[:, 0:n], in0=a_[:, 1:1 + n], scalar=self.taps[:, 1, f:f + 1], in1=c_[:, 0:n],
                                                                               op0=ALU.mult, op1=ALU.add), r_a + r_c + [self.r_const], r_c)
                S.op("dve", lambda h, c_=c_, a_=a_, f=f: h.scalar_tensor_tensor(out=c_[:, 0:n], in0=a_[:, 0:n], scalar=self.taps[:, 0, f:f + 1], in1=c_[:, 0:n],
                                                                               op0=ALU.mult, op1=ALU.add), r_a + r_c + [self.r_const], r_c)
                S.op("act", lambda h, c_=c_, g_=g_: h.activation(out=g_[:, 0:n], in_=c_[:, 0:n], func=AF.Gelu_apprx_tanh), r_c, r_g)
                S.op("dve", lambda h, g_=g_, bkb_=bkb_, f=f: h.tensor_tensor(out=gT3[:, f, 0:n], in0=bkb_[:, 0:n], in1=g_[:, 0:n], op=ALU.mult), [rbkb] + r_g, [r_gT[f]])
        if is_last:
            bk, rbk = self.getbank("mm")
            S.op("pe", lambda h, bk=bk: h.transpose(out=bk[0:44, 0:128], in_=self.hist[:, :, :].rearrange("p f j -> p (f j)"), identity=self.identf[:]),
                 [self.r_hist, self.r_const], [rbk])
            co, r_co = self.av(40, 512, F32)
            evac_copy(co[0:44, 0:128], bk[0:44, 0:128], [rbk], r_co)
            for j in range(2):
                self.out_res.append(Res("oc"))
                S.dma("sp", "so", lambda h, j=j: h.dma_start(out=seq["o_c"][j, :].rearrange("(f p) -> f p", p=128), in_=co[j:44:2, 0:128]),
                      reads=r_co, writes=[self.out_res[-1]])

        chk("s7")
        if pre8 is not None:
            pre8[0]()
        ffv = [self.av(36 + 2 * i, 2048, F32) for i in range(2)]
        def down_tail(m, ff, r_ff):
            bk2, rbk2 = self.getbank("acc")
            for tt in range(ntt):
                tn = min(128, n - tt * 128)
                S.op("pe", lambda h, tt=tt, tn=tn: h.transpose(out=bk2[0:tn, 128 * tt:128 * tt + 128], in_=ff[:, 128 * tt:128 * tt + tn], identity=self.identf[:]),
                     r_ff + [self.r_const], [rbk2])
            for tt in range(ntt):
                tn = min(128, n - tt * 128)
                S.op("dve", lambda h, tt=tt, tn=tn: h.tensor_tensor(out=xh[0:tn, tt, 128 * m:128 * m + 128], in0=bk2[0:tn, 128 * tt:128 * tt + 128],
                                                                  in1=xh[0:tn, tt, 128 * m:128 * m + 128], op=ALU.add),
                     [rbk2, r_xh[tt]], [r_xh[tt]])

        prev_m = None
        for m in range(8):
            pan, rpan = self.panel(NF, 128)
            bk, rbk = self.getbank("mm")
            for f in range(NF):
                S.op("pe", lambda h, f=f, bk=bk, pan=pan: h.matmul(bk[:, 0:n], lhsT=pan[:, f, :], rhs=gT3[:, f, 0:n], start=(f == 0), stop=(f == NF - 1)),
                     [rpan, r_gT[f]], [rbk])
            ff, r_ff = ffv[m % 2]
            evac_copy(ff[:, 0:n], bk[:, 0:n], [rbk], r_ff)
            if prev_m is not None:
                down_tail(*prev_m)
            prev_m = (m, ff, r_ff)
            if pre8 is not None and m % 2 == 1:
                pre8[1](m // 2)
        down_tail(*prev_m)
        ost, r_ost = self.av(40, 4096, F32)
        for tt in range(ntt):
            tn = min(128, n - tt * 128)
            xin = xh[0:tn, tt, :]
            ssv = self.ss[0:tn, 4 + tt:5 + tt]
            S.op("act", lambda h, xin=xin, ssv=ssv, tn=tn: h.activation(out=self.junk[0:tn, :], in_=xin, func=AF.Square, accum_out=ssv), [r_xh[tt]], [self.r_junk, self.r_ss])
            S.op("act", lambda h, ssv=ssv: h.activation(out=ssv, in_=ssv, func=AF.Sqrt, scale=1.0 / D, bias=EPS), [self.r_ss], [self.r_ss])
            S.op("dve", lambda h, ssv=ssv: h.reciprocal(out=ssv, in_=ssv), [self.r_ss], [self.r_ss])
            S.op("dve", lambda h, xin=xin, ssv=ssv, tn=tn: h.scalar_tensor_tensor(out=ost[0:tn, :], in0=xin, scalar=ssv, in1=self.gfin_row[0:tn, :], op0=ALU.mult, op1=ALU.mult),
                 [r_xh[tt], self.r_ss, self.r_const], r_ost)
            tnr = min(tn, nr - tt * 128)
            dst = seq["o_y"][T0 + 128 * tt:T0 + 128 * tt + tnr, :]
            self.out_res.append(Res("oy"))
            S.dma("sp", "ost", lambda h, dst=dst, tnr=tnr: h.dma_start(out=dst, in_=ost[0:tnr, :]), reads=r_ost, writes=[self.out_res[-1]])

    def seq_reset(self, seq):
        S = self.S
        if seq["kind"] == "prompt":
            S.op("pool", lambda h: h.memset(self.car[:], 0.0), [self.r_car], [self.r_car])
            S.op("pool", lambda h: h.memset(self.hist[:], 0.0), [self.r_hist], [self.r_hist])
        else:
            for ri, src in enumerate((self.s0re, self.s0im)):
                S.dma("sp", "sq", lambda h, ri=ri, src=src: h.dma_start(out=self.car[:, ri, :], in_=src.rearrange("(q g) p -> (g p) q", g=2), allow_slow_non_contiguous=True),
                      reads=[self.r_car], writes=[self.r_car])
            for j in range(2):
                S.dma("sp", "sq", lambda h, j=j: h.dma_start(out=self.hist[:, :, j], in_=self.cc[j, :].rearrange("(f p) -> p f", p=128), allow_slow_non_contiguous=True),
                      reads=[self.r_hist], writes=[self.r_hist])
            for t in range(4):
                S.dma("pool", "cv%d" % t, lambda h, t=t: h.dma_start(out=self.vr[:, 0, t, :], in_=self.cv[128 * t:128 * t + 128, :]), writes=[self.r_vr[0][t]])
            kst, r_kst = self.av(40, 1024, BF16)
            for t in range(4):
                S.dma("pool", "ckk", lambda h, t=t: h.dma_start(out=kst, in_=self.ck[128 * t:128 * t + 128, :]), reads=r_kst, writes=r_kst)
                bk, rbk = self.getbank("mm")
                bkb = bk[:].bitcast(BF16).rearrange("p (k c) -> p k c", k=8)
                for hp in range(4):
                    S.op("pe", lambda h, hp=hp, bkb=bkb: h.transpose(out=bkb[:, hp, :], in_=kst[:, 128 * hp:128 * hp + 128], identity=self.identb[:]), r_kst + [self.r_const], [rbk])
                S.op("dve", lambda h, t=t, bkb=bkb: h.tensor_copy(out=self.kT[:, :, 0, 128 * t:128 * t + 128], in_=bkb[:, 0:4, :]), [rbk], [self.r_kT[0]])

    def build(self):
        seqs = []
        for i in range(2):
            seqs.append(dict(kind="prompt", first=(i == 0), n=TB, nblocks=4, x=self.xp[i], o_y=self.o_yp[i], o_k=self.o_kp[i], o_v=self.o_vp[i],
                             o_sre=self.o_srep[i], o_sim=self.o_simp[i], o_c=self.o_cp[i],
                             slot=lambda b: b % 2, has_prev=lambda b: b > 0, is_last=lambda b: b == 3))
        seqs.append(dict(kind="sample", n=128, nreal=16, nblocks=1, x=self.xs, o_y=self.o_ys, o_k=self.o_ks, o_v=self.o_vs,
                         o_sre=self.o_sres, o_sim=self.o_sims, o_c=self.o_cs,
                         slot=lambda b: 1, has_prev=lambda b: True, is_last=lambda b: True))
        self.build_panels(sum(s["nblocks"] for s in seqs))
        try:
            self.setup_consts()
            chk("c")
            self.s5_prep()
            chk("p")
            allb = [(si, seq, b) for si, seq in enumerate(seqs) for b in range(seq["nblocks"])]
            self.stage1(allb[0][1], allb[0][2], 0)
            for gi, (si, seq, b) in enumerate(allb):
                CURSEQ[0] = si
                if b == 0:
                    self.seq_reset(seq)
                    chk("r")
                nxt = allb[gi + 1] if gi + 1 < len(allb) else None
                pre8 = ((lambda nxt=nxt, gi=gi: self.stage1(nxt[1], nxt[2], gi + 1, load_only=True)),
                        (lambda tt, nxt=nxt, gi=gi: self.stage1_norm(nxt[1], gi + 1, tt))) if nxt is not None else None
                self.block(seq, b, gi, pre8)
                chk("b%d_%d" % (si, b))
        except StopBuild:
            pass
        self.S.emit(final_res=self.out_res)
        return self.nc


_CACHE = {}


def kernel(**inp):
    f = lambda a: np.ascontiguousarray(np.asarray(a, dtype=np.float32))
    if "nc" not in _CACHE:
        _CACHE["nc"] = Builder().build()
    nc = _CACHE["nc"]
    shared = {
        "g_mix": f(inp["g_mix"][0]), "w_in": f(inp["w_in"][0]),
        "lam_re": f(inp["ssm_lambda_re"][0]), "lam_im": f(inp["ssm_lambda_im"][0]), "log_dt": f(inp["ssm_log_dt"][0]),
        "b_re": f(inp["ssm_b_re"][0]), "b_im": f(inp["ssm_b_im"][0]),
        "c_re": f(inp["ssm_c_re"][0]).reshape(512, 64), "c_im": f(inp["ssm_c_im"][0]).reshape(512, 64),
        "ssm_d": f(inp["ssm_d"][0]).reshape(512), "w_glu": f(inp["w_ssm_glu"][0]), "relb": f(inp["attn_rel_bias"][0]),
        "w_au": f(inp["w_attn_up"][0]), "w_o": f(inp["w_o"][0]), "g_ffn": f(inp["g_ffn"][0]), "w_up": f(inp["w_up"][0]),
        "conv_w": f(inp["conv_w"][0]), "conv_b": f(inp["conv_b"][0]), "w_down": f(inp["w_down"][0]), "g_fin": f(inp["g_final"]),
    }
    xp = f(inp["x_prompt"])
    xs = f(inp["x_sample"])
    in_maps = []
    for c in range(NCORES):
        m = dict(shared)
        m["xp"] = xp[2 * c:2 * c + 2]
        m["xs"] = xs[c]
        m["s0re"] = f(inp["state_ssm_re"][0, c])
        m["s0im"] = f(inp["state_ssm_im"][0, c])
        m["ck"] = f(inp["cache_attn_k"][0, c]).reshape(512, 512)
        m["cv"] = f(inp["cache_attn_v"][0, c]).reshape(512, 512)
        m["cc"] = f(inp["cache_conv"][0, c])
        in_maps.append(m)
    res = run_bass_kernel_spmd(nc, in_maps, core_ids=list(range(NCORES)))
    R = res.results
    cat = lambda k: np.concatenate([np.asarray(R[c][k]) for c in range(NCORES)], axis=0)
    stk = lambda k: np.stack([np.asarray(R[c][k]) for c in range(NCORES)], axis=0)
    y_prompt = cat("o_yp").astype(np.float32)
    y_sample = stk("o_ys").astype(np.float32)
    outs = (
        y_prompt, y_sample,
        cat("o_srep")[None].astype(np.float32), cat("o_simp")[None].astype(np.float32),
        cat("o_kp").reshape(1, 16, 512, 8, 64).astype(np.float32), cat("o_vp").reshape(1, 16, 512, 8, 64).astype(np.float32),
        cat("o_cp")[None].astype(np.float32),
        stk("o_sres")[None].astype(np.float32), stk("o_sims")[None].astype(np.float32),
        stk("o_ks").reshape(1, 8, 16, 8, 64).astype(np.float32), stk("o_vs").reshape(1, 8, 16, 8, 64).astype(np.float32),
        stk("o_cs")[None].astype(np.float32),
    )
    return outs
```
